# Optimizing a Trainium2 kernel written in Bass

```python
import math
import jax
import jax.numpy as jnp
from jax import lax
import numpy as np

D_MODEL = 2048
BATCH = 2
SEQ = 4096
DEPTH = 2

GRID_W = 64
CTX_LEN = 256

MLA_HEADS = 8
MLA_Q_RANK = 512
MLA_KV_RANK = 256
MLA_NOPE = 128
MLA_ROPE = 64
MLA_V = 128
MLA_SCALE = 1.0 / math.sqrt(MLA_NOPE + MLA_ROPE)
MLA_Q_BLOCK = 128
ROPE_BASE = 10000.0

RG_WIDTH = 1024
RG_BLOCKS = 8
RG_BS = RG_WIDTH // RG_BLOCKS
RG_CONV = 4
RG_C = 8.0

NA_HEADS = 8
NA_HEAD_DIM = 128
NA_WIDTH = NA_HEADS * NA_HEAD_DIM
NA_WIN_ROWS = 8
NA_WIN_COLS = 16
NA_SCALE = 1.0 / math.sqrt(NA_HEAD_DIM)

FFN_HIDDEN = -(-8 * D_MODEL // (3 * 256)) * 256

NORM_EPS = 1e-6
NEG_INF = -1e30

IN_SECTIONS = (MLA_Q_RANK, MLA_KV_RANK, MLA_ROPE, RG_WIDTH, RG_WIDTH, NA_WIDTH, NA_WIDTH, NA_WIDTH, D_MODEL, D_MODEL, D_MODEL)
IN_WIDTH = MLA_Q_RANK + MLA_KV_RANK + MLA_ROPE + 2 * RG_WIDTH + 3 * NA_WIDTH + 3 * D_MODEL

kernel_name = "hybrid_mla_rglru_natten_prefix_dit_block"


def rms_norm(x, g):
    xf = x.astype(jnp.float32)
    y = xf * lax.rsqrt(jnp.mean(xf * xf, axis=-1, keepdims=True) + NORM_EPS)
    return (y * g.astype(jnp.float32)).astype(x.dtype)


def modulate(x, g, shift, scale):
    return rms_norm(x, g) * (1 + scale) + shift


def ada_params(cond, w_ada, b_ada):
    m = jax.nn.silu(cond) @ w_ada + b_ada
    return [t[:, None, :] for t in jnp.split(m, 6, axis=-1)]


def split_in(p):
    offs = np.cumsum(IN_SECTIONS)[:-1].tolist()
    return jnp.split(p, offs, axis=-1)


def axial_rope_tables(n_tok):
    t = jnp.arange(n_tok, dtype=jnp.int32)
    row = (t // GRID_W).astype(jnp.float32)
    col = (t % GRID_W).astype(jnp.float32)
    n_freq = MLA_ROPE // 4
    inv = ROPE_BASE ** (-jnp.arange(n_freq, dtype=jnp.float32) / n_freq)
    ang = jnp.concatenate([row[:, None] * inv, col[:, None] * inv], axis=-1)
    return jnp.cos(ang), jnp.sin(ang)


def apply_axial_rope(x, cos, sin):
    n_freq = MLA_ROPE // 4
    bshape = (1, cos.shape[0]) + (1,) * (x.ndim - 3) + (2, n_freq)
    cs, sn = cos.reshape(bshape), sin.reshape(bshape)
    xf = x.astype(jnp.float32).reshape(x.shape[:-1] + (2, 2, n_freq))
    x1, x2 = xf[..., 0, :], xf[..., 1, :]
    out = jnp.stack([x1 * cs - x2 * sn, x2 * cs + x1 * sn], axis=-2)
    return out.reshape(x.shape).astype(x.dtype)


def mla_qkv(q_a, kv_a, k_rope, g_q_a, w_q_b, g_kv_a, w_kv_b):
    b, n, _ = q_a.shape
    q = (rms_norm(q_a, g_q_a) @ w_q_b).reshape(b, n, MLA_HEADS, MLA_NOPE + MLA_ROPE)
    kv = (rms_norm(kv_a, g_kv_a) @ w_kv_b).reshape(b, n, MLA_HEADS, MLA_NOPE + MLA_V)
    return q[..., :MLA_NOPE], q[..., MLA_NOPE:], kv[..., :MLA_NOPE], k_rope, kv[..., MLA_NOPE:]


def mla_attend(qn, qr, kn, kr, v):
    s = (jnp.einsum('bqhd,bkhd->bhqk', qn, kn, preferred_element_type=jnp.float32)
         + jnp.einsum('bqhr,bkr->bhqk', qr, kr, preferred_element_type=jnp.float32)) * MLA_SCALE
    p = jax.nn.softmax(s, axis=-1).astype(v.dtype)
    return jnp.einsum('bhqk,bkhd->bqhd', p, v)


def mla_latent(qn, qr, kn, kr, v):
    b, s, h, _ = qn.shape
    nblk = s // MLA_Q_BLOCK
    qn_b = qn.reshape(b, nblk, MLA_Q_BLOCK, h, MLA_NOPE).transpose(1, 0, 2, 3, 4)
    qr_b = qr.reshape(b, nblk, MLA_Q_BLOCK, h, MLA_ROPE).transpose(1, 0, 2, 3, 4)
    out = lax.map(lambda a: mla_attend(a[0], a[1], kn, kr, v), (qn_b, qr_b))
    return out.transpose(1, 0, 2, 3, 4).reshape(b, s, h * MLA_V)


def dense_attend(q, k, v, scale):
    s = jnp.einsum('bqhd,bkhd->bhqk', q, k, preferred_element_type=jnp.float32) * scale
    p = jax.nn.softmax(s, axis=-1).astype(v.dtype)
    return jnp.einsum('bhqk,bkhd->bqhd', p, v)


def short_conv(u, w, b):
    pad_l = RG_CONV // 2
    y = lax.conv_general_dilated(u, w[:, None, :].astype(u.dtype), window_strides=(1,),
                                 padding=[(pad_l, RG_CONV - 1 - pad_l)],
                                 dimension_numbers=('NWC', 'WIO', 'NWC'),
                                 feature_group_count=u.shape[-1])
    return y + b


def rglru_coeffs(u, w_a, b_a, w_x, b_x, lam):
    ub = u.reshape(u.shape[:-1] + (RG_BLOCKS, RG_BS))
    r = jax.nn.sigmoid(jnp.einsum('bnhi,hij->bnhj', ub, w_a.astype(jnp.float32)).reshape(u.shape) + b_a.astype(jnp.float32))
    i = jax.nn.sigmoid(jnp.einsum('bnhi,hij->bnhj', ub, w_x.astype(jnp.float32)).reshape(u.shape) + b_x.astype(jnp.float32))
    log_a = -RG_C * r * jax.nn.softplus(-lam.astype(jnp.float32))
    return jnp.exp(log_a), jnp.sqrt(-jnp.expm1(2.0 * log_a)) * (i * u)


def linear_scan(a, b, h0, reverse):
    def combine(left, right):
        return left[0] * right[0], right[0] * left[1] + right[1]
    a_cum, b_cum = lax.associative_scan(combine, (a, b), reverse=reverse, axis=1)
    return a_cum * h0[:, None, :] + b_cum


def bidir_rglru(u_ctx, u_lat, conv_w, conv_b, rg_wa, rg_ba, rg_wx, rg_bx, rg_lambda):
    uc = short_conv(u_ctx, conv_w, conv_b).astype(jnp.float32)
    ul = short_conv(u_lat, conv_w, conv_b).astype(jnp.float32)
    outs_c, outs_l = [], []
    for d, rev in ((0, False), (1, True)):
        a_c, b_c = rglru_coeffs(uc, rg_wa[d], rg_ba[d], rg_wx[d], rg_bx[d], rg_lambda[d])
        h_c = linear_scan(a_c, b_c, jnp.zeros_like(uc[:, 0]), rev)
        h_end = h_c[:, 0] if rev else h_c[:, -1]
        a_l, b_l = rglru_coeffs(ul, rg_wa[d], rg_ba[d], rg_wx[d], rg_bx[d], rg_lambda[d])
        outs_c.append(h_c)
        outs_l.append(linear_scan(a_l, b_l, h_end, rev))
    return (outs_c[0] + outs_c[1]).astype(u_ctx.dtype), (outs_l[0] + outs_l[1]).astype(u_lat.dtype)


def natten_latent(q, k, v, k_ctx, v_ctx, rpb):
    b, n, h, d = q.shape
    rows = n // GRID_W
    win_r = min(NA_WIN_ROWS, rows)
    r = jnp.arange(rows)
    r_idx = jnp.clip(r - win_r // 2, 0, rows - win_r)[:, None] + jnp.arange(win_r)[None, :]
    col = jnp.arange(GRID_W)
    c_start = jnp.clip(col - NA_WIN_COLS // 2, 0, GRID_W - NA_WIN_COLS)
    col_in = (col[None, :] >= c_start[:, None]) & (col[None, :] < c_start[:, None] + NA_WIN_COLS)
    qg = q.reshape(b, rows, GRID_W, h, d)
    kg = k.reshape(b, rows, GRID_W, h, d)[:, r_idx]
    vg = v.reshape(b, rows, GRID_W, h, d)[:, r_idx]
    s_lat = jnp.einsum('brqhd,brkwhd->brhqkw', qg, kg, preferred_element_type=jnp.float32) * NA_SCALE
    dr = r_idx - r[:, None] + (NA_WIN_ROWS - 1)
    dc = jnp.clip(col[None, :] - col[:, None] + (NA_WIN_COLS - 1), 0, 2 * NA_WIN_COLS - 2)
    bias = rpb[:, dr[:, None, :, None], dc[None, :, None, :]].astype(jnp.float32)
    s_lat = jnp.where(col_in[:, None, :], s_lat + bias.transpose(1, 0, 2, 3, 4), NEG_INF)
    s_ctx = jnp.einsum('brqhd,bjhd->brhqj', qg, k_ctx, preferred_element_type=jnp.float32) * NA_SCALE
    n_win = win_r * GRID_W
    s = jnp.concatenate([s_lat.reshape(b, rows, h, GRID_W, n_win), s_ctx], axis=-1)
    p = jax.nn.softmax(s, axis=-1).astype(v.dtype)
    p_lat = p[..., :n_win].reshape(b, rows, h, GRID_W, win_r, GRID_W)
    out = (jnp.einsum('brhqkw,brkwhd->brqhd', p_lat, vg)
           + jnp.einsum('brhqj,bjhd->brqhd', p[..., n_win:], v_ctx))
    return out.reshape(b, n, h * d)


def merge_branches(g_mla, g_rg, g_na, mla_o, rg_o, na_o, w_mla_o, w_rg_o, w_na_o, w_out):
    y = (jax.nn.sigmoid(g_mla) * (mla_o @ w_mla_o)
         + jax.nn.sigmoid(g_rg) * (rg_o @ w_rg_o)
         + jax.nn.sigmoid(g_na) * (na_o @ w_na_o))
    return y @ w_out


def ffn_sublayer(x, shift, scale, gate, g_pre, g_post, w_gate, w_up, w_down):
    h = modulate(x, g_pre, shift, scale)
    y = (jax.nn.silu(h @ w_gate) * (h @ w_up)) @ w_down
    return x + gate * rms_norm(y, g_post)


def trunk_layer(xc, xl, mods_c, mods_l, cos, sin, with_ctx_out,
                g_mix_pre, g_mix_post, g_ffn_pre, g_ffn_post, w_in, g_q_a, w_q_b, g_kv_a, w_kv_b,
                w_mla_o, conv_w, conv_b, rg_wa, rg_ba, rg_wx, rg_bx, rg_lambda, w_rg_o,
                na_rpb, w_na_o, w_out, w_ffn_gate, w_ffn_up, w_ffn_down):
    sh1c, sc1c, gt1c, sh2c, sc2c, gt2c = mods_c
    sh1l, sc1l, gt1l, sh2l, sc2l, gt2l = mods_l
    b, n_ctx, _ = xc.shape
    n_lat = xl.shape[1]
    pc = split_in(modulate(xc, g_mix_pre, sh1c, sc1c) @ w_in)
    pl = split_in(modulate(xl, g_mix_pre, sh1l, sc1l) @ w_in)
    qn_c, qr_c, kn_c, kr_c, v_c = mla_qkv(pc[0], pc[1], pc[2], g_q_a, w_q_b, g_kv_a, w_kv_b)
    qn_l, qr_l, kn_l, kr_l, v_l = mla_qkv(pl[0], pl[1], pl[2], g_q_a, w_q_b, g_kv_a, w_kv_b)
    qr_l = apply_axial_rope(qr_l, cos, sin)
    kr_l = apply_axial_rope(kr_l, cos, sin)
    mla_l = mla_latent(qn_l, qr_l, jnp.concatenate([kn_c, kn_l], axis=1),
                       jnp.concatenate([kr_c, kr_l], axis=1), jnp.concatenate([v_c, v_l], axis=1))
    rec_c, rec_l = bidir_rglru(pc[3], pl[3], conv_w, conv_b, rg_wa, rg_ba, rg_wx, rg_bx, rg_lambda)
    rg_l = jax.nn.gelu(pl[4]) * rec_l
    na_shape_c = (b, n_ctx, NA_HEADS, NA_HEAD_DIM)
    na_shape_l = (b, n_lat, NA_HEADS, NA_HEAD_DIM)
    nk_c, nv_c = pc[6].reshape(na_shape_c), pc[7].reshape(na_shape_c)
    na_l = natten_latent(pl[5].reshape(na_shape_l), pl[6].reshape(na_shape_l), pl[7].reshape(na_shape_l),
                         nk_c, nv_c, na_rpb)
    yl = merge_branches(pl[8], pl[9], pl[10], mla_l, rg_l, na_l, w_mla_o, w_rg_o, w_na_o, w_out)
    xl = xl + gt1l * rms_norm(yl, g_mix_post)
    xl = ffn_sublayer(xl, sh2l, sc2l, gt2l, g_ffn_pre, g_ffn_post, w_ffn_gate, w_ffn_up, w_ffn_down)
    if with_ctx_out:
        mla_c = mla_attend(qn_c, qr_c, kn_c, kr_c, v_c).reshape(b, n_ctx, MLA_HEADS * MLA_V)
        rg_c = jax.nn.gelu(pc[4]) * rec_c
        na_c = dense_attend(pc[5].reshape(na_shape_c), nk_c, nv_c, NA_SCALE).reshape(b, n_ctx, NA_WIDTH)
        yc = merge_branches(pc[8], pc[9], pc[10], mla_c, rg_c, na_c, w_mla_o, w_rg_o, w_na_o, w_out)
        xc = xc + gt1c * rms_norm(yc, g_mix_post)
        xc = ffn_sublayer(xc, sh2c, sc2c, gt2c, g_ffn_pre, g_ffn_post, w_ffn_gate, w_ffn_up, w_ffn_down)
    return xc, xl


def setup_inputs(seed: int = 0) -> dict:
    key = jax.random.key(seed)
    ks = jax.random.split(key, 40)
    counter = [0]

    def nk():
        k = ks[counter[0]]
        counter[0] += 1
        return k

    def nrm(shape, scale):
        return jax.random.normal(nk(), shape, jnp.float32) * scale

    def gain(shape):
        return 1.0 + nrm(shape, 0.02)

    L, D = DEPTH, D_MODEL
    x = nrm((BATCH, SEQ, D), 1.0)
    c = nrm((BATCH, D), 1.0)
    ctx = nrm((BATCH, CTX_LEN, D), 1.0)
    c_ctx = nrm((D,), 1.0)
    w_ada = nrm((L, D, 6 * D), D ** -0.5)
    b_ada = nrm((L, 6 * D), 0.01)
    g_mix_pre = gain((L, D))
    g_mix_post = gain((L, D))
    g_ffn_pre = gain((L, D))
    g_ffn_post = gain((L, D))
    w_in = nrm((L, D, IN_WIDTH), D ** -0.5)
    g_q_a = gain((L, MLA_Q_RANK))
    w_q_b = nrm((L, MLA_Q_RANK, MLA_HEADS * (MLA_NOPE + MLA_ROPE)), MLA_Q_RANK ** -0.5)
    g_kv_a = gain((L, MLA_KV_RANK))
    w_kv_b = nrm((L, MLA_KV_RANK, MLA_HEADS * (MLA_NOPE + MLA_V)), MLA_KV_RANK ** -0.5)
    w_mla_o = nrm((L, MLA_HEADS * MLA_V, D), (MLA_HEADS * MLA_V) ** -0.5)
    conv_w = nrm((L, RG_CONV, RG_WIDTH), RG_CONV ** -0.5)
    conv_b = nrm((L, RG_WIDTH), 0.01)
    rg_wa = nrm((L, 2, RG_BLOCKS, RG_BS, RG_BS), RG_BS ** -0.5)
    rg_ba = nrm((L, 2, RG_WIDTH), 0.01)
    rg_wx = nrm((L, 2, RG_BLOCKS, RG_BS, RG_BS), RG_BS ** -0.5)
    rg_bx = nrm((L, 2, RG_WIDTH), 0.01)
    u = jax.random.uniform(nk(), (L, 2, RG_WIDTH), jnp.float32, 0.9, 0.999)
    a_base = u ** (1.0 / RG_C)
    rg_lambda = jnp.log(a_base) - jnp.log1p(-a_base)
    w_rg_o = nrm((L, RG_WIDTH, D), RG_WIDTH ** -0.5)
    na_rpb = nrm((L, NA_HEADS, 2 * NA_WIN_ROWS - 1, 2 * NA_WIN_COLS - 1), 0.1)
    w_na_o = nrm((L, NA_WIDTH, D), NA_WIDTH ** -0.5)
    w_out = nrm((L, D, D), D ** -0.5)
    w_ffn_gate = nrm((L, D, FFN_HIDDEN), D ** -0.5)
    w_ffn_up = nrm((L, D, FFN_HIDDEN), D ** -0.5)
    w_ffn_down = nrm((L, FFN_HIDDEN, D), FFN_HIDDEN ** -0.5)
    return {"x": x, "c": c, "ctx": ctx, "c_ctx": c_ctx, "w_ada": w_ada, "b_ada": b_ada,
            "g_mix_pre": g_mix_pre, "g_mix_post": g_mix_post, "g_ffn_pre": g_ffn_pre, "g_ffn_post": g_ffn_post,
            "w_in": w_in, "g_q_a": g_q_a, "w_q_b": w_q_b, "g_kv_a": g_kv_a, "w_kv_b": w_kv_b,
            "w_mla_o": w_mla_o, "conv_w": conv_w, "conv_b": conv_b, "rg_wa": rg_wa, "rg_ba": rg_ba,
            "rg_wx": rg_wx, "rg_bx": rg_bx, "rg_lambda": rg_lambda, "w_rg_o": w_rg_o,
            "na_rpb": na_rpb, "w_na_o": w_na_o, "w_out": w_out,
            "w_ffn_gate": w_ffn_gate, "w_ffn_up": w_ffn_up, "w_ffn_down": w_ffn_down}


def reference(x, c, ctx, c_ctx, w_ada, b_ada, g_mix_pre, g_mix_post, g_ffn_pre, g_ffn_post,
              w_in, g_q_a, w_q_b, g_kv_a, w_kv_b, w_mla_o, conv_w, conv_b, rg_wa, rg_ba,
              rg_wx, rg_bx, rg_lambda, w_rg_o, na_rpb, w_na_o, w_out, w_ffn_gate, w_ffn_up, w_ffn_down):
    cos, sin = axial_rope_tables(x.shape[1])
    xc, xl = ctx, x
    for i in range(DEPTH):
        mods_l = ada_params(c, w_ada[i], b_ada[i])
        mods_c = ada_params(c_ctx[None, :], w_ada[i], b_ada[i])
        xc, xl = trunk_layer(
            xc, xl, mods_c, mods_l, cos, sin, i < DEPTH - 1,
            g_mix_pre=g_mix_pre[i], g_mix_post=g_mix_post[i], g_ffn_pre=g_ffn_pre[i], g_ffn_post=g_ffn_post[i],
            w_in=w_in[i], g_q_a=g_q_a[i], w_q_b=w_q_b[i], g_kv_a=g_kv_a[i], w_kv_b=w_kv_b[i],
            w_mla_o=w_mla_o[i], conv_w=conv_w[i], conv_b=conv_b[i], rg_wa=rg_wa[i], rg_ba=rg_ba[i],
            rg_wx=rg_wx[i], rg_bx=rg_bx[i], rg_lambda=rg_lambda[i], w_rg_o=w_rg_o[i],
            na_rpb=na_rpb[i], w_na_o=w_na_o[i], w_out=w_out[i],
            w_ffn_gate=w_ffn_gate[i], w_ffn_up=w_ffn_up[i], w_ffn_down=w_ffn_down[i])
    return xl
```

```python
import math
import numpy as np
import ml_dtypes
import concourse.bass as bass
import concourse.mybir as mybir
from concourse.bass_utils import run_bass_kernel_spmd

F32 = mybir.dt.float32
BF16 = mybir.dt.bfloat16
AF = mybir.ActivationFunctionType
ALU = mybir.AluOpType

ENGS = ("pe", "act", "dve", "pool", "sp")
DT_SIZE = {F32: 4, BF16: 2}

D = 2048
KC = 16
TC = 256
TL = 1024
T = TC + TL
SEQ = 4096
NKEY = TC + SEQ
GRID_W = 64
DEPTH = 2
FFN = 5632
HC = FFN // 128
EPS = 1e-6
MLA_SCALE = 1.0 / math.sqrt(192.0)
NA_SCALE = 1.0 / math.sqrt(128.0)
BLKS = [(0, 512), (512, 512), (1024, 256)]
LBLKS = [(256, 512), (768, 512)]
SEC = [(0, 512), (512, 768), (832, 1856), (1856, 2880), (2880, 3904), (3904, 4928),
       (5952, 8000), (8000, 10048), (10048, 12096), (768, 832)]
NFM = sum(b - a for a, b in SEC)
NA_WIN = []
for _i in range(16):
    if _i < 4:
        lo, n = _i, 12 - _i
    elif _i <= 12:
        lo, n = _i, 8
    else:
        lo, n = 12, _i - 4
    n += n % 2
    NA_WIN.append((lo, n))
NA_NCH = [n // 2 + 2 for _, n in NA_WIN]
NA_CH0 = [int(x) for x in np.cumsum([0] + NA_NCH[:-1])]
NA_TOT = int(sum(NA_NCH))


class Buf:
    def __init__(self, name, ap, shape=None, dtype=None):
        self.name = name
        self.ap = ap
        self.shape = shape
        self.dtype = dtype
        self.w = None
        self.r = []

    def __getitem__(self, key):
        return View(self, self.ap[key])


class View:
    def __init__(self, buf, ap):
        self.buf = buf
        self.ap = ap

    def __getitem__(self, key):
        return View(self.buf, self.ap[key])

    def rearrange(self, s, **kw):
        return View(self.buf, self.ap.rearrange(s, **kw))


def _v(x):
    return x.ap if isinstance(x, View) else x


class Prog:
    def __init__(self):
        self.nc = bass.Bass("TRN2", target_bir_lowering=False)
        self.ops = {e: [] for e in ENGS}
        self.cnt = {}
        self.known = {e: {} for e in ENGS}
        self.ndma = {"sp": 0, "pool": 0, "act": 0}
        self.NDMASEM = 8
        self.sb_off = 16640
        self.sb_mark = []
        self.names = 0
        self.SB_BYTES = 192 * 1024
        self.ep = 0

    def dram(self, name, shape, dtype, kind="Internal"):
        t = self.nc.dram_tensor(name, list(shape), dtype, kind=kind)
        return Buf(name, t.ap(), shape, dtype)

    def sb(self, name, shape, dtype):
        per_part = int(np.prod(shape[1:])) * DT_SIZE[dtype]
        off = (self.sb_off + 63) // 64 * 64
        assert off + per_part <= self.SB_BYTES, f"SBUF overflow at {name}: {off}+{per_part}"
        self.names += 1
        h = self.nc.alloc_sbuf_tensor_at(f"{name}_{self.names}", list(shape), dtype, offset=off)
        self.sb_off = off + per_part
        return Buf(name, h.ap(), shape, dtype)

    def sb_push(self):
        self.sb_mark.append(self.sb_off)

    def sb_pop(self):
        self.barrier()
        self.sb_off = self.sb_mark.pop()

    def ps(self, name, shape, dtype=F32):
        self.names += 1
        h = self.nc.alloc_psum_tensor(f"{name}_{self.names}", list(shape), dtype)
        return Buf(name, h.ap(), shape, dtype)

    def _deps(self, eng, reads, writes):
        waits = {}

        def add(tok):
            if tok is None:
                return
            k, v = tok
            if eng == "pe" and k.split(".")[0] == "pe":
                return
            if waits.get(k, -1) < v:
                waits[k] = v

        for b in reads:
            add(b.w)
        for b in writes:
            add(b.w)
            for t in b.r:
                add(t)
        out = []
        kn = self.known[eng]
        for k, v in waits.items():
            if kn.get(k, -1) >= v:
                continue
            kn[k] = v
            out.append((k, v))
        return out

    def _mark(self, tok, reads, writes):
        for b in reads:
            b.r.append(tok)
            if len(b.r) > 48:
                m = {}
                for k, v in b.r:
                    if m.get(k, -1) < v:
                        m[k] = v
                b.r = list(m.items())
        for b in writes:
            b.w = tok
            b.r = []

    def op(self, eng, fn, reads=(), writes=()):
        reads = [x.buf if isinstance(x, View) else x for x in reads]
        writes = [x.buf if isinstance(x, View) else x for x in writes]
        waits = self._deps(eng, reads, writes)
        ek = eng if self.ep == 0 else f"{eng}.{self.ep}"
        v = self.cnt.get(ek, 0) + 1
        self.cnt[ek] = v
        self.ops[eng].append((fn, waits, (ek, 1)))
        self._mark((ek, v), reads, writes)

    def dma(self, q, out, in_, **kw):
        reads = [in_.buf]
        writes = [out.buf]
        i = self.ndma[q] % self.NDMASEM
        self.ndma[q] += 1
        sk = f"dma_{q}_{i}"
        prev = self.cnt.get(sk, 0)
        waits = self._deps(q, reads, writes)
        if prev > 0 and self.known[q].get(sk, -1) < prev:
            self.known[q][sk] = prev
            waits.append((sk, prev))
        v = prev + 16
        self.cnt[sk] = v
        o, s = out.ap, in_.ap

        def fn(e, o=o, s=s, kw=kw):
            return e.dma_start(out=o, in_=s, **kw)

        self.ops[q].append((fn, waits, (sk, 16)))
        self._mark((sk, v), reads, writes)

    def allgather(self, out, in_, groups):
        reads = [in_.buf if isinstance(in_, View) else in_]
        writes = [out.buf if isinstance(out, View) else out]
        self.ncc = getattr(self, "ncc", 0) + 1
        sk = f"cc_{self.ncc}"
        waits = self._deps("pool", reads, writes)
        self.cnt[sk] = 1
        o, s = _v(out), _v(in_)

        def fn(e, o=o, s=s):
            return e.collective_compute("AllGather", ALU.bypass, replica_groups=groups, ins=[s], outs=[o])

        self.ops["pool"].append((fn, waits, (sk, 1)))
        self._mark((sk, 1), reads, writes)

    def new_epoch(self):
        self.barrier()
        self.ep += 1

    def barrier(self):
        snap = dict(self.cnt)
        for e in ENGS:
            waits = []
            for k, v in snap.items():
                if e == "pe" and k.split(".")[0] == "pe":
                    continue
                if self.known[e].get(k, -1) < v:
                    self.known[e][k] = v
                    waits.append((k, v))
            if waits:
                self.ops[e].append((None, waits, None))

    def build(self):
        self.barrier()
        nc = self.nc
        semkeys = sorted(self.cnt.keys())
        from contextlib import ExitStack

        with ExitStack() as st:
            sems = {k: st.enter_context(nc.semaphore(k)) for k in semkeys}
            block = st.enter_context(nc.Block())

            def emit(engobj, lst):
                for fn, waits, inc in lst:
                    for k, v in waits:
                        engobj.wait_ge(sems[k], v)
                    if fn is None:
                        continue
                    ins = fn(engobj)
                    if inc is not None:
                        ins.then_inc(sems[inc[0]], inc[1])

            @block.tensor
            def _(e):
                emit(e, self.ops["pe"])

            @block.scalar
            def _(e):
                emit(e, self.ops["act"])

            @block.vector
            def _(e):
                emit(e, self.ops["dve"])

            @block.gpsimd
            def _(e):
                emit(e, self.ops["pool"])

            @block.sync
            def _(e):
                emit(e, self.ops["sp"])

        return nc

    def mm(self, out, lhsT, rhs, start=True, stop=True):
        o, l, r = _v(out), _v(lhsT), _v(rhs)
        self.op("pe", lambda e: e.matmul(o, l, r, start=start, stop=stop), [lhsT, rhs], [out])

    def act(self, out, in_, func, bias=None, scale=1.0):
        o, i = _v(out), _v(in_)
        b = _v(bias) if isinstance(bias, View) else bias
        s = _v(scale) if isinstance(scale, View) else scale
        rd = [in_] + [x for x in (bias, scale) if isinstance(x, View)]
        kw = {}
        if b is not None:
            kw["bias"] = b
        self.op("act", lambda e: e.activation(o, i, func, scale=s, **kw), rd, [out])

    def tt(self, out, a, b, op, eng="dve"):
        o, x, y = _v(out), _v(a), _v(b)
        self.op(eng, lambda e: e.tensor_tensor(o, x, y, op), [a, b], [out])

    def ts(self, out, a, s1, s2, op0, op1=None, eng="dve"):
        o, x = _v(out), _v(a)
        s1v = _v(s1) if isinstance(s1, View) else s1
        s2v = _v(s2) if isinstance(s2, View) else s2
        rd = [a] + [x_ for x_ in (s1, s2) if isinstance(x_, View)]
        if op1 is None:
            self.op(eng, lambda e: e.tensor_scalar(o, x, s1v, None, op0), rd, [out])
        else:
            self.op(eng, lambda e: e.tensor_scalar(o, x, s1v, s2v, op0, op1), rd, [out])

    def stt(self, out, a, s, b, op0, op1):
        o, x, y = _v(out), _v(a), _v(b)
        sv = _v(s) if isinstance(s, View) else s
        rd = [a, b] + ([s] if isinstance(s, View) else [])
        self.op("dve", lambda e: e.scalar_tensor_tensor(o, x, sv, y, op0, op1), rd, [out])

    def copy(self, out, in_, eng="dve"):
        o, i = _v(out), _v(in_)
        if eng == "act":
            self.op("act", lambda e: e.copy(o, i), [in_], [out])
        else:
            self.op(eng, lambda e: e.tensor_copy(o, i), [in_], [out])

    def memset(self, out, val, eng="dve"):
        o = _v(out)
        self.op(eng, lambda e: e.memset(o, val), [], [out])

    def scan(self, out, d0, d1, init):
        o, a, b = _v(out), _v(d0), _v(d1)
        iv = _v(init) if isinstance(init, View) else init
        rd = [d0, d1] + ([init] if isinstance(init, View) else [])
        self.op("dve", lambda e: e.tensor_tensor_scan(o, a, b, iv, ALU.mult, ALU.add), rd, [out])

    def recip(self, out, in_):
        o, i = _v(out), _v(in_)
        self.op("dve", lambda e: e.reciprocal(o, i), [in_], [out])


class IO:
    def __init__(self, p, sfx, bound):
        self.p, self.sfx, self.bound, self.made = p, sfx, bound, {}

    def inp(self, n, s, d=F32):
        if n in self.bound:
            return self.bound[n]
        return self.p.dram(n + self.sfx, s, d, kind="ExternalInput")

    def out(self, n, s, d=F32):
        if n in self.bound:
            return self.bound[n]
        b = self.p.dram(n + self.sfx, s, d)
        self.made[n] = b
        return b

    def tmp(self, n, s, d=F32):
        return self.p.dram(n + self.sfx, s, d)


def rstd_from_ssq(p, rstd_out, ssq_ps, n):
    p.ts(rstd_out, ssq_ps, 1.0 / n, EPS, ALU.mult, ALU.add)
    p.act(rstd_out, rstd_out, AF.Sqrt)
    p.recip(rstd_out, rstd_out)


def seg_split(c0, n):
    out = []
    if c0 < TC:
        e = min(TC, c0 + n)
        out.append((c0, e - c0, 1))
        if c0 + n > TC:
            out.append((TC, c0 + n - TC, 0))
    else:
        out.append((c0, n, 0))
    return out


def emit_A(p, io, banks, ones, wc=True):
    IN = io.inp
    OUT = io.out
    LB = BLKS if wc else LBLKS
    xT = IN("xT", [D, T])
    w_fm = IN("w_fm", [D, NFM])
    w_nav = IN("w_nav", [D, 1024])
    gqT = IN("gqT", [128, 4])
    gkvT = IN("gkvT", [128, 2])
    w_qn = IN("w_qn", [512, 1024])
    w_qr = IN("w_qr", [512, 512])
    ropeC = IN("ropeC", [128, TL])
    ropeS = IN("ropeS", [128, TL])
    perm = IN("perm", [128, 128])

    uT_o = OUT("uT", [1024, T])
    rgT_o = OUT("rgT", [1024, T])
    naqT_o = OUT("naqT", [1024, T], BF16)
    nakT_o = OUT("nakT", [1024, T], BF16)
    nav_o = OUT("nav", [T, 1024], BF16)
    sgT_o = OUT("sgT", [6144, T])
    QnT_o = OUT("QnT", [1024, T], BF16)
    QrT_o = OUT("QrT", [512, T], BF16)
    kvnT_o = OUT("kvnT", [256, T], BF16)
    krT_o = OUT("krT", [64, T], BF16)
    p.sb_push()

    mods = p.sb("mods", [128, 6, KC, 2], F32)
    p.dma("sp", mods[:], IN("mods", [128, 6, KC, 2])[:])

    hT = p.sb("hT", [128, KC, T], BF16)
    qa = p.sb("qa", [128, 4, T], F32)
    kva = p.sb("kva", [128, 2, T], F32)
    krp = p.sb("krp", [64, T], F32)
    p.sb_push()
    xs = p.sb("xs", [128, KC, 512], F32)
    sq = p.sb("sq", [128, KC, 512], BF16)
    rstd = p.sb("rstd", [128, 512], F32)
    tmp = [p.sb(f"tmp{i}", [128, 512], F32) for i in range(2)]
    xT_v = xT[:].rearrange("(k p) n -> p k n", p=128)
    for (c0, n) in BLKS:
        p.dma("sp", xs[:, :, 0:n], xT_v[:, :, c0:c0 + n])
        p.act(sq[:, :, 0:n], xs[:, :, 0:n], AF.Square)
        ssq = banks[1]
        for k in range(KC):
            p.mm(ssq[:, 0:n], ones[:], sq[:, k, 0:n], start=(k == 0), stop=(k == KC - 1))
        rstd_from_ssq(p, rstd[:, 0:n], ssq[:, 0:n], D)
        for k in range(KC):
            tm = tmp[k % 2]
            for (s0, sn, cnd) in seg_split(c0, n):
                l0 = s0 - c0
                p.stt(tm[:, l0:l0 + sn], xs[:, k, l0:l0 + sn], mods[:, 0, k, cnd:cnd + 1], rstd[:, l0:l0 + sn],
                      ALU.mult, ALU.mult)
                p.act(hT[:, k, s0:s0 + sn], tm[:, l0:l0 + sn], AF.Identity, bias=mods[:, 1, k, cnd:cnd + 1])
    p.sb_pop()

    p.sb_push()
    wt2 = [p.sb(f"win{i}", [128, KC, 512], BF16) for i in range(2)]
    st32 = [p.sb(f"st32_{i}", [128, T], F32) for i in range(3)]
    st16 = [p.sb(f"st16_{i}", [128, T], BF16) for i in range(2)]
    w_fm_v = w_fm[:].rearrange("(k p) n -> p k n", p=128)
    ntile = (NFM + 511) // 512
    ev = 0
    bi = 0
    for tI in range(ntile):
        w0 = tI * 512
        wn = min(512, NFM - w0)
        wt = wt2[tI % 2]
        p.dma("pool", wt[:, :, 0:wn], w_fm_v[:, :, w0:w0 + wn])
        for cc in range(0, wn, 128):
            m = min(128, wn - cc)
            chunk = (w0 + cc) // 128
            if chunk < 4:
                kind, dst = "sb", qa[:, chunk, :]
            elif chunk < 6:
                kind, dst = "sb", kva[:, chunk - 4, :]
            elif chunk < 14:
                kind, dst, drow = "f32", uT_o, (chunk - 6) * 128
            elif chunk < 22:
                kind, dst, drow = "f32", rgT_o, (chunk - 14) * 128
            elif chunk < 30:
                kind, dst, drow = "bf", naqT_o, (chunk - 22) * 128
            elif chunk < 38:
                kind, dst, drow = "bf", nakT_o, (chunk - 30) * 128
            elif chunk < 86:
                kind, dst, drow = "sig", sgT_o, (chunk - 38) * 128
            else:
                kind, dst = "sb", krp[:, :]
            if kind in ("f32", "sig"):
                stg = st32[ev % 3]
            elif kind == "bf":
                stg = st16[ev % 2]
            ev += 1
            ctx_needed = (4 <= chunk < 14) or (30 <= chunk < 38) or chunk >= 86
            for (c0, n) in (BLKS if ctx_needed else LB):
                ps = banks[2 + bi % 6]
                bi += 1
                for k in range(KC):
                    p.mm(ps[0:m, 0:n], wt[:, k, cc:cc + m], hT[:, k, c0:c0 + n], start=(k == 0), stop=(k == KC - 1))
                if kind == "sb":
                    p.copy(dst[0:m, c0:c0 + n], ps[0:m, 0:n], eng="act" if bi % 2 else "dve")
                elif kind == "sig":
                    p.act(stg[:, c0:c0 + n], ps[:, 0:n], AF.Sigmoid)
                else:
                    p.copy(stg[:, c0:c0 + n], ps[:, 0:n], eng="act" if bi % 2 else "dve")
            if kind != "sb":
                p.dma("sp", dst[drow:drow + 128, :], stg[:])
    w_nav_v = w_nav[:].rearrange("(k p) n -> p k n", p=128)
    for half in range(2):
        wt = wt2[half % 2]
        p.dma("pool", wt[:], w_nav_v[:, :, half * 512:(half + 1) * 512])
        for tcn in range(T // 128):
            ps = banks[2 + bi % 6]
            bi += 1
            for k in range(KC):
                p.mm(ps[:, :], hT[:, k, tcn * 128:(tcn + 1) * 128], wt[:, k, :], start=(k == 0), stop=(k == KC - 1))
            stg = st16[tcn % 2]
            p.copy(stg[:, 0:512], ps[:, :], eng="act" if bi % 2 else "dve")
            p.dma("sp", nav_o[tcn * 128:(tcn + 1) * 128, half * 512:(half + 1) * 512], stg[:, 0:512])
    p.sb_pop()

    p.sb_push()
    qn = p.sb("qn", [128, 4, T], BF16)
    kvn = p.sb("kvn", [128, 2, T], BF16)
    sqq = p.sb("sqq", [128, 4, 512], BF16)
    rs = p.sb("rs", [128, 512], F32)
    tq = p.sb("tq", [128, 512], F32)
    gq = p.sb("gq", [128, 4], F32)
    gkv = p.sb("gkv", [128, 2], F32)
    p.dma("sp", gq[:], gqT[:])
    p.dma("sp", gkv[:], gkvT[:])
    for (src, nch, gsb, dstb, dim) in ((qa, 4, gq, qn, 512), (kva, 2, gkv, kvn, 256)):
        for (c0, n) in (LB if src is qa else BLKS):
            p.act(sqq[:, 0:nch, 0:n], src[:, :, c0:c0 + n], AF.Square)
            ssq = banks[1]
            for k in range(nch):
                p.mm(ssq[:, 0:n], ones[:], sqq[:, k, 0:n], start=(k == 0), stop=(k == nch - 1))
            rstd_from_ssq(p, rs[:, 0:n], ssq[:, 0:n], dim)
            for k in range(nch):
                p.stt(dstb[:, k, c0:c0 + n], src[:, k, c0:c0 + n], gsb[:, k:k + 1], rs[:, 0:n], ALU.mult, ALU.mult)
    for k in range(2):
        p.dma("sp", kvnT_o[k * 128:(k + 1) * 128, :], kvn[:, k, :])
    wqn = p.sb("wqn", [128, 4, 1024], BF16)
    wqr = p.sb("wqr", [128, 4, 512], BF16)
    p.dma("pool", wqn[:], w_qn[:].rearrange("(k p) n -> p k n", p=128))
    p.dma("pool", wqr[:], w_qr[:].rearrange("(k p) n -> p k n", p=128))
    cT = p.sb("cT", [128, TL], F32)
    sT = p.sb("sT", [128, TL], F32)
    pm = p.sb("pm", [128, 128], F32)
    p.dma("sp", cT[:], ropeC[:])
    p.dma("sp", sT[:], ropeS[:])
    p.dma("sp", pm[:], perm[:])
    stq = [p.sb(f"stq{i}", [128, T], BF16) for i in range(2)]
    for h in range(8):
        stg = stq[h % 2]
        for (c0, n) in LB:
            ps = banks[2 + bi % 6]
            bi += 1
            for k in range(4):
                p.mm(ps[:, 0:n], wqn[:, k, h * 128:(h + 1) * 128], qn[:, k, c0:c0 + n], start=(k == 0), stop=(k == 3))
            p.copy(stg[:, c0:c0 + n], ps[:, 0:n], eng="act" if bi % 2 else "dve")
        p.dma("sp", QnT_o[h * 128:(h + 1) * 128, :], stg[:])
    qrf = p.sb("qrf", [128, T], F32)
    t1 = p.sb("t1", [128, 512], F32)
    t2 = p.sb("t2", [128, 512], F32)

    def rope_chunk(src, npart, dst_dram_rows, ctx_cols=True):
        stg = stq[0]
        if ctx_cols:
            p.copy(stg[0:npart, 0:TC], src[0:npart, 0:TC])
        for (c0, n) in ((TC, 512), (TC + 512, 512)):
            ps = banks[2 + (c0 // 512) % 2]
            p.mm(ps[0:npart, 0:n], pm[0:npart, 0:npart], src[0:npart, c0:c0 + n])
            l0 = c0 - TC
            p.tt(t1[0:npart, 0:n], src[0:npart, c0:c0 + n], cT[0:npart, l0:l0 + n], ALU.mult)
            p.tt(t2[0:npart, 0:n], ps[0:npart, 0:n], sT[0:npart, l0:l0 + n], ALU.mult)
            p.tt(stg[0:npart, c0:c0 + n], t1[0:npart, 0:n], t2[0:npart, 0:n], ALU.add)
        p.dma("sp", dst_dram_rows, stg[0:npart, :])

    for hc in range(4):
        for (c0, n) in LB:
            ps = banks[4 + bi % 4]
            bi += 1
            for k in range(4):
                p.mm(ps[:, 0:n], wqr[:, k, hc * 128:(hc + 1) * 128], qn[:, k, c0:c0 + n], start=(k == 0), stop=(k == 3))
            p.copy(qrf[:, c0:c0 + n], ps[:, 0:n], eng="act")
        rope_chunk(qrf, 128, QrT_o[hc * 128:(hc + 1) * 128, :], ctx_cols=wc)
    rope_chunk(krp, 64, krT_o[:, :])
    p.sb_pop()
    p.sb_pop()


def fm(v):
    return np.ascontiguousarray(v.reshape(-1, 128).T)


def rope_tables():
    t = np.arange(SEQ, dtype=np.int32)
    row = (t // GRID_W).astype(np.float32)
    col = (t % GRID_W).astype(np.float32)
    inv = (np.float32(10000.0) ** (-np.arange(16, dtype=np.float32) / np.float32(16))).astype(np.float32)
    ang = np.concatenate([row[:, None] * inv, col[:, None] * inv], axis=-1).astype(np.float32)
    cos, sin = np.cos(ang).astype(np.float32), np.sin(ang).astype(np.float32)
    C = np.zeros((64, SEQ), np.float32)
    S = np.zeros((64, SEQ), np.float32)
    for d in range(64):
        a, b, f = d // 32, (d % 32) // 16, d % 16
        C[d] = cos[:, a * 16 + f]
        S[d] = sin[:, a * 16 + f] * (-1.0 if b == 0 else 1.0)
    pm = np.zeros((128, 128), np.float32)
    for m in range(128):
        d = m % 64
        partner = d + 16 if (d % 32) < 16 else d - 16
        pm[(m // 64) * 64 + partner, m] = 1.0
    return np.concatenate([C, C], 0), np.concatenate([S, S], 0), pm


def initial_xT(inp):
    out = []
    for core in range(8):
        b, j = core // 4, core % 4
        xx = np.concatenate([inp["ctx"][b], inp["x"][b, j * TL:(j + 1) * TL]], 0)
        out.append(np.ascontiguousarray(xx.T))
    return out


def emit_B(p, io, banks, ones, wc=True):
    IN = io.inp
    LB = BLKS if wc else LBLKS
    c_lo = 0 if wc else TC
    SEGS = ((0, TC, 1), (TC, TL, 0)) if wc else ((TC, TL, 0),)
    xT = IN("xT", [D, T])
    mods_i = IN("mods", [128, 6, KC, 2])
    u_all = IN("u_all", [1024, NKEY])
    rgT = IN("rgT", [1024, T])
    convT = IN("convT", [128, 8, 5])
    rg_w = IN("rg_w", [4, 8, 128, 128])
    rg_b = IN("rg_b", [128, 4, 8])
    lamT = IN("lamT", [128, 2, 8])
    onehot = IN("onehot", [128, 4])
    kvn_all = IN("kvn_all", [256, NKEY], BF16)
    kr_all = IN("kr_all", [64, NKEY], BF16)
    QnT = IN("QnT", [1024, T], BF16)
    QrT = IN("QrT", [512, T], BF16)
    w_kn = IN("w_kn", [256, 1024])
    w_v = IN("w_v", [256, 1024])
    naqT = IN("naqT", [1024, T], BF16)
    nak_h = IN("nak_h", [1024, 1536], BF16)
    nav_h = IN("nav_h", [1536, 1024], BF16)
    nak_c = IN("nak_c", [1024, TC], BF16)
    nav_c = IN("nav_c", [TC, 1024], BF16)
    BT = IN("BT", [8, 128, NA_TOT, 64])
    sgT = IN("sgT", [6144, T])
    w_bo = [IN(n, [1024, D]) for n in ("w_mla_o", "w_rg_o", "w_na_o")]
    w_out = IN("w_out", [D, D])
    w_g = IN("w_g", [D, FFN])
    w_u = IN("w_u", [D, FFN])
    w_d = IN("w_d", [FFN, D])
    xo = io.out("xo", [D, T], F32)
    broT = [io.tmp(n, [1024, T], BF16) for n in ("mlaoT", "rgoT", "naoT")]
    y2T = io.tmp("y2T", [D, T], F32)
    x1T = io.tmp("x1T", [D, T], F32)
    y3T = io.tmp("y3T", [D, T], F32)

    p.sb_push()
    mods = p.sb("mods", [128, 6, KC, 2], F32)
    p.dma("sp", mods[:], mods_i[:])

    p.sb_push()
    cw = p.sb("cw", [128, 8, 5], F32)
    rb = p.sb("rb", [128, 4, 8], F32)
    negc = p.sb("negc", [128, 2, 8], F32)
    oh = p.sb("oh", [128, 4], F32)
    rw = p.sb("rw", [128, 32, 128], BF16)
    p.dma("sp", cw[:], convT[:])
    p.dma("sp", rb[:], rg_b[:])
    p.dma("sp", negc[:], lamT[:])
    p.dma("sp", oh[:], onehot[:])
    p.dma("pool", rw[:], rg_w[:].rearrange("a c i j -> i (a c) j"))
    p.act(negc[:], negc[:], AF.Exp, scale=-1.0)
    p.act(negc[:], negc[:], AF.Ln, bias=1.0)
    p.ts(negc[:], negc[:], -8.0, None, ALU.mult)
    U = p.sb("U", [128, NKEY], F32)
    Y = p.sb("Y", [128, NKEY], F32)
    YB = p.sb("YB", [128, NKEY], BF16)
    Rd = [p.sb(f"R{d}", [128, NKEY], F32) for d in range(2)]
    Id = [p.sb(f"I{d}", [128, NKEY], F32) for d in range(2)]
    Sd = [p.sb(f"S{d}", [128, NKEY], F32) for d in range(2)]
    gt = p.sb("gt", [128, T], F32)
    g2 = p.sb("g2", [128, T], F32)
    rec = p.sb("rec", [128, T], F32)
    rgo_st = p.sb("rgo_st", [128, T], BF16)
    KB = [(i * 512, 512) for i in range(8)] + [(4096, 256)]
    bi = 0
    for c in range(8):
        p.dma("sp", U[:], u_all[c * 128:(c + 1) * 128, :])
        p.dma("act", gt[:], rgT[c * 128:(c + 1) * 128, :])
        for (s, e) in ((0, TC), (TC, NKEY)):
            p.ts(Y[:, s:e], U[:, s:e], cw[:, c, 2:3], cw[:, c, 4:5], ALU.mult, ALU.add)
            p.stt(Y[:, s + 2:e], U[:, s:e - 2], cw[:, c, 0:1], Y[:, s + 2:e], ALU.mult, ALU.add)
            p.stt(Y[:, s + 1:e], U[:, s:e - 1], cw[:, c, 1:2], Y[:, s + 1:e], ALU.mult, ALU.add)
            p.stt(Y[:, s:e - 1], U[:, s + 1:e], cw[:, c, 3:4], Y[:, s:e - 1], ALU.mult, ALU.add)
        p.copy(YB[:], Y[:], eng="pool")
        HF = U
        for d in range(2):
            Rb, Ib, Sb = Rd[d], Id[d], Sd[d]
            for (c0, n) in KB:
                psr = banks[bi % 8]
                psi = banks[(bi + 1) % 8]
                bi += 2
                p.mm(psr[:, 0:n], rw[:, d * 8 + c, :], YB[:, c0:c0 + n])
                p.mm(psi[:, 0:n], rw[:, 16 + d * 8 + c, :], YB[:, c0:c0 + n])
                p.act(Rb[:, c0:c0 + n], psr[:, 0:n], AF.Sigmoid, bias=rb[:, d, c:c + 1])
                p.act(Ib[:, c0:c0 + n], psi[:, 0:n], AF.Sigmoid, bias=rb[:, 2 + d, c:c + 1])
            p.act(Rb[:], Rb[:], AF.Exp, scale=negc[:, d, c:c + 1])
            p.tt(Ib[:], Ib[:], Y[:], ALU.mult)
            if d == 1:
                pass
            p.act(Sb[:], Rb[:], AF.Square)
            p.act(Sb[:], Sb[:], AF.Sqrt, bias=1.0, scale=-1.0)
            p.tt(Ib[:], Ib[:], Sb[:], ALU.mult)
            if d == 0:
                p.scan(HF[:, :], Rb[:, :], Ib[:, :], 0.0)
            else:
                HR = Sb
                p.scan(HR[:, 0:TC][:, ::-1], Rb[:, 0:TC][:, ::-1], Ib[:, 0:TC][:, ::-1], 0.0)
                p.scan(HR[:, TC:NKEY][:, ::-1], Rb[:, TC:NKEY][:, ::-1], Ib[:, TC:NKEY][:, ::-1], HR[:, 0:1])
        if wc:
            p.tt(rec[:, 0:TC], HF[:, 0:TC], HR[:, 0:TC], ALU.add)
        p.ts(rec[:, TC:T], HF[:, TC:TC + TL], oh[:, 0:1], None, ALU.mult)
        p.stt(rec[:, TC:T], HR[:, TC:TC + TL], oh[:, 0:1], rec[:, TC:T], ALU.mult, ALU.add)
        for r in range(1, 4):
            o0 = TC + r * TL
            p.stt(rec[:, TC:T], HF[:, o0:o0 + TL], oh[:, r:r + 1], rec[:, TC:T], ALU.mult, ALU.add)
            p.stt(rec[:, TC:T], HR[:, o0:o0 + TL], oh[:, r:r + 1], rec[:, TC:T], ALU.mult, ALU.add)
        p.act(g2[:, c_lo:T], gt[:, c_lo:T], AF.Square)
        p.ts(g2[:, c_lo:T], g2[:, c_lo:T], 0.044715, 1.0, ALU.mult, ALU.add)
        p.tt(g2[:, c_lo:T], g2[:, c_lo:T], gt[:, c_lo:T], ALU.mult)
        p.act(g2[:, c_lo:T], g2[:, c_lo:T], AF.Sigmoid, scale=2.0 * math.sqrt(2.0 / math.pi))
        p.tt(g2[:, c_lo:T], g2[:, c_lo:T], gt[:, c_lo:T], ALU.mult)
        p.tt(rgo_st[:, c_lo:T], g2[:, c_lo:T], rec[:, c_lo:T], ALU.mult)
        p.dma("sp", broT[1][c * 128:(c + 1) * 128, :], rgo_st[:])
    p.sb_pop()

    p.sb_push()
    kvn = p.sb("kvn", [128, 2, NKEY], BF16)
    kr = p.sb("kr", [64, NKEY], BF16)
    wkn = p.sb("wkn", [128, 2, 1024], BF16)
    wv = p.sb("wv", [128, 2, 1024], BF16)
    p.dma("sp", kvn[:], kvn_all[:].rearrange("(k p) n -> p k n", p=128))
    p.dma("sp", kr[:], kr_all[:])
    p.dma("pool", wkn[:], w_kn[:].rearrange("(k p) n -> p k n", p=128))
    p.dma("pool", wv[:], w_v[:].rearrange("(k p) n -> p k n", p=128))
    knT = [p.sb(f"knT{i}", [128, NKEY], BF16) for i in range(2)]
    Vh = [p.sb(f"Vh{i}", [128, 34, 128], BF16) for i in range(2)]
    Qn = [p.sb(f"Qn{i}", [128, T], BF16) for i in range(2)]
    Qr = [p.sb(f"Qr{i}", [64, T], BF16) for i in range(2)]
    Es = [p.sb(f"E{i}", [128, 512], BF16) for i in range(5)]
    SBK = [banks[0], banks[1], banks[6], banks[7]]
    rz = p.sb("rz", [128, 512], F32)
    accs = [p.sb(f"acc{i}", [128, 512], F32) for i in range(2)]
    ones32 = p.sb("ones32", [128, 128], F32)
    p.memset(ones32[:], 1.0)
    mst = [p.sb(f"mst{i}", [128, T], BF16) for i in range(2)]
    ei = 0
    for h in range(8):
        kT, V, qn_, qr_, stg = knT[h % 2], Vh[h % 2], Qn[h % 2], Qr[h % 2], mst[h % 2]
        p.dma("sp", qn_[:], QnT[h * 128:(h + 1) * 128, :])
        p.dma("sp", qr_[:], QrT[h * 64:(h + 1) * 64, :])
        for bidx, (c0, n) in enumerate(KB):
            ps = banks[6 + bidx % 2]
            for k in range(2):
                p.mm(ps[:, 0:n], wkn[:, k, h * 128:(h + 1) * 128], kvn[:, k, c0:c0 + n], start=(k == 0), stop=(k == 1))
            p.copy(kT[:, c0:c0 + n], ps[:, 0:n], eng="dve" if bidx % 2 else "act")
        for g in range(9):
            ncc = 4 if g < 8 else 2
            ps = banks[6 + g % 2]
            for cc in range(ncc):
                kc = g * 4 + cc
                for k in range(2):
                    p.mm(ps[:, cc * 128:(cc + 1) * 128], kvn[:, k, kc * 128:(kc + 1) * 128],
                         wv[:, k, h * 128:(h + 1) * 128], start=(k == 0), stop=(k == 1))
            p.copy(V[:, g * 4:g * 4 + ncc, :].rearrange("p c d -> p (c d)"), ps[:, 0:ncc * 128],
                   eng="act" if g % 2 else "dve")
        qblocks = [(TC, 512, 34), (TC + 512, 512, 34)] + ([(0, TC, 2)] if wc else [])
        for qb, (q0, qn_cols, nkc) in enumerate(qblocks):
            O = banks[2 + 2 * (qb % 2)]
            Z = banks[3 + 2 * (qb % 2)]

            def s_mm(kc):
                S = SBK[kc % 4]
                p.mm(S[:, 0:qn_cols], kT[:, kc * 128:(kc + 1) * 128], qn_[:, q0:q0 + qn_cols], start=True, stop=False)
                p.mm(S[:, 0:qn_cols], kr[0:64, kc * 128:(kc + 1) * 128], qr_[0:64, q0:q0 + qn_cols], start=False, stop=True)

            for kk in range(min(3, nkc)):
                s_mm(kk)
            for kc in range(nkc):
                if kc + 3 < nkc:
                    s_mm(kc + 3)
                S = SBK[kc % 4]
                E = Es[ei % 5]
                ei += 1
                p.act(E[:, 0:qn_cols], S[:, 0:qn_cols], AF.Exp, scale=MLA_SCALE)
                p.mm(O[:, 0:qn_cols], V[:, kc, :], E[:, 0:qn_cols], start=(kc == 0), stop=(kc == nkc - 1))
                acc = accs[qb % 2]
                if kc == 0:
                    p.copy(acc[:, 0:qn_cols], E[:, 0:qn_cols])
                else:
                    p.tt(acc[:, 0:qn_cols], acc[:, 0:qn_cols], E[:, 0:qn_cols], ALU.add)
            p.mm(Z[:, 0:qn_cols], ones32[:], accs[qb % 2][:, 0:qn_cols])
            p.recip(rz[:, 0:qn_cols], Z[:, 0:qn_cols])
            p.tt(stg[:, q0:q0 + qn_cols], O[:, 0:qn_cols], rz[:, 0:qn_cols], ALU.mult)
        p.dma("sp", broT[0][h * 128:(h + 1) * 128, :], stg[:])
    p.sb_pop()

    p.sb_push()
    qT = [p.sb(f"qT{i}", [128, T], BF16) for i in range(2)]
    kTh = [p.sb(f"kTh{i}", [128, 1536], BF16) for i in range(2)]
    kTc = [p.sb(f"kTc{i}", [128, TC], BF16) for i in range(2)]
    Va = [p.sb(f"Va{i}", [128, 12, 128], BF16) for i in range(2)]
    Vb = [p.sb(f"Vb{i}", [128, 11, 128], BF16) for i in range(2)]
    Vc = [p.sb(f"Vc{i}", [128, 2, 128], BF16) for i in range(2)]
    bias = [p.sb(f"bias{i}", [128, NA_TOT, 64], F32) for i in range(2)]
    tS = [p.sb(f"tS{i}", [128, 512], F32) for i in range(3)]
    En = [p.sb(f"En{i}", [128, 512], BF16) for i in range(3)]
    NSB = [banks[0], banks[1], banks[6]]
    rzn = p.sb("rzn", [128, 512], F32)
    nst = [p.sb(f"nst{i}", [128, T], BF16) for i in range(2)]
    for h in range(8):
        x = h % 2
        hs = slice(h * 128, (h + 1) * 128)
        p.dma("sp", qT[x][:], naqT[hs, :])
        p.dma("sp", kTh[x][:], nak_h[hs, :])
        p.dma("sp", kTc[x][:], nak_c[hs, :])
        p.dma("act", Va[x][:], nav_h[:, hs].rearrange("(c p) d -> p c d", p=128))
        p.dma("act", Vb[x][:], nav_h[64:64 + 1408, hs].rearrange("(c p) d -> p c d", p=128))
        p.dma("act", Vc[x][:], nav_c[:, hs].rearrange("(c p) d -> p c d", p=128))
        p.dma("sp", bias[x][:], BT[h])
        stg = nst[x]

        def s_row(i):
            lo, n = NA_WIN[i]
            nch = n // 2
            S = NSB[i % 3]
            q0 = TC + i * 64
            for c in range(nch):
                k0 = lo * 64 + c * 128
                p.mm(S[:, c * 64:(c + 1) * 64], kTh[x][:, k0:k0 + 128], qT[x][:, q0:q0 + 64])
            for c in range(2):
                p.mm(S[:, (nch + c) * 64:(nch + c + 1) * 64], kTc[x][:, c * 128:(c + 1) * 128], qT[x][:, q0:q0 + 64])

        s_row(0)
        s_row(1)
        for i in range(16):
            if i + 2 < 16:
                s_row(i + 2)
            lo, n = NA_WIN[i]
            nch = n // 2
            w = (nch + 2) * 64
            S = NSB[i % 3]
            grp = i // 8
            O = banks[2 + 2 * grp]
            Z = banks[3 + 2 * grp]
            oc = (i % 8) * 64
            ch0 = NA_CH0[i]
            p.stt(tS[i % 3][:, 0:w], S[:, 0:w], NA_SCALE,
                  bias[x][:, ch0:ch0 + nch + 2, :].rearrange("p c q -> p (c q)"), ALU.mult, ALU.add)
            p.act(En[i % 3][:, 0:w], tS[i % 3][:, 0:w], AF.Exp)
            for c in range(nch + 2):
                if c < nch:
                    vv = Va[x][:, lo // 2 + c, :] if lo % 2 == 0 else Vb[x][:, (lo - 1) // 2 + c, :]
                else:
                    vv = Vc[x][:, c - nch, :]
                p.mm(O[:, oc:oc + 64], vv, En[i % 3][:, c * 64:(c + 1) * 64], start=(c == 0), stop=(c == nch + 1))
            for c in range(nch + 2):
                p.mm(Z[:, oc:oc + 64], ones[:], En[i % 3][:, c * 64:(c + 1) * 64], start=(c == 0), stop=(c == nch + 1))
            if i % 8 == 7:
                p.recip(rzn[:], Z[:])
                p.tt(stg[:, TC + grp * 512:TC + (grp + 1) * 512], O[:], rzn[:], ALU.mult)
        if wc:
            S = banks[6]
            O = banks[7]
            Z = banks[0]
            for c in range(2):
                p.mm(S[:, c * 256:(c + 1) * 256], kTc[x][:, c * 128:(c + 1) * 128], qT[x][:, 0:TC])
            p.act(En[0][:], S[:], AF.Exp, scale=NA_SCALE)
            for c in range(2):
                p.mm(O[:, 0:TC], Vc[x][:, c, :], En[0][:, c * 256:(c + 1) * 256], start=(c == 0), stop=(c == 1))
            for c in range(2):
                p.mm(Z[:, 0:TC], ones[:], En[0][:, c * 256:(c + 1) * 256], start=(c == 0), stop=(c == 1))
            p.recip(rzn[:, 0:TC], Z[:, 0:TC])
            p.tt(stg[:, 0:TC], O[:, 0:TC], rzn[:, 0:TC], ALU.mult)
        p.dma("sp", broT[2][hs, :], stg[:])
    p.sb_pop()

    yT = p.sb("yT", [128, KC, T], BF16)
    off_after_yT = p.sb_off
    p.sb_push()
    bro = [p.sb(f"bro{i}", [128, 8, T], BF16) for i in range(3)]
    for i in range(3):
        p.dma("sp", bro[i][:], broT[i][:].rearrange("(k p) n -> p k n", p=128))
    wbo = [[p.sb(f"wbo{i}_{j}", [128, 8, 256], BF16) for j in range(2)] for i in range(3)]
    sgs = [p.sb(f"sgs{i}", [128, 3, T], F32) for i in range(2)]
    ta = p.sb("ta", [128, 512], F32)
    tb = p.sb("tb", [128, 512], F32)
    sg_v = sgT[:].rearrange("(b f p) n -> f p b n", b=3, p=128)
    for fg in range(8):
        for i in range(3):
            p.dma("pool", wbo[i][fg % 2][:], w_bo[i][:, fg * 256:(fg + 1) * 256].rearrange("(k p) n -> p k n", p=128))
        for fj in range(2):
            f = fg * 2 + fj
            sgt = sgs[f % 2]
            p.dma("act", sgt[:], sg_v[f])
            for (c0, n) in LB:
                pss = [banks[(f * 3 + i) % 6] for i in range(3)] if False else [banks[i] for i in range(3)]
                for i in range(3):
                    for k in range(8):
                        p.mm(pss[i][:, 0:n], wbo[i][fg % 2][:, k, fj * 128:(fj + 1) * 128], bro[i][:, k, c0:c0 + n],
                             start=(k == 0), stop=(k == 7))
                p.tt(ta[:, 0:n], pss[0][:, 0:n], sgt[:, 0, c0:c0 + n], ALU.mult)
                p.tt(tb[:, 0:n], pss[1][:, 0:n], sgt[:, 1, c0:c0 + n], ALU.mult)
                p.tt(ta[:, 0:n], ta[:, 0:n], tb[:, 0:n], ALU.add)
                p.tt(tb[:, 0:n], pss[2][:, 0:n], sgt[:, 2, c0:c0 + n], ALU.mult)
                p.tt(yT[:, f, c0:c0 + n], ta[:, 0:n], tb[:, 0:n], ALU.add)
    p.sb_pop()

    def chunk_ssq(src_f32, first):
        for (c0, n) in LB:
            p.act(sqb[:, 0:n], src_f32[:, c0:c0 + n], AF.Square)
            p.mm(banks[7][:, 0:n], ones[:], sqb[:, 0:n])
            if first:
                p.copy(ssq[:, c0:c0 + n], banks[7][:, 0:n])
            else:
                p.tt(ssq[:, c0:c0 + n], ssq[:, c0:c0 + n], banks[7][:, 0:n], ALU.add)

    def finish_rstd():
        p.ts(rstd[:, c_lo:T], ssq[:, c_lo:T], 1.0 / D, EPS, ALU.mult, ALU.add)
        p.act(rstd[:, c_lo:T], rstd[:, c_lo:T], AF.Sqrt)
        p.recip(rstd[:, c_lo:T], rstd[:, c_lo:T])

    p.sb_push()
    ssq = p.sb("ssq", [128, T], F32)
    rstd = p.sb("rstd", [128, T], F32)
    sqb = p.sb("sqb", [128, 512], BF16)
    wo = [p.sb(f"wo{i}", [128, KC, 512], BF16) for i in range(2)]
    st = [p.sb(f"st{i}", [128, T], F32) for i in range(2)]
    xst = [p.sb(f"xst{i}", [128, T], F32) for i in range(2)]
    bi = 0
    for fg in range(4):
        p.dma("pool", wo[fg % 2][:], w_out[:, fg * 512:(fg + 1) * 512].rearrange("(k p) n -> p k n", p=128))
        for fj in range(4):
            f = fg * 4 + fj
            stg = st[f % 2]
            for (c0, n) in LB:
                ps = banks[bi % 6]
                bi += 1
                for k in range(KC):
                    p.mm(ps[:, 0:n], wo[fg % 2][:, k, fj * 128:(fj + 1) * 128], yT[:, k, c0:c0 + n],
                         start=(k == 0), stop=(k == KC - 1))
                p.copy(stg[:, c0:c0 + n], ps[:, 0:n], eng="act" if bi % 2 else "dve")
            chunk_ssq(stg, f == 0)
            p.dma("sp", y2T[f * 128:(f + 1) * 128, :], stg[:])
    finish_rstd()

    def residual_pass(srcT, resT, dstT, gsel, want_ssq):
        for f in range(KC):
            a = st[f % 2]
            b = xst[f % 2]
            p.dma("sp", a[:], srcT[f * 128:(f + 1) * 128, :])
            p.dma("act", b[:], resT[f * 128:(f + 1) * 128, :])
            for (s0, sn, cnd) in SEGS:
                p.stt(a[:, s0:s0 + sn], a[:, s0:s0 + sn], mods[:, gsel, f, cnd:cnd + 1], rstd[:, s0:s0 + sn],
                      ALU.mult, ALU.mult)
            p.tt(a[:, c_lo:T], a[:, c_lo:T], b[:, c_lo:T], ALU.add)
            if want_ssq:
                chunk_ssq(a, f == 0)
            p.dma("sp", dstT[f * 128:(f + 1) * 128, :], a[:])

    residual_pass(y2T, xT, x1T, 2, True)
    finish_rstd()
    h2 = yT
    for f in range(KC):
        a = st[f % 2]
        p.dma("sp", a[:], x1T[f * 128:(f + 1) * 128, :])
        for (s0, sn, cnd) in SEGS:
            p.stt(a[:, s0:s0 + sn], a[:, s0:s0 + sn], mods[:, 3, f, cnd:cnd + 1], rstd[:, s0:s0 + sn], ALU.mult, ALU.mult)
            p.act(h2[:, f, s0:s0 + sn], a[:, s0:s0 + sn], AF.Identity, bias=mods[:, 4, f, cnd:cnd + 1])
    p.sb_pop()

    p.sb_push()
    actT = p.sb("actT", [128, HC, T], BF16)
    p.sb_push()
    wgt = [p.sb(f"wg{i}", [128, KC, 128], BF16) for i in range(2)]
    wut = [p.sb(f"wu{i}", [128, KC, 128], BF16) for i in range(2)]
    sl = [p.sb(f"sl{i}", [128, 512], F32) for i in range(2)]
    w_g_v = w_g[:].rearrange("(k p) n -> p k n", p=128)
    w_u_v = w_u[:].rearrange("(k p) n -> p k n", p=128)
    bi = 0
    for hcn in range(HC):
        wg_, wu_ = wgt[hcn % 2], wut[hcn % 2]
        p.dma("pool", wg_[:], w_g_v[:, :, hcn * 128:(hcn + 1) * 128])
        p.dma("pool", wu_[:], w_u_v[:, :, hcn * 128:(hcn + 1) * 128])
        for (c0, n) in LB:
            pg = banks[(2 * bi) % 8]
            pu = banks[(2 * bi + 1) % 8]
            bi += 1
            for k in range(KC):
                p.mm(pg[:, 0:n], wg_[:, k, :], h2[:, k, c0:c0 + n], start=(k == 0), stop=(k == KC - 1))
            for k in range(KC):
                p.mm(pu[:, 0:n], wu_[:, k, :], h2[:, k, c0:c0 + n], start=(k == 0), stop=(k == KC - 1))
            s_ = sl[bi % 2]
            p.act(s_[:, 0:n], pg[:, 0:n], AF.Silu)
            p.tt(actT[:, hcn, c0:c0 + n], s_[:, 0:n], pu[:, 0:n], ALU.mult)
    p.sb_pop()
    p.barrier()
    cur = p.sb_off
    p.sb_off = off_after_yT - KC * T * 2
    st3 = [p.sb(f"st3_{i}", [128, T], F32) for i in range(2)]
    x3 = [p.sb(f"x3_{i}", [128, T], F32) for i in range(2)]
    ssq = p.sb("ssq3", [128, T], F32)
    rstd = p.sb("rstd3", [128, T], F32)
    sqb = p.sb("sqb3", [128, 512], BF16)
    assert p.sb_off <= off_after_yT
    p.sb_off = cur
    wdt = [p.sb(f"wd{i}", [128, 22, 256], BF16) for i in range(2)]
    w_d_v = w_d[:].rearrange("(k p) n -> p k n", p=128)
    ti = 0
    for fg in range(8):
        for half in range(2):
            wd_ = wdt[ti % 2]
            ti += 1
            p.dma("pool", wd_[:], w_d_v[:, half * 22:(half + 1) * 22, fg * 256:(fg + 1) * 256])
            for fj in range(2):
                for bidx, (c0, n) in enumerate(LB):
                    ps = banks[fj * 3 + bidx]
                    for k in range(22):
                        p.mm(ps[:, 0:n], wd_[:, k, fj * 128:(fj + 1) * 128], actT[:, half * 22 + k, c0:c0 + n],
                             start=(half == 0 and k == 0), stop=(half == 1 and k == 21))
        for fj in range(2):
            f = fg * 2 + fj
            stg = st3[f % 2]
            for bidx, (c0, n) in enumerate(LB):
                p.copy(stg[:, c0:c0 + n], banks[fj * 3 + bidx][:, 0:n], eng="act" if bidx % 2 else "dve")
            chunk_ssq(stg, f == 0)
            p.dma("sp", y3T[f * 128:(f + 1) * 128, :], stg[:])
    finish_rstd()
    st, xst = st3, x3
    residual_pass(y3T, x1T, xo, 5, False)
    p.sb_pop()
    p.sb_pop()


def na_bias_table(rpb, j):
    out = np.zeros((8, 128, NA_TOT, 64), np.float32)
    qc = np.arange(64)
    kc = np.arange(64)
    c_start = np.clip(qc - 8, 0, 48)
    col_in = (kc[:, None] >= c_start[None, :]) & (kc[:, None] < c_start[None, :] + 16)
    dc = np.clip(kc[:, None] - qc[None, :] + 15, 0, 30)
    for i in range(16):
        r = 16 * j + i
        lo, n = NA_WIN[i]
        kr0 = min(max(r - 4, 0), 56)
        tab = np.full((8, n * 64, 64), -1e30, np.float32)
        for a in range(lo, lo + n):
            hr = 16 * j - 4 + a
            if kr0 <= hr < kr0 + 8:
                vals = rpb[:, hr - r + 7, :][:, dc]
                vals = np.where(col_in[None], vals, np.float32(-1e30))
                tab[:, (a - lo) * 64:(a - lo + 1) * 64, :] = vals
        nch = n // 2
        ch0 = NA_CH0[i]
        out[:, :, ch0:ch0 + nch, :] = tab.reshape(8, nch, 128, 64).transpose(0, 2, 1, 3)
    return out


GROUPS = [[0, 1, 2, 3], [4, 5, 6, 7]]
DEPTH_TOTAL = DEPTH
PK = 2368


def emit_mods(p, banks, nlayers):
    outs = []
    p.sb_push()
    cond = p.sb("cond", [128, KC, 2], F32)
    sil = p.sb("sil", [128, KC, 2], F32)
    condT = p.dram("condT", [128, KC, 2], F32, kind="ExternalInput")
    p.dma("sp", cond[:], condT[:])
    p.act(sil[:], cond[:], AF.Silu)
    wts = [p.sb(f"wada{i}", [128, KC, 512], F32) for i in range(2)]
    ti = 0
    gathered = []
    for l in range(nlayers):
        w_q = p.dram(f"w_ada_q_l{l}", [D, 3072], F32, kind="ExternalInput")
        b_q = p.dram(f"b_adaT_q_l{l}", [128, 24], F32, kind="ExternalInput")
        bq = p.sb(f"bq{l}", [128, 24], F32)
        part = p.sb(f"part{l}", [128, 24, 2], F32)
        p.dma("sp", bq[:], b_q[:])
        w_v = w_q[:].rearrange("(k p) n -> p k n", p=128)
        psm = banks[l % 2]
        for jg in range(6):
            wt = wts[ti % 2]
            p.dma("sp" if ti % 2 == 0 else "act", wt[:], w_v[:, :, jg * 512:(jg + 1) * 512])
            ti += 1
            for jj in range(4):
                j = jg * 4 + jj
                for k in range(KC):
                    p.mm(psm[:, 2 * j:2 * j + 2], wt[:, k, jj * 128:(jj + 1) * 128], sil[:, k, :],
                         start=(k == 0), stop=(k == KC - 1))
        for c in range(2):
            p.tt(part[:, :, c], psm[:, 0:48].rearrange("p (j c) -> p j c", c=2)[:, :, c], bq[:], ALU.add)
        mpk = p.dram(f"mpk_l{l}", [128, 48], F32)
        mg = p.dram(f"mg_l{l}", [512, 48], F32)
        p.dma("sp", mpk[:], part[:].rearrange("p j c -> p (j c)"))
        gathered.append((mg, mpk))
    p.barrier()
    for (mg, mpk) in gathered:
        p.allgather(mg[:], mpk[:], GROUPS)
    p.barrier()
    gathered = [g_[0] for g_ in gathered]
    for l in range(nlayers):
        modv = p.sb(f"modv{l}", [128, 96, 2], F32)
        gts = p.sb(f"gts{l}", [128, 4, KC], F32)
        mods = p.sb(f"modsb{l}", [128, 6, KC, 2], F32)
        gT = p.dram(f"gT_l{l}", [128, 4, KC], F32, kind="ExternalInput")
        p.dma("sp", gts[:], gT[:])
        for r in range(4):
            p.dma("sp", modv[:, 24 * r:24 * r + 24, :].rearrange("p j c -> p (j c)"), gathered[l][r * 128:(r + 1) * 128, :])
        for c in range(2):
            p.stt(mods[:, 0, :, c], modv[:, 16:32, c], 1.0, gts[:, 0, :], ALU.add, ALU.mult)
            p.copy(mods[:, 1, :, c], modv[:, 0:16, c])
            p.tt(mods[:, 2, :, c], modv[:, 32:48, c], gts[:, 1, :], ALU.mult)
            p.stt(mods[:, 3, :, c], modv[:, 64:80, c], 1.0, gts[:, 2, :], ALU.add, ALU.mult)
            p.copy(mods[:, 4, :, c], modv[:, 48:64, c])
            p.tt(mods[:, 5, :, c], modv[:, 80:96, c], gts[:, 3, :], ALU.mult)
        mo = p.dram(f"mods_l{l}", [128, 6, KC, 2], F32)
        p.dma("sp", mo[:], mods[:])
        outs.append(mo)
    p.sb_pop()
    return outs


def build_fused(nlayers=DEPTH):
    p = Prog()
    banks = [p.ps(f"bank{i}", [128, 512]) for i in range(8)]
    ones = p.sb("ones", [128, 128], BF16)
    p.memset(ones[:], 1.0)
    ohs = p.sb("ohs", [128, 2, 4], F32)
    oh_in = p.dram("ohpn", [128, 2, 4], F32, kind="ExternalInput")
    p.dma("sp", ohs[:], oh_in[:])
    x_in = p.dram("xT", [D, T], F32, kind="ExternalInput")
    xo_final = p.dram("xo", [D, T], F32, kind="ExternalOutput")
    mods_l = emit_mods(p, banks, nlayers)
    for l in range(nlayers):
        sfx = f"_l{l}"
        ioA = IO(p, sfx, {"xT": x_in, "mods": mods_l[l]})
        wc = l < DEPTH_TOTAL - 1 if nlayers == DEPTH else True
        emit_A(p, ioA, banks, ones, wc)
        A = ioA.made
        q2 = ("sp", "act")
        specs = [("kv", 320), ("k0", 512), ("k1", 512), ("v0", 512), ("v1", 512)]
        pk = {n: p.dram(f"pk_{n}{sfx}", [r, TL], BF16) for n, r in specs}
        g = {n: p.dram(f"g_{n}{sfx}", [4 * r, TL], BF16) for n, r in specs}
        p.dma("sp", pk["kv"][0:256, :], A["kvnT"][:, TC:])
        p.dma("sp", pk["kv"][256:320, :], A["krT"][:, TC:])
        for hf in range(2):
            p.dma(q2[hf], pk[f"k{hf}"][:, :], A["nakT"][hf * 512:(hf + 1) * 512, TC:])
            p.dma(q2[hf], pk[f"v{hf}"][:, :], A["nav"][TC + hf * 512:TC + (hf + 1) * 512, :])
        pku = [p.dram(f"pk_u{k}{sfx}", [256, TL], F32) for k in range(4)]
        gu = [p.dram(f"g_u{k}{sfx}", [1024, TL], F32) for k in range(4)]
        for k in range(4):
            p.dma(q2[k % 2], pku[k][:, :], A["uT"][k * 256:(k + 1) * 256, TC:])
        p.barrier()
        for n, _ in specs:
            p.allgather(g[n][:], pk[n][:], GROUPS)
        for k in range(4):
            p.allgather(gu[k][:], pku[k][:], GROUPS)
        p.barrier()
        u_all = p.dram("u_all" + sfx, [1024, NKEY], F32)
        kvn_all = p.dram("kvn_all" + sfx, [256, NKEY], BF16)
        kr_all = p.dram("kr_all" + sfx, [64, NKEY], BF16)
        nak_h = p.dram("nak_h" + sfx, [1024, 1536], BF16)
        nav_h = p.dram("nav_h" + sfx, [1536, 1024], BF16)
        p.dma("sp", u_all[:, 0:TC], A["uT"][:, 0:TC])
        p.dma("sp", kvn_all[:, 0:TC], A["kvnT"][:, 0:TC])
        p.dma("sp", kr_all[:, 0:TC], A["krT"][:, 0:TC])
        for r in range(4):
            q = q2[r % 2]
            cs = slice(TC + r * TL, TC + (r + 1) * TL)
            for k in range(4):
                p.dma(q, u_all[k * 256:(k + 1) * 256, cs], gu[k][r * 256:(r + 1) * 256, :])
            p.dma(q, kvn_all[:, cs], g["kv"][r * 320:r * 320 + 256, :])
            p.dma(q, kr_all[:, cs], g["kv"][r * 320 + 256:r * 320 + 320, :])
        p.dma("sp", nak_h[:, 256:1280], A["nakT"][:, TC:])
        p.dma("act", nav_h[256:1280, :], A["nav"][TC:, :])
        p.sb_push()
        candk = p.sb("candk", [128, 4, 8, 256], BF16)
        candv = p.sb("candv", [128, 4, 2, 1024], BF16)
        selk = p.sb("selk", [128, 8, 256], BF16)
        selv = p.sb("selv", [128, 2, 1024], BF16)
        for side in range(2):
            c0 = 768 if side == 0 else 0
            d0 = 0 if side == 0 else 1280
            gv = g["v1"] if side == 0 else g["v0"]
            v0 = 256 if side == 0 else 0
            for r in range(4):
                for hf in range(2):
                    p.dma("sp", candk[:, r, 4 * hf:4 * hf + 4, :],
                          g[f"k{hf}"][r * 512:(r + 1) * 512, c0:c0 + 256].rearrange("(h p) n -> p h n", p=128))
                p.dma("act", candv[:, r, :, :],
                      gv[r * 512 + v0:r * 512 + v0 + 256, :].rearrange("(c p) f -> p c f", p=128))
            for (cand, sel) in ((candk, selk), (candv, selv)):
                cf = lambda r_: cand[:, r_, :, :].rearrange("p a b -> p (a b)")
                sf = sel[:].rearrange("p a b -> p (a b)")
                p.ts(sf, cf(0), ohs[:, side, 0:1], None, ALU.mult)
                for r in range(1, 4):
                    p.stt(sf, cf(r), ohs[:, side, r:r + 1], sf, ALU.mult, ALU.add)
            p.dma("sp", nak_h[:, d0:d0 + 256].rearrange("(h p) n -> p h n", p=128), selk[:])
            p.dma("sp", nav_h[d0:d0 + 256, :].rearrange("(c p) f -> p c f", p=128), selv[:])
        p.sb_pop()
        bound = {"xT": x_in, "mods": mods_l[l], "u_all": u_all, "rgT": A["rgT"], "kvn_all": kvn_all,
                 "kr_all": kr_all, "QnT": A["QnT"], "QrT": A["QrT"], "naqT": A["naqT"], "nak_h": nak_h,
                 "nav_h": nav_h, "nak_c": A["nakT"][:, 0:TC], "nav_c": A["nav"][0:TC, :], "sgT": A["sgT"]}
        if l == nlayers - 1:
            bound["xo"] = xo_final
        ioB = IO(p, sfx, bound)
        emit_B(p, ioB, banks, ones, wc)
        if l < nlayers - 1:
            x_in = ioB.made["xo"]
            p.new_epoch()
    return p.build()


def layer_inputs(inp, l, core):
    b, j = core // 4, core % 4
    C2, S2, pm = rope_tables()
    w_in = inp["w_in"][l]
    wq = inp["w_q_b"][l].reshape(512, 8, 192)
    wkv = inp["w_kv_b"][l].reshape(256, 8, 256)
    oh = np.zeros((128, 4), np.float32)
    oh[:, j] = 1.0
    return dict(
        w_ada_q=np.ascontiguousarray(inp["w_ada"][l][:, j * 3072:(j + 1) * 3072]),
        b_adaT_q=np.ascontiguousarray(fm(inp["b_ada"][l])[:, j * 24:(j + 1) * 24]),
        gT=np.ascontiguousarray(np.stack([fm(inp[k][l]) for k in ("g_mix_pre", "g_mix_post", "g_ffn_pre", "g_ffn_post")], 1)),
        w_fm=np.ascontiguousarray(np.concatenate([w_in[:, a:b_] for a, b_ in SEC], axis=1)),
        w_nav=np.ascontiguousarray(w_in[:, 4928:5952]),
        gqT=fm(inp["g_q_a"][l]), gkvT=fm(inp["g_kv_a"][l]),
        w_qn=np.ascontiguousarray(wq[:, :, :128].reshape(512, 1024)),
        w_qr=np.ascontiguousarray(wq[:, :, 128:].reshape(512, 512)),
        ropeC=np.ascontiguousarray(C2[:, j * TL:(j + 1) * TL]), ropeS=np.ascontiguousarray(S2[:, j * TL:(j + 1) * TL]),
        perm=pm,
        convT=np.ascontiguousarray(np.concatenate(
            [np.stack([fm(inp["conv_w"][l][t]) for t in range(4)], -1), fm(inp["conv_b"][l])[:, :, None]], -1)),
        rg_w=np.ascontiguousarray(np.concatenate([inp["rg_wa"][l], inp["rg_wx"][l]], 0)),
        rg_b=np.ascontiguousarray(np.stack([fm(inp["rg_ba"][l][0]), fm(inp["rg_ba"][l][1]),
                                            fm(inp["rg_bx"][l][0]), fm(inp["rg_bx"][l][1])], 1)),
        lamT=np.ascontiguousarray(np.stack([fm(inp["rg_lambda"][l][0]), fm(inp["rg_lambda"][l][1])], 1)),
        onehot=oh,
        w_kn=np.ascontiguousarray(wkv[:, :, :128].reshape(256, 1024)),
        w_v=np.ascontiguousarray(wkv[:, :, 128:].reshape(256, 1024)),
        BT=na_bias_table(inp["na_rpb"][l], j),
        w_mla_o=inp["w_mla_o"][l], w_rg_o=inp["w_rg_o"][l], w_na_o=inp["w_na_o"][l], w_out=inp["w_out"][l],
        w_g=inp["w_ffn_gate"][l], w_u=inp["w_ffn_up"][l], w_d=inp["w_ffn_down"][l])


_NC = {}


LAYERS_PER_LAUNCH = 2


def kernel(**inputs):
    inp = {k: np.asarray(v) for k, v in inputs.items()}
    if "F" not in _NC:
        _NC["F"] = build_fused(LAYERS_PER_LAUNCH)
    xTs = initial_xT(inp)
    for l0 in range(0, DEPTH, LAYERS_PER_LAUNCH):
        cache = {}
        maps = []
        for core in range(8):
            b, j = core // 4, core % 4
            ohpn = np.zeros((128, 2, 4), np.float32)
            ohpn[:, 0, (j - 1) % 4] = 1.0
            ohpn[:, 1, (j + 1) % 4] = 1.0
            cond = np.stack([inp["c"][b], inp["c_ctx"]], 0)
            m = {"xT": xTs[core], "ohpn": ohpn,
                 "condT": np.ascontiguousarray(cond.reshape(2, KC, 128).transpose(2, 1, 0))}
            for li_ in range(LAYERS_PER_LAUNCH):
                li = layer_inputs(inp, l0 + li_, core)
                for k, v in li.items():
                    if k not in ("w_ada_q", "b_adaT_q", "ropeC", "ropeS", "onehot", "BT"):
                        v = cache.setdefault((k, li_), v)
                    m[f"{k}_l{li_}"] = np.ascontiguousarray(v)
            maps.append(m)
        res = run_bass_kernel_spmd(_NC["F"], maps, core_ids=list(range(8)))
        xTs = [np.ascontiguousarray(np.asarray(r["xo"])) for r in res.results]
    out = np.zeros((2, SEQ, D), np.float32)
    for core in range(8):
        b, j = core // 4, core % 4
        out[b, j * TL:(j + 1) * TL, :] = xTs[core][:, TC:].T
    return out
```

```python
import math
import numpy as np
import ml_dtypes
import concourse.bass as bass
import concourse.mybir as mybir
from concourse.bass_utils import run_bass_kernel_spmd

F32 = mybir.dt.float32
BF16 = mybir.dt.bfloat16
AF = mybir.ActivationFunctionType
ALU = mybir.AluOpType

ENGS = ("pe", "act", "dve", "pool", "sp")
DT_SIZE = {F32: 4, BF16: 2}

D = 2048
KC = 16
TC = 256
TL = 1024
T = TC + TL
SEQ = 4096
NKEY = TC + SEQ
GRID_W = 64
DEPTH = 2
FFN = 5632
HC = FFN // 128
EPS = 1e-6
MLA_SCALE = 1.0 / math.sqrt(192.0)
NA_SCALE = 1.0 / math.sqrt(128.0)
BLKS = [(0, 512), (512, 512), (1024, 256)]
LBLKS = [(256, 512), (768, 512)]
SEC = [(0, 512), (512, 768), (832, 1856), (1856, 2880), (2880, 3904), (3904, 4928),
       (5952, 8000), (8000, 10048), (10048, 12096), (768, 832)]
NFM = sum(b - a for a, b in SEC)
NA_WIN = []
for _i in range(16):
    if _i < 4:
        lo, n = _i, 12 - _i
    elif _i <= 12:
        lo, n = _i, 8
    else:
        lo, n = 12, _i - 4
    n += n % 2
    NA_WIN.append((lo, n))
NA_NCH = [n // 2 + 2 for _, n in NA_WIN]
NA_CH0 = [int(x) for x in np.cumsum([0] + NA_NCH[:-1])]
NA_TOT = int(sum(NA_NCH))


class Buf:
    def __init__(self, name, ap, shape=None, dtype=None):
        self.name = name
        self.ap = ap
        self.shape = shape
        self.dtype = dtype
        self.w = None
        self.r = []

    def __getitem__(self, key):
        return View(self, self.ap[key])


class View:
    def __init__(self, buf, ap):
        self.buf = buf
        self.ap = ap

    def __getitem__(self, key):
        return View(self.buf, self.ap[key])

    def rearrange(self, s, **kw):
        return View(self.buf, self.ap.rearrange(s, **kw))


def _v(x):
    return x.ap if isinstance(x, View) else x


class Prog:
    def __init__(self):
        self.nc = bass.Bass("TRN2", target_bir_lowering=False)
        self.ops = {e: [] for e in ENGS}
        self.cnt = {}
        self.known = {e: {} for e in ENGS}
        self.ndma = {"sp": 0, "pool": 0, "act": 0}
        self.NDMASEM = 8
        self.sb_off = 16640
        self.sb_mark = []
        self.names = 0
        self.SB_BYTES = 192 * 1024
        self.ep = 0

    def dram(self, name, shape, dtype, kind="Internal"):
        t = self.nc.dram_tensor(name, list(shape), dtype, kind=kind)
        return Buf(name, t.ap(), shape, dtype)

    def sb(self, name, shape, dtype):
        per_part = int(np.prod(shape[1:])) * DT_SIZE[dtype]
        off = (self.sb_off + 63) // 64 * 64
        assert off + per_part <= self.SB_BYTES, f"SBUF overflow at {name}: {off}+{per_part}"
        self.names += 1
        h = self.nc.alloc_sbuf_tensor_at(f"{name}_{self.names}", list(shape), dtype, offset=off)
        self.sb_off = off + per_part
        return Buf(name, h.ap(), shape, dtype)

    def sb_push(self):
        self.sb_mark.append(self.sb_off)

    def sb_pop(self):
        self.barrier()
        self.sb_off = self.sb_mark.pop()

    def ps(self, name, shape, dtype=F32):
        self.names += 1
        h = self.nc.alloc_psum_tensor(f"{name}_{self.names}", list(shape), dtype)
        return Buf(name, h.ap(), shape, dtype)

    def _deps(self, eng, reads, writes):
        waits = {}

        def add(tok):
            if tok is None:
                return
            k, v = tok
            if eng == "pe" and k.split(".")[0] == "pe":
                return
            if waits.get(k, -1) < v:
                waits[k] = v

        for b in reads:
            add(b.w)
        for b in writes:
            add(b.w)
            for t in b.r:
                add(t)
        out = []
        kn = self.known[eng]
        for k, v in waits.items():
            if kn.get(k, -1) >= v:
                continue
            kn[k] = v
            out.append((k, v))
        return out

    def _mark(self, tok, reads, writes):
        for b in reads:
            b.r.append(tok)
            if len(b.r) > 48:
                m = {}
                for k, v in b.r:
                    if m.get(k, -1) < v:
                        m[k] = v
                b.r = list(m.items())
        for b in writes:
            b.w = tok
            b.r = []

    def op(self, eng, fn, reads=(), writes=()):
        reads = [x.buf if isinstance(x, View) else x for x in reads]
        writes = [x.buf if isinstance(x, View) else x for x in writes]
        waits = self._deps(eng, reads, writes)
        ek = eng if self.ep == 0 else f"{eng}.{self.ep}"
        v = self.cnt.get(ek, 0) + 1
        self.cnt[ek] = v
        self.ops[eng].append((fn, waits, (ek, 1)))
        self._mark((ek, v), reads, writes)

    def dma(self, q, out, in_, **kw):
        reads = [in_.buf]
        writes = [out.buf]
        i = self.ndma[q] % self.NDMASEM
        self.ndma[q] += 1
        sk = f"dma_{q}_{i}"
        prev = self.cnt.get(sk, 0)
        waits = self._deps(q, reads, writes)
        if prev > 0 and self.known[q].get(sk, -1) < prev:
            self.known[q][sk] = prev
            waits.append((sk, prev))
        v = prev + 16
        self.cnt[sk] = v
        o, s = out.ap, in_.ap

        def fn(e, o=o, s=s, kw=kw):
            return e.dma_start(out=o, in_=s, **kw)

        self.ops[q].append((fn, waits, (sk, 16)))
        self._mark((sk, v), reads, writes)

    def allgather(self, out, in_, groups):
        reads = [in_.buf if isinstance(in_, View) else in_]
        writes = [out.buf if isinstance(out, View) else out]
        self.ncc = getattr(self, "ncc", 0) + 1
        sk = f"cc_{self.ncc}"
        waits = self._deps("pool", reads, writes)
        self.cnt[sk] = 1
        o, s = _v(out), _v(in_)

        def fn(e, o=o, s=s):
            return e.collective_compute("AllGather", ALU.bypass, replica_groups=groups, ins=[s], outs=[o])

        self.ops["pool"].append((fn, waits, (sk, 1)))
        self._mark((sk, 1), reads, writes)

    def new_epoch(self):
        self.barrier()
        self.ep += 1

    def barrier(self):
        snap = dict(self.cnt)
        for e in ENGS:
            waits = []
            for k, v in snap.items():
                if e == "pe" and k.split(".")[0] == "pe":
                    continue
                if self.known[e].get(k, -1) < v:
                    self.known[e][k] = v
                    waits.append((k, v))
            if waits:
                self.ops[e].append((None, waits, None))

    def build(self):
        self.barrier()
        nc = self.nc
        semkeys = sorted(self.cnt.keys())
        from contextlib import ExitStack

        with ExitStack() as st:
            sems = {k: st.enter_context(nc.semaphore(k)) for k in semkeys}
            block = st.enter_context(nc.Block())

            def emit(engobj, lst):
                for fn, waits, inc in lst:
                    for k, v in waits:
                        engobj.wait_ge(sems[k], v)
                    if fn is None:
                        continue
                    ins = fn(engobj)
                    if inc is not None:
                        ins.then_inc(sems[inc[0]], inc[1])

            @block.tensor
            def _(e):
                emit(e, self.ops["pe"])

            @block.scalar
            def _(e):
                emit(e, self.ops["act"])

            @block.vector
            def _(e):
                emit(e, self.ops["dve"])

            @block.gpsimd
            def _(e):
                emit(e, self.ops["pool"])

            @block.sync
            def _(e):
                emit(e, self.ops["sp"])

        return nc

    def mm(self, out, lhsT, rhs, start=True, stop=True):
        o, l, r = _v(out), _v(lhsT), _v(rhs)
        self.op("pe", lambda e: e.matmul(o, l, r, start=start, stop=stop), [lhsT, rhs], [out])

    def act(self, out, in_, func, bias=None, scale=1.0):
        o, i = _v(out), _v(in_)
        b = _v(bias) if isinstance(bias, View) else bias
        s = _v(scale) if isinstance(scale, View) else scale
        rd = [in_] + [x for x in (bias, scale) if isinstance(x, View)]
        kw = {}
        if b is not None:
            kw["bias"] = b
        self.op("act", lambda e: e.activation(o, i, func, scale=s, **kw), rd, [out])

    def tt(self, out, a, b, op, eng="dve"):
        o, x, y = _v(out), _v(a), _v(b)
        self.op(eng, lambda e: e.tensor_tensor(o, x, y, op), [a, b], [out])

    def ts(self, out, a, s1, s2, op0, op1=None, eng="dve"):
        o, x = _v(out), _v(a)
        s1v = _v(s1) if isinstance(s1, View) else s1
        s2v = _v(s2) if isinstance(s2, View) else s2
        rd = [a] + [x_ for x_ in (s1, s2) if isinstance(x_, View)]
        if op1 is None:
            self.op(eng, lambda e: e.tensor_scalar(o, x, s1v, None, op0), rd, [out])
        else:
            self.op(eng, lambda e: e.tensor_scalar(o, x, s1v, s2v, op0, op1), rd, [out])

    def stt(self, out, a, s, b, op0, op1):
        o, x, y = _v(out), _v(a), _v(b)
        sv = _v(s) if isinstance(s, View) else s
        rd = [a, b] + ([s] if isinstance(s, View) else [])
        self.op("dve", lambda e: e.scalar_tensor_tensor(o, x, sv, y, op0, op1), rd, [out])

    def copy(self, out, in_, eng="dve"):
        o, i = _v(out), _v(in_)
        if eng == "act":
            self.op("act", lambda e: e.copy(o, i), [in_], [out])
        else:
            self.op(eng, lambda e: e.tensor_copy(o, i), [in_], [out])

    def memset(self, out, val, eng="dve"):
        o = _v(out)
        self.op(eng, lambda e: e.memset(o, val), [], [out])

    def scan(self, out, d0, d1, init):
        o, a, b = _v(out), _v(d0), _v(d1)
        iv = _v(init) if isinstance(init, View) else init
        rd = [d0, d1] + ([init] if isinstance(init, View) else [])
        self.op("dve", lambda e: e.tensor_tensor_scan(o, a, b, iv, ALU.mult, ALU.add), rd, [out])

    def recip(self, out, in_):
        o, i = _v(out), _v(in_)
        self.op("dve", lambda e: e.reciprocal(o, i), [in_], [out])


class IO:
    def __init__(self, p, sfx, bound):
        self.p, self.sfx, self.bound, self.made = p, sfx, bound, {}

    def inp(self, n, s, d=F32):
        if n in self.bound:
            return self.bound[n]
        return self.p.dram(n + self.sfx, s, d, kind="ExternalInput")

    def out(self, n, s, d=F32):
        if n in self.bound:
            return self.bound[n]
        b = self.p.dram(n + self.sfx, s, d)
        self.made[n] = b
        return b

    def tmp(self, n, s, d=F32):
        return self.p.dram(n + self.sfx, s, d)


def rstd_from_ssq(p, rstd_out, ssq_ps, n):
    p.ts(rstd_out, ssq_ps, 1.0 / n, EPS, ALU.mult, ALU.add)
    p.act(rstd_out, rstd_out, AF.Sqrt)
    p.recip(rstd_out, rstd_out)


def seg_split(c0, n):
    out = []
    if c0 < TC:
        e = min(TC, c0 + n)
        out.append((c0, e - c0, 1))
        if c0 + n > TC:
            out.append((TC, c0 + n - TC, 0))
    else:
        out.append((c0, n, 0))
    return out


def emit_A(p, io, banks, ones, wc=True):
    IN = io.inp
    OUT = io.out
    LB = BLKS if wc else LBLKS
    xT = IN("xT", [D, T])
    w_fm = IN("w_fm", [D, NFM])
    w_nav = IN("w_nav", [D, 1024])
    gqT = IN("gqT", [128, 4])
    gkvT = IN("gkvT", [128, 2])
    w_qn = IN("w_qn", [512, 1024])
    w_qr = IN("w_qr", [512, 512])
    ropeC = IN("ropeC", [128, TL])
    ropeS = IN("ropeS", [128, TL])
    perm = IN("perm", [128, 128])

    uT_o = OUT("uT", [1024, T])
    rgT_o = OUT("rgT", [1024, T])
    naqT_o = OUT("naqT", [1024, T], BF16)
    nakT_o = OUT("nakT", [1024, T], BF16)
    nav_o = OUT("nav", [T, 1024], BF16)
    sgT_o = OUT("sgT", [6144, T])
    QnT_o = OUT("QnT", [1024, T], BF16)
    QrT_o = OUT("QrT", [512, T], BF16)
    kvnT_o = OUT("kvnT", [256, T], BF16)
    krT_o = OUT("krT", [64, T], BF16)
    p.sb_push()

    mods = p.sb("mods", [128, 6, KC, 2], F32)
    p.dma("sp", mods[:], IN("mods", [128, 6, KC, 2])[:])

    hT = p.sb("hT", [128, KC, T], BF16)
    qa = p.sb("qa", [128, 4, T], F32)
    kva = p.sb("kva", [128, 2, T], F32)
    krp = p.sb("krp", [64, T], F32)
    p.sb_push()
    xs = p.sb("xs", [128, KC, 512], F32)
    sq = p.sb("sq", [128, KC, 512], BF16)
    rstd = p.sb("rstd", [128, 512], F32)
    tmp = [p.sb(f"tmp{i}", [128, 512], F32) for i in range(2)]
    xT_v = xT[:].rearrange("(k p) n -> p k n", p=128)
    for (c0, n) in BLKS:
        p.dma("sp", xs[:, :, 0:n], xT_v[:, :, c0:c0 + n])
        p.act(sq[:, :, 0:n], xs[:, :, 0:n], AF.Square)
        ssq = banks[1]
        for k in range(KC):
            p.mm(ssq[:, 0:n], ones[:], sq[:, k, 0:n], start=(k == 0), stop=(k == KC - 1))
        rstd_from_ssq(p, rstd[:, 0:n], ssq[:, 0:n], D)
        for k in range(KC):
            tm = tmp[k % 2]
            for (s0, sn, cnd) in seg_split(c0, n):
                l0 = s0 - c0
                p.stt(tm[:, l0:l0 + sn], xs[:, k, l0:l0 + sn], mods[:, 0, k, cnd:cnd + 1], rstd[:, l0:l0 + sn],
                      ALU.mult, ALU.mult)
                p.act(hT[:, k, s0:s0 + sn], tm[:, l0:l0 + sn], AF.Identity, bias=mods[:, 1, k, cnd:cnd + 1])
    p.sb_pop()

    p.sb_push()
    wt2 = [p.sb(f"win{i}", [128, KC, 512], BF16) for i in range(2)]
    st32 = [p.sb(f"st32_{i}", [128, T], F32) for i in range(3)]
    st16 = [p.sb(f"st16_{i}", [128, T], BF16) for i in range(2)]
    w_fm_v = w_fm[:].rearrange("(k p) n -> p k n", p=128)
    ntile = (NFM + 511) // 512
    ev = 0
    bi = 0
    for tI in range(ntile):
        w0 = tI * 512
        wn = min(512, NFM - w0)
        wt = wt2[tI % 2]
        p.dma("pool", wt[:, :, 0:wn], w_fm_v[:, :, w0:w0 + wn])
        for cc in range(0, wn, 128):
            m = min(128, wn - cc)
            chunk = (w0 + cc) // 128
            if chunk < 4:
                kind, dst = "sb", qa[:, chunk, :]
            elif chunk < 6:
                kind, dst = "sb", kva[:, chunk - 4, :]
            elif chunk < 14:
                kind, dst, drow = "f32", uT_o, (chunk - 6) * 128
            elif chunk < 22:
                kind, dst, drow = "f32", rgT_o, (chunk - 14) * 128
            elif chunk < 30:
                kind, dst, drow = "bf", naqT_o, (chunk - 22) * 128
            elif chunk < 38:
                kind, dst, drow = "bf", nakT_o, (chunk - 30) * 128
            elif chunk < 86:
                kind, dst, drow = "sig", sgT_o, (chunk - 38) * 128
            else:
                kind, dst = "sb", krp[:, :]
            if kind in ("f32", "sig"):
                stg = st32[ev % 3]
            elif kind == "bf":
                stg = st16[ev % 2]
            ev += 1
            ctx_needed = (4 <= chunk < 14) or (30 <= chunk < 38) or chunk >= 86
            for (c0, n) in (BLKS if ctx_needed else LB):
                ps = banks[2 + bi % 6]
                bi += 1
                for k in range(KC):
                    p.mm(ps[0:m, 0:n], wt[:, k, cc:cc + m], hT[:, k, c0:c0 + n], start=(k == 0), stop=(k == KC - 1))
                if kind == "sb":
                    p.copy(dst[0:m, c0:c0 + n], ps[0:m, 0:n], eng="act" if bi % 2 else "dve")
                elif kind == "sig":
                    p.act(stg[:, c0:c0 + n], ps[:, 0:n], AF.Sigmoid)
                else:
                    p.copy(stg[:, c0:c0 + n], ps[:, 0:n], eng="act" if bi % 2 else "dve")
            if kind != "sb":
                p.dma("sp", dst[drow:drow + 128, :], stg[:])
    w_nav_v = w_nav[:].rearrange("(k p) n -> p k n", p=128)
    for half in range(2):
        wt = wt2[half % 2]
        p.dma("pool", wt[:], w_nav_v[:, :, half * 512:(half + 1) * 512])
        for tcn in range(T // 128):
            ps = banks[2 + bi % 6]
            bi += 1
            for k in range(KC):
                p.mm(ps[:, :], hT[:, k, tcn * 128:(tcn + 1) * 128], wt[:, k, :], start=(k == 0), stop=(k == KC - 1))
            stg = st16[tcn % 2]
            p.copy(stg[:, 0:512], ps[:, :], eng="act" if bi % 2 else "dve")
            p.dma("sp", nav_o[tcn * 128:(tcn + 1) * 128, half * 512:(half + 1) * 512], stg[:, 0:512])
    p.sb_pop()

    p.sb_push()
    qn = p.sb("qn", [128, 4, T], BF16)
    kvn = p.sb("kvn", [128, 2, T], BF16)
    sqq = p.sb("sqq", [128, 4, 512], BF16)
    rs = p.sb("rs", [128, 512], F32)
    tq = p.sb("tq", [128, 512], F32)
    gq = p.sb("gq", [128, 4], F32)
    gkv = p.sb("gkv", [128, 2], F32)
    p.dma("sp", gq[:], gqT[:])
    p.dma("sp", gkv[:], gkvT[:])
    for (src, nch, gsb, dstb, dim) in ((qa, 4, gq, qn, 512), (kva, 2, gkv, kvn, 256)):
        for (c0, n) in (LB if src is qa else BLKS):
            p.act(sqq[:, 0:nch, 0:n], src[:, :, c0:c0 + n], AF.Square)
            ssq = banks[1]
            for k in range(nch):
                p.mm(ssq[:, 0:n], ones[:], sqq[:, k, 0:n], start=(k == 0), stop=(k == nch - 1))
            rstd_from_ssq(p, rs[:, 0:n], ssq[:, 0:n], dim)
            for k in range(nch):
                p.stt(dstb[:, k, c0:c0 + n], src[:, k, c0:c0 + n], gsb[:, k:k + 1], rs[:, 0:n], ALU.mult, ALU.mult)
    for k in range(2):
        p.dma("sp", kvnT_o[k * 128:(k + 1) * 128, :], kvn[:, k, :])
    wqn = p.sb("wqn", [128, 4, 1024], BF16)
    wqr = p.sb("wqr", [128, 4, 512], BF16)
    p.dma("pool", wqn[:], w_qn[:].rearrange("(k p) n -> p k n", p=128))
    p.dma("pool", wqr[:], w_qr[:].rearrange("(k p) n -> p k n", p=128))
    cT = p.sb("cT", [128, TL], F32)
    sT = p.sb("sT", [128, TL], F32)
    pm = p.sb("pm", [128, 128], F32)
    p.dma("sp", cT[:], ropeC[:])
    p.dma("sp", sT[:], ropeS[:])
    p.dma("sp", pm[:], perm[:])
    stq = [p.sb(f"stq{i}", [128, T], BF16) for i in range(2)]
    for h in range(8):
        stg = stq[h % 2]
        for (c0, n) in LB:
            ps = banks[2 + bi % 6]
            bi += 1
            for k in range(4):
                p.mm(ps[:, 0:n], wqn[:, k, h * 128:(h + 1) * 128], qn[:, k, c0:c0 + n], start=(k == 0), stop=(k == 3))
            p.copy(stg[:, c0:c0 + n], ps[:, 0:n], eng="act" if bi % 2 else "dve")
        p.dma("sp", QnT_o[h * 128:(h + 1) * 128, :], stg[:])
    qrf = p.sb("qrf", [128, T], F32)
    t1 = p.sb("t1", [128, 512], F32)
    t2 = p.sb("t2", [128, 512], F32)

    def rope_chunk(src, npart, dst_dram_rows, ctx_cols=True):
        stg = stq[0]
        if ctx_cols:
            p.copy(stg[0:npart, 0:TC], src[0:npart, 0:TC])
        for (c0, n) in ((TC, 512), (TC + 512, 512)):
            ps = banks[2 + (c0 // 512) % 2]
            p.mm(ps[0:npart, 0:n], pm[0:npart, 0:npart], src[0:npart, c0:c0 + n])
            l0 = c0 - TC
            p.tt(t1[0:npart, 0:n], src[0:npart, c0:c0 + n], cT[0:npart, l0:l0 + n], ALU.mult)
            p.tt(t2[0:npart, 0:n], ps[0:npart, 0:n], sT[0:npart, l0:l0 + n], ALU.mult)
            p.tt(stg[0:npart, c0:c0 + n], t1[0:npart, 0:n], t2[0:npart, 0:n], ALU.add)
        p.dma("sp", dst_dram_rows, stg[0:npart, :])

    for hc in range(4):
        for (c0, n) in LB:
            ps = banks[4 + bi % 4]
            bi += 1
            for k in range(4):
                p.mm(ps[:, 0:n], wqr[:, k, hc * 128:(hc + 1) * 128], qn[:, k, c0:c0 + n], start=(k == 0), stop=(k == 3))
            p.copy(qrf[:, c0:c0 + n], ps[:, 0:n], eng="act")
        rope_chunk(qrf, 128, QrT_o[hc * 128:(hc + 1) * 128, :], ctx_cols=wc)
    rope_chunk(krp, 64, krT_o[:, :])
    p.sb_pop()
    p.sb_pop()


def fm(v):
    return np.ascontiguousarray(v.reshape(-1, 128).T)


def rope_tables():
    t = np.arange(SEQ, dtype=np.int32)
    row = (t // GRID_W).astype(np.float32)
    col = (t % GRID_W).astype(np.float32)
    inv = (np.float32(10000.0) ** (-np.arange(16, dtype=np.float32) / np.float32(16))).astype(np.float32)
    ang = np.concatenate([row[:, None] * inv, col[:, None] * inv], axis=-1).astype(np.float32)
    cos, sin = np.cos(ang).astype(np.float32), np.sin(ang).astype(np.float32)
    C = np.zeros((64, SEQ), np.float32)
    S = np.zeros((64, SEQ), np.float32)
    for d in range(64):
        a, b, f = d // 32, (d % 32) // 16, d % 16
        C[d] = cos[:, a * 16 + f]
        S[d] = sin[:, a * 16 + f] * (-1.0 if b == 0 else 1.0)
    pm = np.zeros((128, 128), np.float32)
    for m in range(128):
        d = m % 64
        partner = d + 16 if (d % 32) < 16 else d - 16
        pm[(m // 64) * 64 + partner, m] = 1.0
    return np.concatenate([C, C], 0), np.concatenate([S, S], 0), pm


def initial_xT(inp):
    out = []
    for core in range(8):
        b, j = core // 4, core % 4
        xx = np.concatenate([inp["ctx"][b], inp["x"][b, j * TL:(j + 1) * TL]], 0)
        out.append(np.ascontiguousarray(xx.T))
    return out


def emit_B(p, io, banks, ones, wc=True):
    IN = io.inp
    LB = BLKS if wc else LBLKS
    c_lo = 0 if wc else TC
    SEGS = ((0, TC, 1), (TC, TL, 0)) if wc else ((TC, TL, 0),)
    xT = IN("xT", [D, T])
    mods_i = IN("mods", [128, 6, KC, 2])
    u_parts = io.bound.get("u_parts")
    u_all = None if u_parts is not None else IN("u_all", [1024, NKEY])
    rgT = IN("rgT", [1024, T])
    convT = IN("convT", [128, 8, 5])
    rg_w = IN("rg_w", [4, 8, 128, 128])
    rg_b = IN("rg_b", [128, 4, 8])
    lamT = IN("lamT", [128, 2, 8])
    onehot = IN("onehot", [128, 4])
    kvn_all = IN("kvn_all", [256, NKEY], BF16)
    kr_all = IN("kr_all", [64, NKEY], BF16)
    QnT = IN("QnT", [1024, T], BF16)
    QrT = IN("QrT", [512, T], BF16)
    w_kn = IN("w_kn", [256, 1024])
    w_v = IN("w_v", [256, 1024])
    naqT = IN("naqT", [1024, T], BF16)
    nak_h = IN("nak_h", [1024, 1536], BF16)
    nav_h = IN("nav_h", [1536, 1024], BF16)
    nak_c = IN("nak_c", [1024, TC], BF16)
    nav_c = IN("nav_c", [TC, 1024], BF16)
    BT = IN("BT", [8, 128, NA_TOT, 64])
    sgT = IN("sgT", [6144, T])
    w_bo = [IN(n, [1024, D]) for n in ("w_mla_o", "w_rg_o", "w_na_o")]
    w_out = IN("w_out", [D, D])
    w_g = IN("w_g", [D, FFN])
    w_u = IN("w_u", [D, FFN])
    w_d = IN("w_d", [FFN, D])
    xo = io.out("xo", [D, T], F32)
    broT = [io.tmp(n, [1024, T], BF16) for n in ("mlaoT", "rgoT", "naoT")]
    y2T = io.tmp("y2T", [D, T], F32)
    x1T = io.tmp("x1T", [D, T], F32)
    y3T = io.tmp("y3T", [D, T], F32)

    p.sb_push()
    mods = p.sb("mods", [128, 6, KC, 2], F32)
    p.dma("sp", mods[:], mods_i[:])

    p.sb_push()
    cw = p.sb("cw", [128, 8, 5], F32)
    rb = p.sb("rb", [128, 4, 8], F32)
    negc = p.sb("negc", [128, 2, 8], F32)
    oh = p.sb("oh", [128, 4], F32)
    rw = p.sb("rw", [128, 32, 128], BF16)
    p.dma("sp", cw[:], convT[:])
    p.dma("sp", rb[:], rg_b[:])
    p.dma("sp", negc[:], lamT[:])
    p.dma("sp", oh[:], onehot[:])
    p.dma("pool", rw[:], rg_w[:].rearrange("a c i j -> i (a c) j"))
    p.act(negc[:], negc[:], AF.Exp, scale=-1.0)
    p.act(negc[:], negc[:], AF.Ln, bias=1.0)
    p.ts(negc[:], negc[:], -8.0, None, ALU.mult)
    U = p.sb("U", [128, NKEY], F32)
    Y = p.sb("Y", [128, NKEY], F32)
    YB = p.sb("YB", [128, NKEY], BF16)
    Rd = [p.sb(f"R{d}", [128, NKEY], F32) for d in range(2)]
    Id = [p.sb(f"I{d}", [128, NKEY], F32) for d in range(2)]
    Sd = [p.sb(f"S{d}", [128, NKEY], F32) for d in range(2)]
    gt = p.sb("gt", [128, T], F32)
    g2 = p.sb("g2", [128, T], F32)
    rec = p.sb("rec", [128, T], F32)
    rgo_st = p.sb("rgo_st", [128, T], BF16)
    KB = [(i * 512, 512) for i in range(8)] + [(4096, 256)]
    bi = 0
    for c in range(8):
        if u_parts is None:
            p.dma("sp", U[:], u_all[c * 128:(c + 1) * 128, :])
        else:
            uT_own, gu_l = u_parts
            p.dma("sp", U[:, 0:TC], uT_own[c * 128:(c + 1) * 128, 0:TC])
            for r in range(4):
                r0 = r * 256 + (c % 2) * 128
                p.dma("sp" if r % 2 == 0 else "act", U[:, TC + r * TL:TC + (r + 1) * TL], gu_l[c // 2][r0:r0 + 128, :])
        p.dma("act", gt[:], rgT[c * 128:(c + 1) * 128, :])
        for (s, e) in ((0, TC), (TC, NKEY)):
            p.ts(Y[:, s:e], U[:, s:e], cw[:, c, 2:3], cw[:, c, 4:5], ALU.mult, ALU.add)
            p.stt(Y[:, s + 2:e], U[:, s:e - 2], cw[:, c, 0:1], Y[:, s + 2:e], ALU.mult, ALU.add)
            p.stt(Y[:, s + 1:e], U[:, s:e - 1], cw[:, c, 1:2], Y[:, s + 1:e], ALU.mult, ALU.add)
            p.stt(Y[:, s:e - 1], U[:, s + 1:e], cw[:, c, 3:4], Y[:, s:e - 1], ALU.mult, ALU.add)
        p.copy(YB[:], Y[:], eng="pool")
        HF = U
        for d in range(2):
            Rb, Ib, Sb = Rd[d], Id[d], Sd[d]
            for (c0, n) in KB:
                psr = banks[bi % 8]
                psi = banks[(bi + 1) % 8]
                bi += 2
                p.mm(psr[:, 0:n], rw[:, d * 8 + c, :], YB[:, c0:c0 + n])
                p.mm(psi[:, 0:n], rw[:, 16 + d * 8 + c, :], YB[:, c0:c0 + n])
                p.act(Rb[:, c0:c0 + n], psr[:, 0:n], AF.Sigmoid, bias=rb[:, d, c:c + 1])
                p.act(Ib[:, c0:c0 + n], psi[:, 0:n], AF.Sigmoid, bias=rb[:, 2 + d, c:c + 1])
            p.act(Rb[:], Rb[:], AF.Exp, scale=negc[:, d, c:c + 1])
            p.tt(Ib[:], Ib[:], Y[:], ALU.mult)
            if d == 1:
                pass
            p.act(Sb[:], Rb[:], AF.Square)
            p.act(Sb[:], Sb[:], AF.Sqrt, bias=1.0, scale=-1.0)
            p.tt(Ib[:], Ib[:], Sb[:], ALU.mult)
            if d == 0:
                p.scan(HF[:, :], Rb[:, :], Ib[:, :], 0.0)
            else:
                HR = Sb
                p.scan(HR[:, 0:TC][:, ::-1], Rb[:, 0:TC][:, ::-1], Ib[:, 0:TC][:, ::-1], 0.0)
                p.scan(HR[:, TC:NKEY][:, ::-1], Rb[:, TC:NKEY][:, ::-1], Ib[:, TC:NKEY][:, ::-1], HR[:, 0:1])
        if wc:
            p.tt(rec[:, 0:TC], HF[:, 0:TC], HR[:, 0:TC], ALU.add)
        p.ts(rec[:, TC:T], HF[:, TC:TC + TL], oh[:, 0:1], None, ALU.mult)
        p.stt(rec[:, TC:T], HR[:, TC:TC + TL], oh[:, 0:1], rec[:, TC:T], ALU.mult, ALU.add)
        for r in range(1, 4):
            o0 = TC + r * TL
            p.stt(rec[:, TC:T], HF[:, o0:o0 + TL], oh[:, r:r + 1], rec[:, TC:T], ALU.mult, ALU.add)
            p.stt(rec[:, TC:T], HR[:, o0:o0 + TL], oh[:, r:r + 1], rec[:, TC:T], ALU.mult, ALU.add)
        p.act(g2[:, c_lo:T], gt[:, c_lo:T], AF.Square)
        p.ts(g2[:, c_lo:T], g2[:, c_lo:T], 0.044715, 1.0, ALU.mult, ALU.add)
        p.tt(g2[:, c_lo:T], g2[:, c_lo:T], gt[:, c_lo:T], ALU.mult)
        p.act(g2[:, c_lo:T], g2[:, c_lo:T], AF.Sigmoid, scale=2.0 * math.sqrt(2.0 / math.pi))
        p.tt(g2[:, c_lo:T], g2[:, c_lo:T], gt[:, c_lo:T], ALU.mult)
        p.tt(rgo_st[:, c_lo:T], g2[:, c_lo:T], rec[:, c_lo:T], ALU.mult)
        p.dma("sp", broT[1][c * 128:(c + 1) * 128, :], rgo_st[:])
    p.sb_pop()

    p.sb_push()
    kvn = p.sb("kvn", [128, 2, NKEY], BF16)
    kr = p.sb("kr", [64, NKEY], BF16)
    wkn = p.sb("wkn", [128, 2, 1024], BF16)
    wv = p.sb("wv", [128, 2, 1024], BF16)
    p.dma("sp", kvn[:], kvn_all[:].rearrange("(k p) n -> p k n", p=128))
    p.dma("sp", kr[:], kr_all[:])
    p.dma("pool", wkn[:], w_kn[:].rearrange("(k p) n -> p k n", p=128))
    p.dma("pool", wv[:], w_v[:].rearrange("(k p) n -> p k n", p=128))
    knT = [p.sb(f"knT{i}", [128, NKEY], BF16) for i in range(2)]
    Vh = [p.sb(f"Vh{i}", [128, 34, 128], BF16) for i in range(2)]
    Qn = [p.sb(f"Qn{i}", [128, T], BF16) for i in range(2)]
    Qr = [p.sb(f"Qr{i}", [64, T], BF16) for i in range(2)]
    Es = [p.sb(f"E{i}", [128, 512], BF16) for i in range(5)]
    SBK = [banks[0], banks[1], banks[6], banks[7]]
    rz = p.sb("rz", [128, 512], F32)
    accs = [p.sb(f"acc{i}", [128, 512], F32) for i in range(2)]
    ones32 = p.sb("ones32", [128, 128], F32)
    p.memset(ones32[:], 1.0)
    mst = [p.sb(f"mst{i}", [128, T], BF16) for i in range(2)]
    ei = 0
    for h in range(8):
        kT, V, qn_, qr_, stg = knT[h % 2], Vh[h % 2], Qn[h % 2], Qr[h % 2], mst[h % 2]
        p.dma("sp", qn_[:], QnT[h * 128:(h + 1) * 128, :])
        p.dma("sp", qr_[:], QrT[h * 64:(h + 1) * 64, :])
        for bidx, (c0, n) in enumerate(KB):
            ps = banks[6 + bidx % 2]
            for k in range(2):
                p.mm(ps[:, 0:n], wkn[:, k, h * 128:(h + 1) * 128], kvn[:, k, c0:c0 + n], start=(k == 0), stop=(k == 1))
            p.copy(kT[:, c0:c0 + n], ps[:, 0:n], eng="dve" if bidx % 2 else "act")
        for g in range(9):
            ncc = 4 if g < 8 else 2
            ps = banks[6 + g % 2]
            for cc in range(ncc):
                kc = g * 4 + cc
                for k in range(2):
                    p.mm(ps[:, cc * 128:(cc + 1) * 128], kvn[:, k, kc * 128:(kc + 1) * 128],
                         wv[:, k, h * 128:(h + 1) * 128], start=(k == 0), stop=(k == 1))
            p.copy(V[:, g * 4:g * 4 + ncc, :].rearrange("p c d -> p (c d)"), ps[:, 0:ncc * 128],
                   eng="act" if g % 2 else "dve")
        qblocks = [(TC, 512, 34), (TC + 512, 512, 34)] + ([(0, TC, 2)] if wc else [])
        for qb, (q0, qn_cols, nkc) in enumerate(qblocks):
            O = banks[2 + 2 * (qb % 2)]
            Z = banks[3 + 2 * (qb % 2)]

            def s_mm(kc):
                S = SBK[kc % 4]
                p.mm(S[:, 0:qn_cols], kT[:, kc * 128:(kc + 1) * 128], qn_[:, q0:q0 + qn_cols], start=True, stop=False)
                p.mm(S[:, 0:qn_cols], kr[0:64, kc * 128:(kc + 1) * 128], qr_[0:64, q0:q0 + qn_cols], start=False, stop=True)

            for kk in range(min(3, nkc)):
                s_mm(kk)
            for kc in range(nkc):
                if kc + 3 < nkc:
                    s_mm(kc + 3)
                S = SBK[kc % 4]
                E = Es[ei % 5]
                ei += 1
                p.act(E[:, 0:qn_cols], S[:, 0:qn_cols], AF.Exp, scale=MLA_SCALE)
                p.mm(O[:, 0:qn_cols], V[:, kc, :], E[:, 0:qn_cols], start=(kc == 0), stop=(kc == nkc - 1))
                acc = accs[qb % 2]
                if kc == 0:
                    p.copy(acc[:, 0:qn_cols], E[:, 0:qn_cols])
                else:
                    p.tt(acc[:, 0:qn_cols], acc[:, 0:qn_cols], E[:, 0:qn_cols], ALU.add)
            p.mm(Z[:, 0:qn_cols], ones32[:], accs[qb % 2][:, 0:qn_cols])
            p.recip(rz[:, 0:qn_cols], Z[:, 0:qn_cols])
            p.tt(stg[:, q0:q0 + qn_cols], O[:, 0:qn_cols], rz[:, 0:qn_cols], ALU.mult)
        p.dma("sp", broT[0][h * 128:(h + 1) * 128, :], stg[:])
    p.sb_pop()

    p.sb_push()
    qT = [p.sb(f"qT{i}", [128, T], BF16) for i in range(2)]
    kTh = [p.sb(f"kTh{i}", [128, 1536], BF16) for i in range(2)]
    kTc = [p.sb(f"kTc{i}", [128, TC], BF16) for i in range(2)]
    Va = [p.sb(f"Va{i}", [128, 12, 128], BF16) for i in range(2)]
    Vb = [p.sb(f"Vb{i}", [128, 11, 128], BF16) for i in range(2)]
    Vc = [p.sb(f"Vc{i}", [128, 2, 128], BF16) for i in range(2)]
    bias = [p.sb(f"bias{i}", [128, NA_TOT, 64], F32) for i in range(2)]
    tS = [p.sb(f"tS{i}", [128, 512], F32) for i in range(3)]
    En = [p.sb(f"En{i}", [128, 512], BF16) for i in range(3)]
    NSB = [banks[0], banks[1], banks[6]]
    rzn = p.sb("rzn", [128, 512], F32)
    nst = [p.sb(f"nst{i}", [128, T], BF16) for i in range(2)]
    for h in range(8):
        x = h % 2
        hs = slice(h * 128, (h + 1) * 128)
        p.dma("sp", qT[x][:], naqT[hs, :])
        p.dma("sp", kTh[x][:], nak_h[hs, :])
        p.dma("sp", kTc[x][:], nak_c[hs, :])
        p.dma("act", Va[x][:], nav_h[:, hs].rearrange("(c p) d -> p c d", p=128))
        p.dma("act", Vb[x][:], nav_h[64:64 + 1408, hs].rearrange("(c p) d -> p c d", p=128))
        p.dma("act", Vc[x][:], nav_c[:, hs].rearrange("(c p) d -> p c d", p=128))
        p.dma("sp", bias[x][:], BT[h])
        stg = nst[x]

        def s_row(i):
            lo, n = NA_WIN[i]
            nch = n // 2
            S = NSB[i % 3]
            q0 = TC + i * 64
            for c in range(nch):
                k0 = lo * 64 + c * 128
                p.mm(S[:, c * 64:(c + 1) * 64], kTh[x][:, k0:k0 + 128], qT[x][:, q0:q0 + 64])
            for c in range(2):
                p.mm(S[:, (nch + c) * 64:(nch + c + 1) * 64], kTc[x][:, c * 128:(c + 1) * 128], qT[x][:, q0:q0 + 64])

        s_row(0)
        s_row(1)
        for i in range(16):
            if i + 2 < 16:
                s_row(i + 2)
            lo, n = NA_WIN[i]
            nch = n // 2
            w = (nch + 2) * 64
            S = NSB[i % 3]
            grp = i // 8
            O = banks[2 + 2 * grp]
            Z = banks[3 + 2 * grp]
            oc = (i % 8) * 64
            ch0 = NA_CH0[i]
            p.stt(tS[i % 3][:, 0:w], S[:, 0:w], NA_SCALE,
                  bias[x][:, ch0:ch0 + nch + 2, :].rearrange("p c q -> p (c q)"), ALU.mult, ALU.add)
            p.act(En[i % 3][:, 0:w], tS[i % 3][:, 0:w], AF.Exp)
            for c in range(nch + 2):
                if c < nch:
                    vv = Va[x][:, lo // 2 + c, :] if lo % 2 == 0 else Vb[x][:, (lo - 1) // 2 + c, :]
                else:
                    vv = Vc[x][:, c - nch, :]
                p.mm(O[:, oc:oc + 64], vv, En[i % 3][:, c * 64:(c + 1) * 64], start=(c == 0), stop=(c == nch + 1))
            for c in range(nch + 2):
                p.mm(Z[:, oc:oc + 64], ones[:], En[i % 3][:, c * 64:(c + 1) * 64], start=(c == 0), stop=(c == nch + 1))
            if i % 8 == 7:
                p.recip(rzn[:], Z[:])
                p.tt(stg[:, TC + grp * 512:TC + (grp + 1) * 512], O[:], rzn[:], ALU.mult)
        if wc:
            S = banks[6]
            O = banks[7]
            Z = banks[0]
            for c in range(2):
                p.mm(S[:, c * 256:(c + 1) * 256], kTc[x][:, c * 128:(c + 1) * 128], qT[x][:, 0:TC])
            p.act(En[0][:], S[:], AF.Exp, scale=NA_SCALE)
            for c in range(2):
                p.mm(O[:, 0:TC], Vc[x][:, c, :], En[0][:, c * 256:(c + 1) * 256], start=(c == 0), stop=(c == 1))
            for c in range(2):
                p.mm(Z[:, 0:TC], ones[:], En[0][:, c * 256:(c + 1) * 256], start=(c == 0), stop=(c == 1))
            p.recip(rzn[:, 0:TC], Z[:, 0:TC])
            p.tt(stg[:, 0:TC], O[:, 0:TC], rzn[:, 0:TC], ALU.mult)
        p.dma("sp", broT[2][hs, :], stg[:])
    p.sb_pop()

    yT = p.sb("yT", [128, KC, T], BF16)
    off_after_yT = p.sb_off
    p.sb_push()
    bro = [p.sb(f"bro{i}", [128, 8, T], BF16) for i in range(3)]
    for i in range(3):
        p.dma("sp", bro[i][:], broT[i][:].rearrange("(k p) n -> p k n", p=128))
    wbo = [[p.sb(f"wbo{i}_{j}", [128, 8, 256], BF16) for j in range(2)] for i in range(3)]
    sgs = [p.sb(f"sgs{i}", [128, 3, T], F32) for i in range(2)]
    ta = p.sb("ta", [128, 512], F32)
    tb = p.sb("tb", [128, 512], F32)
    sg_v = sgT[:].rearrange("(b f p) n -> f p b n", b=3, p=128)
    for fg in range(8):
        for i in range(3):
            p.dma("pool", wbo[i][fg % 2][:], w_bo[i][:, fg * 256:(fg + 1) * 256].rearrange("(k p) n -> p k n", p=128))
        for fj in range(2):
            f = fg * 2 + fj
            sgt = sgs[f % 2]
            p.dma("act", sgt[:], sg_v[f])
            for (c0, n) in LB:
                pss = [banks[(f * 3 + i) % 6] for i in range(3)] if False else [banks[i] for i in range(3)]
                for i in range(3):
                    for k in range(8):
                        p.mm(pss[i][:, 0:n], wbo[i][fg % 2][:, k, fj * 128:(fj + 1) * 128], bro[i][:, k, c0:c0 + n],
                             start=(k == 0), stop=(k == 7))
                p.tt(ta[:, 0:n], pss[0][:, 0:n], sgt[:, 0, c0:c0 + n], ALU.mult)
                p.tt(tb[:, 0:n], pss[1][:, 0:n], sgt[:, 1, c0:c0 + n], ALU.mult)
                p.tt(ta[:, 0:n], ta[:, 0:n], tb[:, 0:n], ALU.add)
                p.tt(tb[:, 0:n], pss[2][:, 0:n], sgt[:, 2, c0:c0 + n], ALU.mult)
                p.tt(yT[:, f, c0:c0 + n], ta[:, 0:n], tb[:, 0:n], ALU.add)
    p.sb_pop()

    def chunk_ssq(src_f32, first):
        for (c0, n) in LB:
            p.act(sqb[:, 0:n], src_f32[:, c0:c0 + n], AF.Square)
            p.mm(banks[7][:, 0:n], ones[:], sqb[:, 0:n])
            if first:
                p.copy(ssq[:, c0:c0 + n], banks[7][:, 0:n])
            else:
                p.tt(ssq[:, c0:c0 + n], ssq[:, c0:c0 + n], banks[7][:, 0:n], ALU.add)

    def finish_rstd():
        p.ts(rstd[:, c_lo:T], ssq[:, c_lo:T], 1.0 / D, EPS, ALU.mult, ALU.add)
        p.act(rstd[:, c_lo:T], rstd[:, c_lo:T], AF.Sqrt)
        p.recip(rstd[:, c_lo:T], rstd[:, c_lo:T])

    p.sb_push()
    ssq = p.sb("ssq", [128, T], F32)
    rstd = p.sb("rstd", [128, T], F32)
    sqb = p.sb("sqb", [128, 512], BF16)
    wo = [p.sb(f"wo{i}", [128, KC, 512], BF16) for i in range(2)]
    st = [p.sb(f"st{i}", [128, T], F32) for i in range(2)]
    xst = [p.sb(f"xst{i}", [128, T], F32) for i in range(2)]
    bi = 0
    for fg in range(4):
        p.dma("pool", wo[fg % 2][:], w_out[:, fg * 512:(fg + 1) * 512].rearrange("(k p) n -> p k n", p=128))
        for fj in range(4):
            f = fg * 4 + fj
            stg = st[f % 2]
            for (c0, n) in LB:
                ps = banks[bi % 6]
                bi += 1
                for k in range(KC):
                    p.mm(ps[:, 0:n], wo[fg % 2][:, k, fj * 128:(fj + 1) * 128], yT[:, k, c0:c0 + n],
                         start=(k == 0), stop=(k == KC - 1))
                p.copy(stg[:, c0:c0 + n], ps[:, 0:n], eng="act" if bi % 2 else "dve")
            chunk_ssq(stg, f == 0)
            p.dma("sp", y2T[f * 128:(f + 1) * 128, :], stg[:])
    finish_rstd()

    def residual_pass(srcT, resT, dstT, gsel, want_ssq):
        for f in range(KC):
            a = st[f % 2]
            b = xst[f % 2]
            p.dma("sp", a[:], srcT[f * 128:(f + 1) * 128, :])
            p.dma("act", b[:], resT[f * 128:(f + 1) * 128, :])
            for (s0, sn, cnd) in SEGS:
                p.stt(a[:, s0:s0 + sn], a[:, s0:s0 + sn], mods[:, gsel, f, cnd:cnd + 1], rstd[:, s0:s0 + sn],
                      ALU.mult, ALU.mult)
            p.tt(a[:, c_lo:T], a[:, c_lo:T], b[:, c_lo:T], ALU.add)
            if want_ssq:
                chunk_ssq(a, f == 0)
            p.dma("sp", dstT[f * 128:(f + 1) * 128, :], a[:])

    residual_pass(y2T, xT, x1T, 2, True)
    finish_rstd()
    h2 = yT
    for f in range(KC):
        a = st[f % 2]
        p.dma("sp", a[:], x1T[f * 128:(f + 1) * 128, :])
        for (s0, sn, cnd) in SEGS:
            p.stt(a[:, s0:s0 + sn], a[:, s0:s0 + sn], mods[:, 3, f, cnd:cnd + 1], rstd[:, s0:s0 + sn], ALU.mult, ALU.mult)
            p.act(h2[:, f, s0:s0 + sn], a[:, s0:s0 + sn], AF.Identity, bias=mods[:, 4, f, cnd:cnd + 1])
    p.sb_pop()

    p.sb_push()
    actT = p.sb("actT", [128, HC, T], BF16)
    p.sb_push()
    wgt = [p.sb(f"wg{i}", [128, KC, 128], BF16) for i in range(2)]
    wut = [p.sb(f"wu{i}", [128, KC, 128], BF16) for i in range(2)]
    sl = [p.sb(f"sl{i}", [128, 512], F32) for i in range(2)]
    w_g_v = w_g[:].rearrange("(k p) n -> p k n", p=128)
    w_u_v = w_u[:].rearrange("(k p) n -> p k n", p=128)
    bi = 0
    for hcn in range(HC):
        wg_, wu_ = wgt[hcn % 2], wut[hcn % 2]
        p.dma("pool", wg_[:], w_g_v[:, :, hcn * 128:(hcn + 1) * 128])
        p.dma("pool", wu_[:], w_u_v[:, :, hcn * 128:(hcn + 1) * 128])
        for (c0, n) in LB:
            pg = banks[(2 * bi) % 8]
            pu = banks[(2 * bi + 1) % 8]
            bi += 1
            for k in range(KC):
                p.mm(pg[:, 0:n], wg_[:, k, :], h2[:, k, c0:c0 + n], start=(k == 0), stop=(k == KC - 1))
            for k in range(KC):
                p.mm(pu[:, 0:n], wu_[:, k, :], h2[:, k, c0:c0 + n], start=(k == 0), stop=(k == KC - 1))
            s_ = sl[bi % 2]
            p.act(s_[:, 0:n], pg[:, 0:n], AF.Silu)
            p.tt(actT[:, hcn, c0:c0 + n], s_[:, 0:n], pu[:, 0:n], ALU.mult)
    p.sb_pop()
    p.barrier()
    cur = p.sb_off
    p.sb_off = off_after_yT - KC * T * 2
    st3 = [p.sb(f"st3_{i}", [128, T], F32) for i in range(2)]
    x3 = [p.sb(f"x3_{i}", [128, T], F32) for i in range(2)]
    ssq = p.sb("ssq3", [128, T], F32)
    rstd = p.sb("rstd3", [128, T], F32)
    sqb = p.sb("sqb3", [128, 512], BF16)
    assert p.sb_off <= off_after_yT
    p.sb_off = cur
    wdt = [p.sb(f"wd{i}", [128, 22, 256], BF16) for i in range(2)]
    w_d_v = w_d[:].rearrange("(k p) n -> p k n", p=128)
    ti = 0
    for fg in range(8):
        for half in range(2):
            wd_ = wdt[ti % 2]
            ti += 1
            p.dma("pool", wd_[:], w_d_v[:, half * 22:(half + 1) * 22, fg * 256:(fg + 1) * 256])
            for fj in range(2):
                for bidx, (c0, n) in enumerate(LB):
                    ps = banks[fj * 3 + bidx]
                    for k in range(22):
                        p.mm(ps[:, 0:n], wd_[:, k, fj * 128:(fj + 1) * 128], actT[:, half * 22 + k, c0:c0 + n],
                             start=(half == 0 and k == 0), stop=(half == 1 and k == 21))
        for fj in range(2):
            f = fg * 2 + fj
            stg = st3[f % 2]
            for bidx, (c0, n) in enumerate(LB):
                p.copy(stg[:, c0:c0 + n], banks[fj * 3 + bidx][:, 0:n], eng="act" if bidx % 2 else "dve")
            chunk_ssq(stg, f == 0)
            p.dma("sp", y3T[f * 128:(f + 1) * 128, :], stg[:])
    finish_rstd()
    st, xst = st3, x3
    residual_pass(y3T, x1T, xo, 5, False)
    p.sb_pop()
    p.sb_pop()


def na_bias_table(rpb, j):
    out = np.zeros((8, 128, NA_TOT, 64), np.float32)
    qc = np.arange(64)
    kc = np.arange(64)
    c_start = np.clip(qc - 8, 0, 48)
    col_in = (kc[:, None] >= c_start[None, :]) & (kc[:, None] < c_start[None, :] + 16)
    dc = np.clip(kc[:, None] - qc[None, :] + 15, 0, 30)
    for i in range(16):
        r = 16 * j + i
        lo, n = NA_WIN[i]
        kr0 = min(max(r - 4, 0), 56)
        tab = np.full((8, n * 64, 64), -1e30, np.float32)
        for a in range(lo, lo + n):
            hr = 16 * j - 4 + a
            if kr0 <= hr < kr0 + 8:
                vals = rpb[:, hr - r + 7, :][:, dc]
                vals = np.where(col_in[None], vals, np.float32(-1e30))
                tab[:, (a - lo) * 64:(a - lo + 1) * 64, :] = vals
        nch = n // 2
        ch0 = NA_CH0[i]
        out[:, :, ch0:ch0 + nch, :] = tab.reshape(8, nch, 128, 64).transpose(0, 2, 1, 3)
    return out


GROUPS = [[0, 1, 2, 3], [4, 5, 6, 7]]
DEPTH_TOTAL = DEPTH
PK = 2368


def emit_mods(p, banks, nlayers):
    outs = []
    p.sb_push()
    cond = p.sb("cond", [128, KC, 2], F32)
    sil = p.sb("sil", [128, KC, 2], F32)
    condT = p.dram("condT", [128, KC, 2], F32, kind="ExternalInput")
    p.dma("sp", cond[:], condT[:])
    p.act(sil[:], cond[:], AF.Silu)
    wts = [p.sb(f"wada{i}", [128, KC, 512], F32) for i in range(2)]
    ti = 0
    gathered = []
    for l in range(nlayers):
        w_q = p.dram(f"w_ada_q_l{l}", [D, 3072], F32, kind="ExternalInput")
        b_q = p.dram(f"b_adaT_q_l{l}", [128, 24], F32, kind="ExternalInput")
        bq = p.sb(f"bq{l}", [128, 24], F32)
        part = p.sb(f"part{l}", [128, 24, 2], F32)
        p.dma("sp", bq[:], b_q[:])
        w_v = w_q[:].rearrange("(k p) n -> p k n", p=128)
        psm = banks[l % 2]
        for jg in range(6):
            wt = wts[ti % 2]
            p.dma("sp" if ti % 2 == 0 else "act", wt[:], w_v[:, :, jg * 512:(jg + 1) * 512])
            ti += 1
            for jj in range(4):
                j = jg * 4 + jj
                for k in range(KC):
                    p.mm(psm[:, 2 * j:2 * j + 2], wt[:, k, jj * 128:(jj + 1) * 128], sil[:, k, :],
                         start=(k == 0), stop=(k == KC - 1))
        for c in range(2):
            p.tt(part[:, :, c], psm[:, 0:48].rearrange("p (j c) -> p j c", c=2)[:, :, c], bq[:], ALU.add)
        mpk = p.dram(f"mpk_l{l}", [128, 48], F32)
        mg = p.dram(f"mg_l{l}", [512, 48], F32)
        p.dma("sp", mpk[:], part[:].rearrange("p j c -> p (j c)"))
        gathered.append((mg, mpk))
    p.barrier()
    for (mg, mpk) in gathered:
        p.allgather(mg[:], mpk[:], GROUPS)
    p.barrier()
    gathered = [g_[0] for g_ in gathered]
    for l in range(nlayers):
        modv = p.sb(f"modv{l}", [128, 96, 2], F32)
        gts = p.sb(f"gts{l}", [128, 4, KC], F32)
        mods = p.sb(f"modsb{l}", [128, 6, KC, 2], F32)
        gT = p.dram(f"gT_l{l}", [128, 4, KC], F32, kind="ExternalInput")
        p.dma("sp", gts[:], gT[:])
        for r in range(4):
            p.dma("sp", modv[:, 24 * r:24 * r + 24, :].rearrange("p j c -> p (j c)"), gathered[l][r * 128:(r + 1) * 128, :])
        for c in range(2):
            p.stt(mods[:, 0, :, c], modv[:, 16:32, c], 1.0, gts[:, 0, :], ALU.add, ALU.mult)
            p.copy(mods[:, 1, :, c], modv[:, 0:16, c])
            p.tt(mods[:, 2, :, c], modv[:, 32:48, c], gts[:, 1, :], ALU.mult)
            p.stt(mods[:, 3, :, c], modv[:, 64:80, c], 1.0, gts[:, 2, :], ALU.add, ALU.mult)
            p.copy(mods[:, 4, :, c], modv[:, 48:64, c])
            p.tt(mods[:, 5, :, c], modv[:, 80:96, c], gts[:, 3, :], ALU.mult)
        mo = p.dram(f"mods_l{l}", [128, 6, KC, 2], F32)
        p.dma("sp", mo[:], mods[:])
        outs.append(mo)
    p.sb_pop()
    return outs


def build_fused(nlayers=DEPTH):
    p = Prog()
    banks = [p.ps(f"bank{i}", [128, 512]) for i in range(8)]
    ones = p.sb("ones", [128, 128], BF16)
    p.memset(ones[:], 1.0)
    ohs = p.sb("ohs", [128, 2, 4], F32)
    oh_in = p.dram("ohpn", [128, 2, 4], F32, kind="ExternalInput")
    p.dma("sp", ohs[:], oh_in[:])
    x_in = p.dram("xT", [D, T], F32, kind="ExternalInput")
    xo_final = p.dram("xo", [D, T], F32, kind="ExternalOutput")
    mods_l = emit_mods(p, banks, nlayers)
    for l in range(nlayers):
        sfx = f"_l{l}"
        ioA = IO(p, sfx, {"xT": x_in, "mods": mods_l[l]})
        wc = l < DEPTH_TOTAL - 1 if nlayers == DEPTH else True
        emit_A(p, ioA, banks, ones, wc)
        A = ioA.made
        q2 = ("sp", "act")
        specs = [("kv", 320), ("k0", 512), ("k1", 512), ("v0", 512), ("v1", 512)]
        pk = {n: p.dram(f"pk_{n}{sfx}", [r, TL], BF16) for n, r in specs}
        g = {n: p.dram(f"g_{n}{sfx}", [4 * r, TL], BF16) for n, r in specs}
        p.dma("sp", pk["kv"][0:256, :], A["kvnT"][:, TC:])
        p.dma("sp", pk["kv"][256:320, :], A["krT"][:, TC:])
        for hf in range(2):
            p.dma(q2[hf], pk[f"k{hf}"][:, :], A["nakT"][hf * 512:(hf + 1) * 512, TC:])
            p.dma(q2[hf], pk[f"v{hf}"][:, :], A["nav"][TC + hf * 512:TC + (hf + 1) * 512, :])
        pku = [p.dram(f"pk_u{k}{sfx}", [256, TL], F32) for k in range(4)]
        gu = [p.dram(f"g_u{k}{sfx}", [1024, TL], F32) for k in range(4)]
        for k in range(4):
            p.dma(q2[k % 2], pku[k][:, :], A["uT"][k * 256:(k + 1) * 256, TC:])
        p.barrier()
        for n, _ in specs:
            p.allgather(g[n][:], pk[n][:], GROUPS)
        for k in range(4):
            p.allgather(gu[k][:], pku[k][:], GROUPS)
        p.barrier()
        kvn_all = p.dram("kvn_all" + sfx, [256, NKEY], BF16)
        kr_all = p.dram("kr_all" + sfx, [64, NKEY], BF16)
        nak_h = p.dram("nak_h" + sfx, [1024, 1536], BF16)
        nav_h = p.dram("nav_h" + sfx, [1536, 1024], BF16)
        p.dma("sp", kvn_all[:, 0:TC], A["kvnT"][:, 0:TC])
        p.dma("sp", kr_all[:, 0:TC], A["krT"][:, 0:TC])
        for r in range(4):
            q = q2[r % 2]
            cs = slice(TC + r * TL, TC + (r + 1) * TL)
            p.dma(q, kvn_all[:, cs], g["kv"][r * 320:r * 320 + 256, :])
            p.dma(q, kr_all[:, cs], g["kv"][r * 320 + 256:r * 320 + 320, :])
        p.dma("sp", nak_h[:, 256:1280], A["nakT"][:, TC:])
        p.dma("act", nav_h[256:1280, :], A["nav"][TC:, :])
        p.sb_push()
        candk = p.sb("candk", [128, 4, 8, 256], BF16)
        candv = p.sb("candv", [128, 4, 2, 1024], BF16)
        selk = p.sb("selk", [128, 8, 256], BF16)
        selv = p.sb("selv", [128, 2, 1024], BF16)
        for side in range(2):
            c0 = 768 if side == 0 else 0
            d0 = 0 if side == 0 else 1280
            gv = g["v1"] if side == 0 else g["v0"]
            v0 = 256 if side == 0 else 0
            for r in range(4):
                for hf in range(2):
                    p.dma("sp", candk[:, r, 4 * hf:4 * hf + 4, :],
                          g[f"k{hf}"][r * 512:(r + 1) * 512, c0:c0 + 256].rearrange("(h p) n -> p h n", p=128))
                p.dma("act", candv[:, r, :, :],
                      gv[r * 512 + v0:r * 512 + v0 + 256, :].rearrange("(c p) f -> p c f", p=128))
            for (cand, sel) in ((candk, selk), (candv, selv)):
                cf = lambda r_: cand[:, r_, :, :].rearrange("p a b -> p (a b)")
                sf = sel[:].rearrange("p a b -> p (a b)")
                p.ts(sf, cf(0), ohs[:, side, 0:1], None, ALU.mult)
                for r in range(1, 4):
                    p.stt(sf, cf(r), ohs[:, side, r:r + 1], sf, ALU.mult, ALU.add)
            p.dma("sp", nak_h[:, d0:d0 + 256].rearrange("(h p) n -> p h n", p=128), selk[:])
            p.dma("sp", nav_h[d0:d0 + 256, :].rearrange("(c p) f -> p c f", p=128), selv[:])
        p.sb_pop()
        bound = {"xT": x_in, "mods": mods_l[l], "u_parts": (A["uT"], gu), "rgT": A["rgT"], "kvn_all": kvn_all,
                 "kr_all": kr_all, "QnT": A["QnT"], "QrT": A["QrT"], "naqT": A["naqT"], "nak_h": nak_h,
                 "nav_h": nav_h, "nak_c": A["nakT"][:, 0:TC], "nav_c": A["nav"][0:TC, :], "sgT": A["sgT"]}
        if l == nlayers - 1:
            bound["xo"] = xo_final
        ioB = IO(p, sfx, bound)
        emit_B(p, ioB, banks, ones, wc)
        if l < nlayers - 1:
            x_in = ioB.made["xo"]
            p.new_epoch()
    return p.build()


def layer_inputs(inp, l, core):
    b, j = core // 4, core % 4
    C2, S2, pm = rope_tables()
    w_in = inp["w_in"][l]
    wq = inp["w_q_b"][l].reshape(512, 8, 192)
    wkv = inp["w_kv_b"][l].reshape(256, 8, 256)
    oh = np.zeros((128, 4), np.float32)
    oh[:, j] = 1.0
    return dict(
        w_ada_q=np.ascontiguousarray(inp["w_ada"][l][:, j * 3072:(j + 1) * 3072]),
        b_adaT_q=np.ascontiguousarray(fm(inp["b_ada"][l])[:, j * 24:(j + 1) * 24]),
        gT=np.ascontiguousarray(np.stack([fm(inp[k][l]) for k in ("g_mix_pre", "g_mix_post", "g_ffn_pre", "g_ffn_post")], 1)),
        w_fm=np.ascontiguousarray(np.concatenate([w_in[:, a:b_] for a, b_ in SEC], axis=1)),
        w_nav=np.ascontiguousarray(w_in[:, 4928:5952]),
        gqT=fm(inp["g_q_a"][l]), gkvT=fm(inp["g_kv_a"][l]),
        w_qn=np.ascontiguousarray(wq[:, :, :128].reshape(512, 1024)),
        w_qr=np.ascontiguousarray(wq[:, :, 128:].reshape(512, 512)),
        ropeC=np.ascontiguousarray(C2[:, j * TL:(j + 1) * TL]), ropeS=np.ascontiguousarray(S2[:, j * TL:(j + 1) * TL]),
        perm=pm,
        convT=np.ascontiguousarray(np.concatenate(
            [np.stack([fm(inp["conv_w"][l][t]) for t in range(4)], -1), fm(inp["conv_b"][l])[:, :, None]], -1)),
        rg_w=np.ascontiguousarray(np.concatenate([inp["rg_wa"][l], inp["rg_wx"][l]], 0)),
        rg_b=np.ascontiguousarray(np.stack([fm(inp["rg_ba"][l][0]), fm(inp["rg_ba"][l][1]),
                                            fm(inp["rg_bx"][l][0]), fm(inp["rg_bx"][l][1])], 1)),
        lamT=np.ascontiguousarray(np.stack([fm(inp["rg_lambda"][l][0]), fm(inp["rg_lambda"][l][1])], 1)),
        onehot=oh,
        w_kn=np.ascontiguousarray(wkv[:, :, :128].reshape(256, 1024)),
        w_v=np.ascontiguousarray(wkv[:, :, 128:].reshape(256, 1024)),
        BT=na_bias_table(inp["na_rpb"][l], j),
        w_mla_o=inp["w_mla_o"][l], w_rg_o=inp["w_rg_o"][l], w_na_o=inp["w_na_o"][l], w_out=inp["w_out"][l],
        w_g=inp["w_ffn_gate"][l], w_u=inp["w_ffn_up"][l], w_d=inp["w_ffn_down"][l])


_NC = {}


LAYERS_PER_LAUNCH = 2


def kernel(**inputs):
    inp = {k: np.asarray(v) for k, v in inputs.items()}
    if "F" not in _NC:
        _NC["F"] = build_fused(LAYERS_PER_LAUNCH)
    xTs = initial_xT(inp)
    for l0 in range(0, DEPTH, LAYERS_PER_LAUNCH):
        cache = {}
        maps = []
        for core in range(8):
            b, j = core // 4, core % 4
            ohpn = np.zeros((128, 2, 4), np.float32)
            ohpn[:, 0, (j - 1) % 4] = 1.0
            ohpn[:, 1, (j + 1) % 4] = 1.0
            cond = np.stack([inp["c"][b], inp["c_ctx"]], 0)
            m = {"xT": xTs[core], "ohpn": ohpn,
                 "condT": np.ascontiguousarray(cond.reshape(2, KC, 128).transpose(2, 1, 0))}
            for li_ in range(LAYERS_PER_LAUNCH):
                li = layer_inputs(inp, l0 + li_, core)
                for k, v in li.items():
                    if k not in ("w_ada_q", "b_adaT_q", "ropeC", "ropeS", "onehot", "BT"):
                        v = cache.setdefault((k, li_), v)
                    m[f"{k}_l{li_}"] = np.ascontiguousarray(v)
            maps.append(m)
        res = run_bass_kernel_spmd(_NC["F"], maps, core_ids=list(range(8)))
        xTs = [np.ascontiguousarray(np.asarray(r["xo"])) for r in res.results]
    out = np.zeros((2, SEQ, D), np.float32)
    for core in range(8):
        b, j = core // 4, core % 4
        out[b, j * TL:(j + 1) * TL, :] = xTs[core][:, TC:].T
    return out
```

```python
import math
import numpy as np
import ml_dtypes
import concourse.bass as bass
import concourse.mybir as mybir
from concourse.bass_utils import run_bass_kernel_spmd

F32 = mybir.dt.float32
BF16 = mybir.dt.bfloat16
AF = mybir.ActivationFunctionType
ALU = mybir.AluOpType

ENGS = ("pe", "act", "dve", "pool", "sp")
DT_SIZE = {F32: 4, BF16: 2}

D = 2048
KC = 16
TC = 256
TL = 1024
T = TC + TL
SEQ = 4096
NKEY = TC + SEQ
GRID_W = 64
DEPTH = 2
FFN = 5632
HC = FFN // 128
EPS = 1e-6
MLA_SCALE = 1.0 / math.sqrt(192.0)
NA_SCALE = 1.0 / math.sqrt(128.0)
BLKS = [(0, 512), (512, 512), (1024, 256)]
LBLKS = [(256, 512), (768, 512)]
SEC = [(0, 512), (512, 768), (832, 1856), (1856, 2880), (2880, 3904), (3904, 4928),
       (5952, 8000), (8000, 10048), (10048, 12096), (768, 832)]
NFM = sum(b - a for a, b in SEC)
NA_WIN = []
for _i in range(16):
    if _i < 4:
        lo, n = _i, 12 - _i
    elif _i <= 12:
        lo, n = _i, 8
    else:
        lo, n = 12, _i - 4
    n += n % 2
    NA_WIN.append((lo, n))
NA_NCH = [n // 2 + 2 for _, n in NA_WIN]
NA_CH0 = [int(x) for x in np.cumsum([0] + NA_NCH[:-1])]
NA_TOT = int(sum(NA_NCH))


class Buf:
    def __init__(self, name, ap, shape=None, dtype=None):
        self.name = name
        self.ap = ap
        self.shape = shape
        self.dtype = dtype
        self.w = None
        self.r = []

    def __getitem__(self, key):
        return View(self, self.ap[key])


class View:
    def __init__(self, buf, ap):
        self.buf = buf
        self.ap = ap

    def __getitem__(self, key):
        return View(self.buf, self.ap[key])

    def rearrange(self, s, **kw):
        return View(self.buf, self.ap.rearrange(s, **kw))


def _v(x):
    return x.ap if isinstance(x, View) else x


class Prog:
    def __init__(self):
        self.nc = bass.Bass("TRN2", target_bir_lowering=False)
        self.ops = {e: [] for e in ENGS}
        self.cnt = {}
        self.known = {e: {} for e in ENGS}
        self.ndma = {"sp": 0, "pool": 0, "act": 0}
        self.NDMASEM = 8
        self.sb_off = 16640
        self.sb_mark = []
        self.names = 0
        self.SB_BYTES = 192 * 1024
        self.ep = 0

    def dram(self, name, shape, dtype, kind="Internal"):
        t = self.nc.dram_tensor(name, list(shape), dtype, kind=kind)
        return Buf(name, t.ap(), shape, dtype)

    def sb(self, name, shape, dtype):
        per_part = int(np.prod(shape[1:])) * DT_SIZE[dtype]
        off = (self.sb_off + 63) // 64 * 64
        assert off + per_part <= self.SB_BYTES, f"SBUF overflow at {name}: {off}+{per_part}"
        self.names += 1
        h = self.nc.alloc_sbuf_tensor_at(f"{name}_{self.names}", list(shape), dtype, offset=off)
        self.sb_off = off + per_part
        return Buf(name, h.ap(), shape, dtype)

    def sb_push(self):
        self.sb_mark.append(self.sb_off)

    def sb_pop(self):
        self.barrier()
        self.sb_off = self.sb_mark.pop()

    def ps(self, name, shape, dtype=F32):
        self.names += 1
        h = self.nc.alloc_psum_tensor(f"{name}_{self.names}", list(shape), dtype)
        return Buf(name, h.ap(), shape, dtype)

    def _deps(self, eng, reads, writes):
        waits = {}

        def add(tok):
            if tok is None:
                return
            k, v = tok
            if eng == "pe" and k.split(".")[0] == "pe":
                return
            if waits.get(k, -1) < v:
                waits[k] = v

        for b in reads:
            add(b.w)
        for b in writes:
            add(b.w)
            for t in b.r:
                add(t)
        out = []
        kn = self.known[eng]
        for k, v in waits.items():
            if kn.get(k, -1) >= v:
                continue
            kn[k] = v
            out.append((k, v))
        return out

    def _mark(self, tok, reads, writes):
        for b in reads:
            b.r.append(tok)
            if len(b.r) > 48:
                m = {}
                for k, v in b.r:
                    if m.get(k, -1) < v:
                        m[k] = v
                b.r = list(m.items())
        for b in writes:
            b.w = tok
            b.r = []

    def op(self, eng, fn, reads=(), writes=()):
        reads = [x.buf if isinstance(x, View) else x for x in reads]
        writes = [x.buf if isinstance(x, View) else x for x in writes]
        waits = self._deps(eng, reads, writes)
        ek = eng if self.ep == 0 else f"{eng}.{self.ep}"
        v = self.cnt.get(ek, 0) + 1
        self.cnt[ek] = v
        self.ops[eng].append((fn, waits, (ek, 1)))
        self._mark((ek, v), reads, writes)

    def dma(self, q, out, in_, **kw):
        reads = [in_.buf]
        writes = [out.buf]
        i = self.ndma[q] % self.NDMASEM
        self.ndma[q] += 1
        sk = f"dma_{q}_{i}"
        prev = self.cnt.get(sk, 0)
        waits = self._deps(q, reads, writes)
        if prev > 0 and self.known[q].get(sk, -1) < prev:
            self.known[q][sk] = prev
            waits.append((sk, prev))
        v = prev + 16
        self.cnt[sk] = v
        o, s = out.ap, in_.ap

        def fn(e, o=o, s=s, kw=kw):
            return e.dma_start(out=o, in_=s, **kw)

        self.ops[q].append((fn, waits, (sk, 16)))
        self._mark((sk, v), reads, writes)

    def allgather(self, out, in_, groups):
        reads = [in_.buf if isinstance(in_, View) else in_]
        writes = [out.buf if isinstance(out, View) else out]
        self.ncc = getattr(self, "ncc", 0) + 1
        sk = f"cc_{self.ncc}"
        waits = self._deps("pool", reads, writes)
        self.cnt[sk] = 1
        o, s = _v(out), _v(in_)

        def fn(e, o=o, s=s):
            return e.collective_compute("AllGather", ALU.bypass, replica_groups=groups, ins=[s], outs=[o])

        self.ops["pool"].append((fn, waits, (sk, 1)))
        self._mark((sk, 1), reads, writes)

    def new_epoch(self):
        self.barrier()
        self.ep += 1

    def barrier(self):
        snap = dict(self.cnt)
        for e in ENGS:
            waits = []
            for k, v in snap.items():
                if e == "pe" and k.split(".")[0] == "pe":
                    continue
                if self.known[e].get(k, -1) < v:
                    self.known[e][k] = v
                    waits.append((k, v))
            if waits:
                self.ops[e].append((None, waits, None))

    def build(self):
        self.barrier()
        nc = self.nc
        semkeys = sorted(self.cnt.keys())
        from contextlib import ExitStack

        with ExitStack() as st:
            sems = {k: st.enter_context(nc.semaphore(k)) for k in semkeys}
            block = st.enter_context(nc.Block())

            def emit(engobj, lst):
                for fn, waits, inc in lst:
                    for k, v in waits:
                        engobj.wait_ge(sems[k], v)
                    if fn is None:
                        continue
                    ins = fn(engobj)
                    if inc is not None:
                        ins.then_inc(sems[inc[0]], inc[1])

            @block.tensor
            def _(e):
                emit(e, self.ops["pe"])

            @block.scalar
            def _(e):
                emit(e, self.ops["act"])

            @block.vector
            def _(e):
                emit(e, self.ops["dve"])

            @block.gpsimd
            def _(e):
                emit(e, self.ops["pool"])

            @block.sync
            def _(e):
                emit(e, self.ops["sp"])

        return nc

    def mm(self, out, lhsT, rhs, start=True, stop=True):
        o, l, r = _v(out), _v(lhsT), _v(rhs)
        self.op("pe", lambda e: e.matmul(o, l, r, start=start, stop=stop), [lhsT, rhs], [out])

    def act(self, out, in_, func, bias=None, scale=1.0):
        o, i = _v(out), _v(in_)
        b = _v(bias) if isinstance(bias, View) else bias
        s = _v(scale) if isinstance(scale, View) else scale
        rd = [in_] + [x for x in (bias, scale) if isinstance(x, View)]
        kw = {}
        if b is not None:
            kw["bias"] = b
        self.op("act", lambda e: e.activation(o, i, func, scale=s, **kw), rd, [out])

    def tt(self, out, a, b, op, eng="dve"):
        o, x, y = _v(out), _v(a), _v(b)
        self.op(eng, lambda e: e.tensor_tensor(o, x, y, op), [a, b], [out])

    def ts(self, out, a, s1, s2, op0, op1=None, eng="dve"):
        o, x = _v(out), _v(a)
        s1v = _v(s1) if isinstance(s1, View) else s1
        s2v = _v(s2) if isinstance(s2, View) else s2
        rd = [a] + [x_ for x_ in (s1, s2) if isinstance(x_, View)]
        if op1 is None:
            self.op(eng, lambda e: e.tensor_scalar(o, x, s1v, None, op0), rd, [out])
        else:
            self.op(eng, lambda e: e.tensor_scalar(o, x, s1v, s2v, op0, op1), rd, [out])

    def stt(self, out, a, s, b, op0, op1):
        o, x, y = _v(out), _v(a), _v(b)
        sv = _v(s) if isinstance(s, View) else s
        rd = [a, b] + ([s] if isinstance(s, View) else [])
        self.op("dve", lambda e: e.scalar_tensor_tensor(o, x, sv, y, op0, op1), rd, [out])

    def copy(self, out, in_, eng="dve"):
        o, i = _v(out), _v(in_)
        if eng == "act":
            self.op("act", lambda e: e.copy(o, i), [in_], [out])
        else:
            self.op(eng, lambda e: e.tensor_copy(o, i), [in_], [out])

    def memset(self, out, val, eng="dve"):
        o = _v(out)
        self.op(eng, lambda e: e.memset(o, val), [], [out])

    def scan(self, out, d0, d1, init):
        o, a, b = _v(out), _v(d0), _v(d1)
        iv = _v(init) if isinstance(init, View) else init
        rd = [d0, d1] + ([init] if isinstance(init, View) else [])
        self.op("dve", lambda e: e.tensor_tensor_scan(o, a, b, iv, ALU.mult, ALU.add), rd, [out])

    def recip(self, out, in_):
        o, i = _v(out), _v(in_)
        self.op("dve", lambda e: e.reciprocal(o, i), [in_], [out])


class IO:
    def __init__(self, p, sfx, bound):
        self.p, self.sfx, self.bound, self.made = p, sfx, bound, {}

    def inp(self, n, s, d=F32):
        if n in self.bound:
            return self.bound[n]
        return self.p.dram(n + self.sfx, s, d, kind="ExternalInput")

    def out(self, n, s, d=F32):
        if n in self.bound:
            return self.bound[n]
        b = self.p.dram(n + self.sfx, s, d)
        self.made[n] = b
        return b

    def tmp(self, n, s, d=F32):
        return self.p.dram(n + self.sfx, s, d)


def rstd_from_ssq(p, rstd_out, ssq_ps, n):
    p.ts(rstd_out, ssq_ps, 1.0 / n, EPS, ALU.mult, ALU.add)
    p.act(rstd_out, rstd_out, AF.Sqrt)
    p.recip(rstd_out, rstd_out)


def seg_split(c0, n):
    out = []
    if c0 < TC:
        e = min(TC, c0 + n)
        out.append((c0, e - c0, 1))
        if c0 + n > TC:
            out.append((TC, c0 + n - TC, 0))
    else:
        out.append((c0, n, 0))
    return out


def emit_A(p, io, banks, ones, wc=True):
    IN = io.inp
    OUT = io.out
    LB = BLKS if wc else LBLKS
    xch = io.bound.get("xch")
    xT = IN("xT", [D, T])
    w_fm = IN("w_fm", [D, NFM])
    w_nav = IN("w_nav", [D, 1024])
    gqT = IN("gqT", [128, 4])
    gkvT = IN("gkvT", [128, 2])
    w_qn = IN("w_qn", [512, 1024])
    w_qr = IN("w_qr", [512, 512])
    ropeC = IN("ropeC", [128, TL])
    ropeS = IN("ropeS", [128, TL])
    perm = IN("perm", [128, 128])

    uT_o = OUT("uT", [1024, T])
    rgT_o = OUT("rgT", [1024, T])
    naqT_o = OUT("naqT", [1024, T], BF16)
    nakT_o = OUT("nakT", [1024, T], BF16)
    nav_o = OUT("nav", [T, 1024], BF16)
    sgT_o = OUT("sgT", [6144, T])
    QnT_o = OUT("QnT", [1024, T], BF16)
    QrT_o = OUT("QrT", [512, T], BF16)
    kvnT_o = OUT("kvnT", [256, T], BF16)
    krT_o = OUT("krT", [64, T], BF16)
    p.sb_push()

    mods = p.sb("mods", [128, 6, KC, 2], F32)
    p.dma("sp", mods[:], IN("mods", [128, 6, KC, 2])[:])

    hT = p.sb("hT", [128, KC, T], BF16)
    qa = p.sb("qa", [128, 4, T], F32)
    kva = p.sb("kva", [128, 2, T], F32)
    krp = p.sb("krp", [64, T], F32)
    p.sb_push()
    xs = p.sb("xs", [128, KC, 512], F32)
    sq = p.sb("sq", [128, KC, 512], BF16)
    rstd = p.sb("rstd", [128, 512], F32)
    tmp = [p.sb(f"tmp{i}", [128, 512], F32) for i in range(2)]
    xT_v = xT[:].rearrange("(k p) n -> p k n", p=128)
    for (c0, n) in BLKS:
        p.dma("sp", xs[:, :, 0:n], xT_v[:, :, c0:c0 + n])
        p.act(sq[:, :, 0:n], xs[:, :, 0:n], AF.Square)
        ssq = banks[1]
        for k in range(KC):
            p.mm(ssq[:, 0:n], ones[:], sq[:, k, 0:n], start=(k == 0), stop=(k == KC - 1))
        rstd_from_ssq(p, rstd[:, 0:n], ssq[:, 0:n], D)
        for k in range(KC):
            tm = tmp[k % 2]
            for (s0, sn, cnd) in seg_split(c0, n):
                l0 = s0 - c0
                p.stt(tm[:, l0:l0 + sn], xs[:, k, l0:l0 + sn], mods[:, 0, k, cnd:cnd + 1], rstd[:, l0:l0 + sn],
                      ALU.mult, ALU.mult)
                p.act(hT[:, k, s0:s0 + sn], tm[:, l0:l0 + sn], AF.Identity, bias=mods[:, 1, k, cnd:cnd + 1])
    p.sb_pop()

    p.sb_push()
    wt2 = [p.sb(f"win{i}", [128, KC, 512], BF16) for i in range(2)]
    st32 = [p.sb(f"st32_{i}", [128, T], F32) for i in range(3)]
    st16 = [p.sb(f"st16_{i}", [128, T], BF16) for i in range(2)]
    w_fm_v = w_fm[:].rearrange("(k p) n -> p k n", p=128)
    ntile = (NFM + 511) // 512
    ev = 0
    bi = 0
    for tI in range(ntile):
        w0 = tI * 512
        wn = min(512, NFM - w0)
        wt = wt2[tI % 2]
        p.dma("pool", wt[:, :, 0:wn], w_fm_v[:, :, w0:w0 + wn])
        for cc in range(0, wn, 128):
            m = min(128, wn - cc)
            chunk = (w0 + cc) // 128
            if chunk < 4:
                kind, dst = "sb", qa[:, chunk, :]
            elif chunk < 6:
                kind, dst = "sb", kva[:, chunk - 4, :]
            elif chunk < 14:
                kind, dst, drow = "f32", uT_o, (chunk - 6) * 128
            elif chunk < 22:
                kind, dst, drow = "f32", rgT_o, (chunk - 14) * 128
            elif chunk < 30:
                kind, dst, drow = "bf", naqT_o, (chunk - 22) * 128
            elif chunk < 38:
                kind, dst, drow = "bf", nakT_o, (chunk - 30) * 128
            elif chunk < 86:
                kind, dst, drow = "sig", sgT_o, (chunk - 38) * 128
            else:
                kind, dst = "sb", krp[:, :]
            if kind in ("f32", "sig"):
                stg = st32[ev % 3]
            elif kind == "bf":
                stg = st16[ev % 2]
            ev += 1
            ctx_needed = (4 <= chunk < 14) or (30 <= chunk < 38) or chunk >= 86
            for (c0, n) in (BLKS if ctx_needed else LB):
                ps = banks[2 + bi % 6]
                bi += 1
                for k in range(KC):
                    p.mm(ps[0:m, 0:n], wt[:, k, cc:cc + m], hT[:, k, c0:c0 + n], start=(k == 0), stop=(k == KC - 1))
                if kind == "sb":
                    p.copy(dst[0:m, c0:c0 + n], ps[0:m, 0:n], eng="act" if bi % 2 else "dve")
                elif kind == "sig":
                    p.act(stg[:, c0:c0 + n], ps[:, 0:n], AF.Sigmoid)
                else:
                    p.copy(stg[:, c0:c0 + n], ps[:, 0:n], eng="act" if bi % 2 else "dve")
            if kind != "sb":
                if xch is not None and dst is uT_o:
                    c8 = chunk - 6
                    p.dma("sp", dst[drow:drow + 128, 0:TC], stg[:, 0:TC])
                    p.dma("sp", xch["pku"][c8 // 2][(c8 % 2) * 128:(c8 % 2) * 128 + 128, :], stg[:, TC:T])
                elif xch is not None and dst is nakT_o:
                    h8 = chunk - 30
                    p.dma("sp", dst[drow:drow + 128, 0:TC], stg[:, 0:TC])
                    p.dma("sp", xch["pk"][f"k{h8 // 4}"][(h8 % 4) * 128:(h8 % 4) * 128 + 128, :], stg[:, TC:T])
                    p.dma("sp", xch["nak_h"][drow:drow + 128, 256:1280], stg[:, TC:T])
                else:
                    p.dma("sp", dst[drow:drow + 128, :], stg[:])
    w_nav_v = w_nav[:].rearrange("(k p) n -> p k n", p=128)
    for half in range(2):
        wt = wt2[half % 2]
        p.dma("pool", wt[:], w_nav_v[:, :, half * 512:(half + 1) * 512])
        for tcn in range(T // 128):
            ps = banks[2 + bi % 6]
            bi += 1
            for k in range(KC):
                p.mm(ps[:, :], hT[:, k, tcn * 128:(tcn + 1) * 128], wt[:, k, :], start=(k == 0), stop=(k == KC - 1))
            stg = st16[tcn % 2]
            p.copy(stg[:, 0:512], ps[:, :], eng="act" if bi % 2 else "dve")
            if xch is None or tcn < 2:
                p.dma("sp", nav_o[tcn * 128:(tcn + 1) * 128, half * 512:(half + 1) * 512], stg[:, 0:512])
            else:
                t8 = tcn - 2
                p.dma("sp", xch["pk"][f"v{t8 // 4}"][(t8 % 4) * 128:(t8 % 4) * 128 + 128, half * 512:(half + 1) * 512],
                      stg[:, 0:512])
                p.dma("sp", xch["nav_h"][256 + t8 * 128:256 + (t8 + 1) * 128, half * 512:(half + 1) * 512], stg[:, 0:512])
    p.sb_pop()

    p.sb_push()
    qn = p.sb("qn", [128, 4, T], BF16)
    kvn = p.sb("kvn", [128, 2, T], BF16)
    sqq = p.sb("sqq", [128, 4, 512], BF16)
    rs = p.sb("rs", [128, 512], F32)
    tq = p.sb("tq", [128, 512], F32)
    gq = p.sb("gq", [128, 4], F32)
    gkv = p.sb("gkv", [128, 2], F32)
    p.dma("sp", gq[:], gqT[:])
    p.dma("sp", gkv[:], gkvT[:])
    for (src, nch, gsb, dstb, dim) in ((qa, 4, gq, qn, 512), (kva, 2, gkv, kvn, 256)):
        for (c0, n) in (LB if src is qa else BLKS):
            p.act(sqq[:, 0:nch, 0:n], src[:, :, c0:c0 + n], AF.Square)
            ssq = banks[1]
            for k in range(nch):
                p.mm(ssq[:, 0:n], ones[:], sqq[:, k, 0:n], start=(k == 0), stop=(k == nch - 1))
            rstd_from_ssq(p, rs[:, 0:n], ssq[:, 0:n], dim)
            for k in range(nch):
                p.stt(dstb[:, k, c0:c0 + n], src[:, k, c0:c0 + n], gsb[:, k:k + 1], rs[:, 0:n], ALU.mult, ALU.mult)
    for k in range(2):
        if xch is None:
            p.dma("sp", kvnT_o[k * 128:(k + 1) * 128, :], kvn[:, k, :])
        else:
            p.dma("sp", kvnT_o[k * 128:(k + 1) * 128, 0:TC], kvn[:, k, 0:TC])
            p.dma("sp", xch["pk"]["kv"][k * 128:(k + 1) * 128, :], kvn[:, k, TC:T])
    wqn = p.sb("wqn", [128, 4, 1024], BF16)
    wqr = p.sb("wqr", [128, 4, 512], BF16)
    p.dma("pool", wqn[:], w_qn[:].rearrange("(k p) n -> p k n", p=128))
    p.dma("pool", wqr[:], w_qr[:].rearrange("(k p) n -> p k n", p=128))
    cT = p.sb("cT", [128, TL], F32)
    sT = p.sb("sT", [128, TL], F32)
    pm = p.sb("pm", [128, 128], F32)
    p.dma("sp", cT[:], ropeC[:])
    p.dma("sp", sT[:], ropeS[:])
    p.dma("sp", pm[:], perm[:])
    stq = [p.sb(f"stq{i}", [128, T], BF16) for i in range(2)]
    for h in range(8):
        stg = stq[h % 2]
        for (c0, n) in LB:
            ps = banks[2 + bi % 6]
            bi += 1
            for k in range(4):
                p.mm(ps[:, 0:n], wqn[:, k, h * 128:(h + 1) * 128], qn[:, k, c0:c0 + n], start=(k == 0), stop=(k == 3))
            p.copy(stg[:, c0:c0 + n], ps[:, 0:n], eng="act" if bi % 2 else "dve")
        p.dma("sp", QnT_o[h * 128:(h + 1) * 128, :], stg[:])
    qrf = p.sb("qrf", [128, T], F32)
    t1 = p.sb("t1", [128, 512], F32)
    t2 = p.sb("t2", [128, 512], F32)

    def rope_chunk(src, npart, dst_dram_rows, ctx_cols=True):
        stg = stq[0]
        if ctx_cols:
            p.copy(stg[0:npart, 0:TC], src[0:npart, 0:TC])
        for (c0, n) in ((TC, 512), (TC + 512, 512)):
            ps = banks[2 + (c0 // 512) % 2]
            p.mm(ps[0:npart, 0:n], pm[0:npart, 0:npart], src[0:npart, c0:c0 + n])
            l0 = c0 - TC
            p.tt(t1[0:npart, 0:n], src[0:npart, c0:c0 + n], cT[0:npart, l0:l0 + n], ALU.mult)
            p.tt(t2[0:npart, 0:n], ps[0:npart, 0:n], sT[0:npart, l0:l0 + n], ALU.mult)
            p.tt(stg[0:npart, c0:c0 + n], t1[0:npart, 0:n], t2[0:npart, 0:n], ALU.add)
        if dst_dram_rows is None:
            p.dma("sp", krT_o[:, 0:TC], stg[0:npart, 0:TC])
            p.dma("sp", xch["pk"]["kv"][256:320, :], stg[0:npart, TC:T])
        else:
            p.dma("sp", dst_dram_rows, stg[0:npart, :])

    for hc in range(4):
        for (c0, n) in LB:
            ps = banks[4 + bi % 4]
            bi += 1
            for k in range(4):
                p.mm(ps[:, 0:n], wqr[:, k, hc * 128:(hc + 1) * 128], qn[:, k, c0:c0 + n], start=(k == 0), stop=(k == 3))
            p.copy(qrf[:, c0:c0 + n], ps[:, 0:n], eng="act")
        rope_chunk(qrf, 128, QrT_o[hc * 128:(hc + 1) * 128, :], ctx_cols=wc)
    rope_chunk(krp, 64, krT_o[:, :] if xch is None else None)
    p.sb_pop()
    p.sb_pop()


def fm(v):
    return np.ascontiguousarray(v.reshape(-1, 128).T)


def rope_tables():
    t = np.arange(SEQ, dtype=np.int32)
    row = (t // GRID_W).astype(np.float32)
    col = (t % GRID_W).astype(np.float32)
    inv = (np.float32(10000.0) ** (-np.arange(16, dtype=np.float32) / np.float32(16))).astype(np.float32)
    ang = np.concatenate([row[:, None] * inv, col[:, None] * inv], axis=-1).astype(np.float32)
    cos, sin = np.cos(ang).astype(np.float32), np.sin(ang).astype(np.float32)
    C = np.zeros((64, SEQ), np.float32)
    S = np.zeros((64, SEQ), np.float32)
    for d in range(64):
        a, b, f = d // 32, (d % 32) // 16, d % 16
        C[d] = cos[:, a * 16 + f]
        S[d] = sin[:, a * 16 + f] * (-1.0 if b == 0 else 1.0)
    pm = np.zeros((128, 128), np.float32)
    for m in range(128):
        d = m % 64
        partner = d + 16 if (d % 32) < 16 else d - 16
        pm[(m // 64) * 64 + partner, m] = 1.0
    return np.concatenate([C, C], 0), np.concatenate([S, S], 0), pm


def initial_xT(inp):
    out = []
    for core in range(8):
        b, j = core // 4, core % 4
        xx = np.concatenate([inp["ctx"][b], inp["x"][b, j * TL:(j + 1) * TL]], 0)
        out.append(np.ascontiguousarray(xx.T))
    return out


def emit_B(p, io, banks, ones, wc=True):
    IN = io.inp
    LB = BLKS if wc else LBLKS
    c_lo = 0 if wc else TC
    SEGS = ((0, TC, 1), (TC, TL, 0)) if wc else ((TC, TL, 0),)
    xT = IN("xT", [D, T])
    mods_i = IN("mods", [128, 6, KC, 2])
    u_parts = io.bound.get("u_parts")
    u_all = None if u_parts is not None else IN("u_all", [1024, NKEY])
    rgT = IN("rgT", [1024, T])
    convT = IN("convT", [128, 8, 5])
    rg_w = IN("rg_w", [4, 8, 128, 128])
    rg_b = IN("rg_b", [128, 4, 8])
    lamT = IN("lamT", [128, 2, 8])
    onehot = IN("onehot", [128, 4])
    kv_parts = io.bound.get("kv_parts")
    kvn_all = None if kv_parts is not None else IN("kvn_all", [256, NKEY], BF16)
    kr_all = None if kv_parts is not None else IN("kr_all", [64, NKEY], BF16)
    QnT = IN("QnT", [1024, T], BF16)
    QrT = IN("QrT", [512, T], BF16)
    w_kn = IN("w_kn", [256, 1024])
    w_v = IN("w_v", [256, 1024])
    naqT = IN("naqT", [1024, T], BF16)
    nak_h = IN("nak_h", [1024, 1536], BF16)
    nav_h = IN("nav_h", [1536, 1024], BF16)
    nak_c = IN("nak_c", [1024, TC], BF16)
    nav_c = IN("nav_c", [TC, 1024], BF16)
    BT = IN("BT", [8, 128, NA_TOT, 64])
    sgT = IN("sgT", [6144, T])
    w_bo = [IN(n, [1024, D]) for n in ("w_mla_o", "w_rg_o", "w_na_o")]
    w_out = IN("w_out", [D, D])
    w_g = IN("w_g", [D, FFN])
    w_u = IN("w_u", [D, FFN])
    w_d = IN("w_d", [FFN, D])
    xo = io.out("xo", [D, T], F32)
    broT = [io.tmp(n, [1024, T], BF16) for n in ("mlaoT", "rgoT", "naoT")]
    y2T = io.tmp("y2T", [D, T], F32)
    x1T = io.tmp("x1T", [D, T], F32)
    y3T = io.tmp("y3T", [D, T], F32)

    p.sb_push()
    mods = p.sb("mods", [128, 6, KC, 2], F32)
    p.dma("sp", mods[:], mods_i[:])

    p.sb_push()
    cw = p.sb("cw", [128, 8, 5], F32)
    rb = p.sb("rb", [128, 4, 8], F32)
    negc = p.sb("negc", [128, 2, 8], F32)
    oh = p.sb("oh", [128, 4], F32)
    rw = p.sb("rw", [128, 32, 128], BF16)
    p.dma("sp", cw[:], convT[:])
    p.dma("sp", rb[:], rg_b[:])
    p.dma("sp", negc[:], lamT[:])
    p.dma("sp", oh[:], onehot[:])
    p.dma("pool", rw[:], rg_w[:].rearrange("a c i j -> i (a c) j"))
    p.act(negc[:], negc[:], AF.Exp, scale=-1.0)
    p.act(negc[:], negc[:], AF.Ln, bias=1.0)
    p.ts(negc[:], negc[:], -8.0, None, ALU.mult)
    U = p.sb("U", [128, NKEY], F32)
    Y = p.sb("Y", [128, NKEY], F32)
    YB = p.sb("YB", [128, NKEY], BF16)
    Rd = [p.sb(f"R{d}", [128, NKEY], F32) for d in range(2)]
    Id = [p.sb(f"I{d}", [128, NKEY], F32) for d in range(2)]
    Sd = [p.sb(f"S{d}", [128, NKEY], F32) for d in range(2)]
    gt = p.sb("gt", [128, T], F32)
    g2 = p.sb("g2", [128, T], F32)
    rec = p.sb("rec", [128, T], F32)
    rgo_st = p.sb("rgo_st", [128, T], BF16)
    KB = [(i * 512, 512) for i in range(8)] + [(4096, 256)]
    bi = 0
    for c in range(8):
        if u_parts is None:
            p.dma("sp", U[:], u_all[c * 128:(c + 1) * 128, :])
        else:
            uT_own, gu_l = u_parts
            p.dma("sp", U[:, 0:TC], uT_own[c * 128:(c + 1) * 128, 0:TC])
            for r in range(4):
                r0 = r * 256 + (c % 2) * 128
                p.dma("sp" if r % 2 == 0 else "act", U[:, TC + r * TL:TC + (r + 1) * TL], gu_l[c // 2][r0:r0 + 128, :])
        p.dma("act", gt[:], rgT[c * 128:(c + 1) * 128, :])
        for (s, e) in ((0, TC), (TC, NKEY)):
            p.ts(Y[:, s:e], U[:, s:e], cw[:, c, 2:3], cw[:, c, 4:5], ALU.mult, ALU.add)
            p.stt(Y[:, s + 2:e], U[:, s:e - 2], cw[:, c, 0:1], Y[:, s + 2:e], ALU.mult, ALU.add)
            p.stt(Y[:, s + 1:e], U[:, s:e - 1], cw[:, c, 1:2], Y[:, s + 1:e], ALU.mult, ALU.add)
            p.stt(Y[:, s:e - 1], U[:, s + 1:e], cw[:, c, 3:4], Y[:, s:e - 1], ALU.mult, ALU.add)
        p.copy(YB[:], Y[:], eng="pool")
        HF = U
        for d in range(2):
            Rb, Ib, Sb = Rd[d], Id[d], Sd[d]
            for (c0, n) in KB:
                psr = banks[bi % 8]
                psi = banks[(bi + 1) % 8]
                bi += 2
                p.mm(psr[:, 0:n], rw[:, d * 8 + c, :], YB[:, c0:c0 + n])
                p.mm(psi[:, 0:n], rw[:, 16 + d * 8 + c, :], YB[:, c0:c0 + n])
                p.act(Rb[:, c0:c0 + n], psr[:, 0:n], AF.Sigmoid, bias=rb[:, d, c:c + 1])
                p.act(Ib[:, c0:c0 + n], psi[:, 0:n], AF.Sigmoid, bias=rb[:, 2 + d, c:c + 1])
            p.act(Rb[:], Rb[:], AF.Exp, scale=negc[:, d, c:c + 1])
            p.tt(Ib[:], Ib[:], Y[:], ALU.mult)
            if d == 1:
                pass
            p.act(Sb[:], Rb[:], AF.Square)
            p.act(Sb[:], Sb[:], AF.Sqrt, bias=1.0, scale=-1.0)
            p.tt(Ib[:], Ib[:], Sb[:], ALU.mult)
            if d == 0:
                p.scan(HF[:, :], Rb[:, :], Ib[:, :], 0.0)
            else:
                HR = Sb
                p.scan(HR[:, 0:TC][:, ::-1], Rb[:, 0:TC][:, ::-1], Ib[:, 0:TC][:, ::-1], 0.0)
                p.scan(HR[:, TC:NKEY][:, ::-1], Rb[:, TC:NKEY][:, ::-1], Ib[:, TC:NKEY][:, ::-1], HR[:, 0:1])
        if wc:
            p.tt(rec[:, 0:TC], HF[:, 0:TC], HR[:, 0:TC], ALU.add)
        p.ts(rec[:, TC:T], HF[:, TC:TC + TL], oh[:, 0:1], None, ALU.mult)
        p.stt(rec[:, TC:T], HR[:, TC:TC + TL], oh[:, 0:1], rec[:, TC:T], ALU.mult, ALU.add)
        for r in range(1, 4):
            o0 = TC + r * TL
            p.stt(rec[:, TC:T], HF[:, o0:o0 + TL], oh[:, r:r + 1], rec[:, TC:T], ALU.mult, ALU.add)
            p.stt(rec[:, TC:T], HR[:, o0:o0 + TL], oh[:, r:r + 1], rec[:, TC:T], ALU.mult, ALU.add)
        p.act(g2[:, c_lo:T], gt[:, c_lo:T], AF.Square)
        p.ts(g2[:, c_lo:T], g2[:, c_lo:T], 0.044715, 1.0, ALU.mult, ALU.add)
        p.tt(g2[:, c_lo:T], g2[:, c_lo:T], gt[:, c_lo:T], ALU.mult)
        p.act(g2[:, c_lo:T], g2[:, c_lo:T], AF.Sigmoid, scale=2.0 * math.sqrt(2.0 / math.pi))
        p.tt(g2[:, c_lo:T], g2[:, c_lo:T], gt[:, c_lo:T], ALU.mult)
        p.tt(rgo_st[:, c_lo:T], g2[:, c_lo:T], rec[:, c_lo:T], ALU.mult)
        p.dma("sp", broT[1][c * 128:(c + 1) * 128, :], rgo_st[:])
    p.sb_pop()

    p.sb_push()
    kvn = p.sb("kvn", [128, 2, NKEY], BF16)
    kr = p.sb("kr", [64, NKEY], BF16)
    wkn = p.sb("wkn", [128, 2, 1024], BF16)
    wv = p.sb("wv", [128, 2, 1024], BF16)
    if kv_parts is None:
        p.dma("sp", kvn[:], kvn_all[:].rearrange("(k p) n -> p k n", p=128))
        p.dma("sp", kr[:], kr_all[:])
    else:
        kvnT_own, krT_own, gkv = kv_parts
        for k in range(2):
            p.dma("sp", kvn[:, k, 0:TC], kvnT_own[k * 128:(k + 1) * 128, 0:TC])
            for r in range(4):
                p.dma("sp" if r % 2 == 0 else "act", kvn[:, k, TC + r * TL:TC + (r + 1) * TL],
                      gkv[r * 320 + k * 128:r * 320 + (k + 1) * 128, :])
        p.dma("sp", kr[:, 0:TC], krT_own[:, 0:TC])
        for r in range(4):
            p.dma("sp" if r % 2 == 0 else "act", kr[:, TC + r * TL:TC + (r + 1) * TL], gkv[r * 320 + 256:r * 320 + 320, :])
    p.dma("pool", wkn[:], w_kn[:].rearrange("(k p) n -> p k n", p=128))
    p.dma("pool", wv[:], w_v[:].rearrange("(k p) n -> p k n", p=128))
    knT = [p.sb(f"knT{i}", [128, NKEY], BF16) for i in range(2)]
    Vh = [p.sb(f"Vh{i}", [128, 34, 128], BF16) for i in range(2)]
    Qn = [p.sb(f"Qn{i}", [128, T], BF16) for i in range(2)]
    Qr = [p.sb(f"Qr{i}", [64, T], BF16) for i in range(2)]
    Es = [p.sb(f"E{i}", [128, 512], BF16) for i in range(5)]
    SBK = [banks[0], banks[1], banks[6], banks[7]]
    rz = p.sb("rz", [128, 512], F32)
    accs = [p.sb(f"acc{i}", [128, 512], F32) for i in range(2)]
    ones32 = p.sb("ones32", [128, 128], F32)
    p.memset(ones32[:], 1.0)
    mst = [p.sb(f"mst{i}", [128, T], BF16) for i in range(2)]
    ei = 0
    for h in range(8):
        kT, V, qn_, qr_, stg = knT[h % 2], Vh[h % 2], Qn[h % 2], Qr[h % 2], mst[h % 2]
        p.dma("sp", qn_[:], QnT[h * 128:(h + 1) * 128, :])
        p.dma("sp", qr_[:], QrT[h * 64:(h + 1) * 64, :])
        for bidx, (c0, n) in enumerate(KB):
            ps = banks[6 + bidx % 2]
            for k in range(2):
                p.mm(ps[:, 0:n], wkn[:, k, h * 128:(h + 1) * 128], kvn[:, k, c0:c0 + n], start=(k == 0), stop=(k == 1))
            p.copy(kT[:, c0:c0 + n], ps[:, 0:n], eng="dve" if bidx % 2 else "act")
        for g in range(9):
            ncc = 4 if g < 8 else 2
            ps = banks[6 + g % 2]
            for cc in range(ncc):
                kc = g * 4 + cc
                for k in range(2):
                    p.mm(ps[:, cc * 128:(cc + 1) * 128], kvn[:, k, kc * 128:(kc + 1) * 128],
                         wv[:, k, h * 128:(h + 1) * 128], start=(k == 0), stop=(k == 1))
            p.copy(V[:, g * 4:g * 4 + ncc, :].rearrange("p c d -> p (c d)"), ps[:, 0:ncc * 128],
                   eng="act" if g % 2 else "dve")
        qblocks = [(TC, 512, 34), (TC + 512, 512, 34)] + ([(0, TC, 2)] if wc else [])
        for qb, (q0, qn_cols, nkc) in enumerate(qblocks):
            O = banks[2 + 2 * (qb % 2)]
            Z = banks[3 + 2 * (qb % 2)]

            def s_mm(kc):
                S = SBK[kc % 4]
                p.mm(S[:, 0:qn_cols], kT[:, kc * 128:(kc + 1) * 128], qn_[:, q0:q0 + qn_cols], start=True, stop=False)
                p.mm(S[:, 0:qn_cols], kr[0:64, kc * 128:(kc + 1) * 128], qr_[0:64, q0:q0 + qn_cols], start=False, stop=True)

            for kk in range(min(3, nkc)):
                s_mm(kk)
            for kc in range(nkc):
                if kc + 3 < nkc:
                    s_mm(kc + 3)
                S = SBK[kc % 4]
                E = Es[ei % 5]
                ei += 1
                p.act(E[:, 0:qn_cols], S[:, 0:qn_cols], AF.Exp, scale=MLA_SCALE)
                p.mm(O[:, 0:qn_cols], V[:, kc, :], E[:, 0:qn_cols], start=(kc == 0), stop=(kc == nkc - 1))
                acc = accs[qb % 2]
                if kc == 0:
                    p.copy(acc[:, 0:qn_cols], E[:, 0:qn_cols])
                else:
                    p.tt(acc[:, 0:qn_cols], acc[:, 0:qn_cols], E[:, 0:qn_cols], ALU.add)
            p.mm(Z[:, 0:qn_cols], ones32[:], accs[qb % 2][:, 0:qn_cols])
            p.recip(rz[:, 0:qn_cols], Z[:, 0:qn_cols])
            p.tt(stg[:, q0:q0 + qn_cols], O[:, 0:qn_cols], rz[:, 0:qn_cols], ALU.mult)
        p.dma("sp", broT[0][h * 128:(h + 1) * 128, :], stg[:])
    p.sb_pop()

    p.sb_push()
    qT = [p.sb(f"qT{i}", [128, T], BF16) for i in range(2)]
    kTh = [p.sb(f"kTh{i}", [128, 1536], BF16) for i in range(2)]
    kTc = [p.sb(f"kTc{i}", [128, TC], BF16) for i in range(2)]
    Va = [p.sb(f"Va{i}", [128, 12, 128], BF16) for i in range(2)]
    Vb = [p.sb(f"Vb{i}", [128, 11, 128], BF16) for i in range(2)]
    Vc = [p.sb(f"Vc{i}", [128, 2, 128], BF16) for i in range(2)]
    bias = [p.sb(f"bias{i}", [128, NA_TOT, 64], F32) for i in range(2)]
    tS = [p.sb(f"tS{i}", [128, 512], F32) for i in range(3)]
    En = [p.sb(f"En{i}", [128, 512], BF16) for i in range(3)]
    NSB = [banks[0], banks[1], banks[6]]
    rzn = p.sb("rzn", [128, 512], F32)
    nst = [p.sb(f"nst{i}", [128, T], BF16) for i in range(2)]
    for h in range(8):
        x = h % 2
        hs = slice(h * 128, (h + 1) * 128)
        p.dma("sp", qT[x][:], naqT[hs, :])
        p.dma("sp", kTh[x][:], nak_h[hs, :])
        p.dma("sp", kTc[x][:], nak_c[hs, :])
        p.dma("act", Va[x][:], nav_h[:, hs].rearrange("(c p) d -> p c d", p=128))
        p.dma("act", Vb[x][:], nav_h[64:64 + 1408, hs].rearrange("(c p) d -> p c d", p=128))
        p.dma("act", Vc[x][:], nav_c[:, hs].rearrange("(c p) d -> p c d", p=128))
        p.dma("sp", bias[x][:], BT[h])
        stg = nst[x]

        def s_row(i):
            lo, n = NA_WIN[i]
            nch = n // 2
            S = NSB[i % 3]
            q0 = TC + i * 64
            for c in range(nch):
                k0 = lo * 64 + c * 128
                p.mm(S[:, c * 64:(c + 1) * 64], kTh[x][:, k0:k0 + 128], qT[x][:, q0:q0 + 64])
            for c in range(2):
                p.mm(S[:, (nch + c) * 64:(nch + c + 1) * 64], kTc[x][:, c * 128:(c + 1) * 128], qT[x][:, q0:q0 + 64])

        s_row(0)
        s_row(1)
        for i in range(16):
            if i + 2 < 16:
                s_row(i + 2)
            lo, n = NA_WIN[i]
            nch = n // 2
            w = (nch + 2) * 64
            S = NSB[i % 3]
            grp = i // 8
            O = banks[2 + 2 * grp]
            Z = banks[3 + 2 * grp]
            oc = (i % 8) * 64
            ch0 = NA_CH0[i]
            p.stt(tS[i % 3][:, 0:w], S[:, 0:w], NA_SCALE,
                  bias[x][:, ch0:ch0 + nch + 2, :].rearrange("p c q -> p (c q)"), ALU.mult, ALU.add)
            p.act(En[i % 3][:, 0:w], tS[i % 3][:, 0:w], AF.Exp)
            for c in range(nch + 2):
                if c < nch:
                    vv = Va[x][:, lo // 2 + c, :] if lo % 2 == 0 else Vb[x][:, (lo - 1) // 2 + c, :]
                else:
                    vv = Vc[x][:, c - nch, :]
                p.mm(O[:, oc:oc + 64], vv, En[i % 3][:, c * 64:(c + 1) * 64], start=(c == 0), stop=(c == nch + 1))
            for c in range(nch + 2):
                p.mm(Z[:, oc:oc + 64], ones[:], En[i % 3][:, c * 64:(c + 1) * 64], start=(c == 0), stop=(c == nch + 1))
            if i % 8 == 7:
                p.recip(rzn[:], Z[:])
                p.tt(stg[:, TC + grp * 512:TC + (grp + 1) * 512], O[:], rzn[:], ALU.mult)
        if wc:
            S = banks[6]
            O = banks[7]
            Z = banks[0]
            for c in range(2):
                p.mm(S[:, c * 256:(c + 1) * 256], kTc[x][:, c * 128:(c + 1) * 128], qT[x][:, 0:TC])
            p.act(En[0][:], S[:], AF.Exp, scale=NA_SCALE)
            for c in range(2):
                p.mm(O[:, 0:TC], Vc[x][:, c, :], En[0][:, c * 256:(c + 1) * 256], start=(c == 0), stop=(c == 1))
            for c in range(2):
                p.mm(Z[:, 0:TC], ones[:], En[0][:, c * 256:(c + 1) * 256], start=(c == 0), stop=(c == 1))
            p.recip(rzn[:, 0:TC], Z[:, 0:TC])
            p.tt(stg[:, 0:TC], O[:, 0:TC], rzn[:, 0:TC], ALU.mult)
        p.dma("sp", broT[2][hs, :], stg[:])
    p.sb_pop()

    yT = p.sb("yT", [128, KC, T], BF16)
    off_after_yT = p.sb_off
    p.sb_push()
    bro = [p.sb(f"bro{i}", [128, 8, T], BF16) for i in range(3)]
    for i in range(3):
        p.dma("sp", bro[i][:], broT[i][:].rearrange("(k p) n -> p k n", p=128))
    wbo = [[p.sb(f"wbo{i}_{j}", [128, 8, 256], BF16) for j in range(2)] for i in range(3)]
    sgs = [p.sb(f"sgs{i}", [128, 3, T], F32) for i in range(2)]
    ta = p.sb("ta", [128, 512], F32)
    tb = p.sb("tb", [128, 512], F32)
    sg_v = sgT[:].rearrange("(b f p) n -> f p b n", b=3, p=128)
    for fg in range(8):
        for i in range(3):
            p.dma("pool", wbo[i][fg % 2][:], w_bo[i][:, fg * 256:(fg + 1) * 256].rearrange("(k p) n -> p k n", p=128))
        for fj in range(2):
            f = fg * 2 + fj
            sgt = sgs[f % 2]
            p.dma("act", sgt[:], sg_v[f])
            for (c0, n) in LB:
                pss = [banks[(f * 3 + i) % 6] for i in range(3)] if False else [banks[i] for i in range(3)]
                for i in range(3):
                    for k in range(8):
                        p.mm(pss[i][:, 0:n], wbo[i][fg % 2][:, k, fj * 128:(fj + 1) * 128], bro[i][:, k, c0:c0 + n],
                             start=(k == 0), stop=(k == 7))
                p.tt(ta[:, 0:n], pss[0][:, 0:n], sgt[:, 0, c0:c0 + n], ALU.mult)
                p.tt(tb[:, 0:n], pss[1][:, 0:n], sgt[:, 1, c0:c0 + n], ALU.mult)
                p.tt(ta[:, 0:n], ta[:, 0:n], tb[:, 0:n], ALU.add)
                p.tt(tb[:, 0:n], pss[2][:, 0:n], sgt[:, 2, c0:c0 + n], ALU.mult)
                p.tt(yT[:, f, c0:c0 + n], ta[:, 0:n], tb[:, 0:n], ALU.add)
    p.sb_pop()

    def chunk_ssq(src_f32, first):
        for (c0, n) in LB:
            p.act(sqb[:, 0:n], src_f32[:, c0:c0 + n], AF.Square)
            p.mm(banks[7][:, 0:n], ones[:], sqb[:, 0:n])
            if first:
                p.copy(ssq[:, c0:c0 + n], banks[7][:, 0:n])
            else:
                p.tt(ssq[:, c0:c0 + n], ssq[:, c0:c0 + n], banks[7][:, 0:n], ALU.add)

    def finish_rstd():
        p.ts(rstd[:, c_lo:T], ssq[:, c_lo:T], 1.0 / D, EPS, ALU.mult, ALU.add)
        p.act(rstd[:, c_lo:T], rstd[:, c_lo:T], AF.Sqrt)
        p.recip(rstd[:, c_lo:T], rstd[:, c_lo:T])

    p.sb_push()
    ssq = p.sb("ssq", [128, T], F32)
    rstd = p.sb("rstd", [128, T], F32)
    sqb = p.sb("sqb", [128, 512], BF16)
    wo = [p.sb(f"wo{i}", [128, KC, 512], BF16) for i in range(2)]
    st = [p.sb(f"st{i}", [128, T], F32) for i in range(2)]
    xst = [p.sb(f"xst{i}", [128, T], F32) for i in range(2)]
    bi = 0
    for fg in range(4):
        p.dma("pool", wo[fg % 2][:], w_out[:, fg * 512:(fg + 1) * 512].rearrange("(k p) n -> p k n", p=128))
        for fj in range(4):
            f = fg * 4 + fj
            stg = st[f % 2]
            for (c0, n) in LB:
                ps = banks[bi % 6]
                bi += 1
                for k in range(KC):
                    p.mm(ps[:, 0:n], wo[fg % 2][:, k, fj * 128:(fj + 1) * 128], yT[:, k, c0:c0 + n],
                         start=(k == 0), stop=(k == KC - 1))
                p.copy(stg[:, c0:c0 + n], ps[:, 0:n], eng="act" if bi % 2 else "dve")
            chunk_ssq(stg, f == 0)
            p.dma("sp", y2T[f * 128:(f + 1) * 128, :], stg[:])
    finish_rstd()

    def residual_pass(srcT, resT, dstT, gsel, want_ssq):
        for f in range(KC):
            a = st[f % 2]
            b = xst[f % 2]
            p.dma("sp", a[:], srcT[f * 128:(f + 1) * 128, :])
            p.dma("act", b[:], resT[f * 128:(f + 1) * 128, :])
            for (s0, sn, cnd) in SEGS:
                p.stt(a[:, s0:s0 + sn], a[:, s0:s0 + sn], mods[:, gsel, f, cnd:cnd + 1], rstd[:, s0:s0 + sn],
                      ALU.mult, ALU.mult)
            p.tt(a[:, c_lo:T], a[:, c_lo:T], b[:, c_lo:T], ALU.add)
            if want_ssq:
                chunk_ssq(a, f == 0)
            p.dma("sp", dstT[f * 128:(f + 1) * 128, :], a[:])

    residual_pass(y2T, xT, x1T, 2, True)
    finish_rstd()
    h2 = yT
    for f in range(KC):
        a = st[f % 2]
        p.dma("sp", a[:], x1T[f * 128:(f + 1) * 128, :])
        for (s0, sn, cnd) in SEGS:
            p.stt(a[:, s0:s0 + sn], a[:, s0:s0 + sn], mods[:, 3, f, cnd:cnd + 1], rstd[:, s0:s0 + sn], ALU.mult, ALU.mult)
            p.act(h2[:, f, s0:s0 + sn], a[:, s0:s0 + sn], AF.Identity, bias=mods[:, 4, f, cnd:cnd + 1])
    p.sb_pop()

    p.sb_push()
    actT = p.sb("actT", [128, HC, T], BF16)
    p.sb_push()
    wgt = [p.sb(f"wg{i}", [128, KC, 128], BF16) for i in range(2)]
    wut = [p.sb(f"wu{i}", [128, KC, 128], BF16) for i in range(2)]
    sl = [p.sb(f"sl{i}", [128, 512], F32) for i in range(2)]
    w_g_v = w_g[:].rearrange("(k p) n -> p k n", p=128)
    w_u_v = w_u[:].rearrange("(k p) n -> p k n", p=128)
    bi = 0
    for hcn in range(HC):
        wg_, wu_ = wgt[hcn % 2], wut[hcn % 2]
        p.dma("pool", wg_[:], w_g_v[:, :, hcn * 128:(hcn + 1) * 128])
        p.dma("pool", wu_[:], w_u_v[:, :, hcn * 128:(hcn + 1) * 128])
        for (c0, n) in LB:
            pg = banks[(2 * bi) % 8]
            pu = banks[(2 * bi + 1) % 8]
            bi += 1
            for k in range(KC):
                p.mm(pg[:, 0:n], wg_[:, k, :], h2[:, k, c0:c0 + n], start=(k == 0), stop=(k == KC - 1))
            for k in range(KC):
                p.mm(pu[:, 0:n], wu_[:, k, :], h2[:, k, c0:c0 + n], start=(k == 0), stop=(k == KC - 1))
            s_ = sl[bi % 2]
            p.act(s_[:, 0:n], pg[:, 0:n], AF.Silu)
            p.tt(actT[:, hcn, c0:c0 + n], s_[:, 0:n], pu[:, 0:n], ALU.mult)
    p.sb_pop()
    p.barrier()
    cur = p.sb_off
    p.sb_off = off_after_yT - KC * T * 2
    st3 = [p.sb(f"st3_{i}", [128, T], F32) for i in range(2)]
    x3 = [p.sb(f"x3_{i}", [128, T], F32) for i in range(2)]
    ssq = p.sb("ssq3", [128, T], F32)
    rstd = p.sb("rstd3", [128, T], F32)
    sqb = p.sb("sqb3", [128, 512], BF16)
    assert p.sb_off <= off_after_yT
    p.sb_off = cur
    wdt = [p.sb(f"wd{i}", [128, 22, 256], BF16) for i in range(2)]
    w_d_v = w_d[:].rearrange("(k p) n -> p k n", p=128)
    ti = 0
    for fg in range(8):
        for half in range(2):
            wd_ = wdt[ti % 2]
            ti += 1
            p.dma("pool", wd_[:], w_d_v[:, half * 22:(half + 1) * 22, fg * 256:(fg + 1) * 256])
            for fj in range(2):
                for bidx, (c0, n) in enumerate(LB):
                    ps = banks[fj * 3 + bidx]
                    for k in range(22):
                        p.mm(ps[:, 0:n], wd_[:, k, fj * 128:(fj + 1) * 128], actT[:, half * 22 + k, c0:c0 + n],
                             start=(half == 0 and k == 0), stop=(half == 1 and k == 21))
        for fj in range(2):
            f = fg * 2 + fj
            stg = st3[f % 2]
            for bidx, (c0, n) in enumerate(LB):
                p.copy(stg[:, c0:c0 + n], banks[fj * 3 + bidx][:, 0:n], eng="act" if bidx % 2 else "dve")
            chunk_ssq(stg, f == 0)
            p.dma("sp", y3T[f * 128:(f + 1) * 128, :], stg[:])
    finish_rstd()
    st, xst = st3, x3
    residual_pass(y3T, x1T, xo, 5, False)
    p.sb_pop()
    p.sb_pop()


def na_bias_table(rpb, j):
    out = np.zeros((8, 128, NA_TOT, 64), np.float32)
    qc = np.arange(64)
    kc = np.arange(64)
    c_start = np.clip(qc - 8, 0, 48)
    col_in = (kc[:, None] >= c_start[None, :]) & (kc[:, None] < c_start[None, :] + 16)
    dc = np.clip(kc[:, None] - qc[None, :] + 15, 0, 30)
    for i in range(16):
        r = 16 * j + i
        lo, n = NA_WIN[i]
        kr0 = min(max(r - 4, 0), 56)
        tab = np.full((8, n * 64, 64), -1e30, np.float32)
        for a in range(lo, lo + n):
            hr = 16 * j - 4 + a
            if kr0 <= hr < kr0 + 8:
                vals = rpb[:, hr - r + 7, :][:, dc]
                vals = np.where(col_in[None], vals, np.float32(-1e30))
                tab[:, (a - lo) * 64:(a - lo + 1) * 64, :] = vals
        nch = n // 2
        ch0 = NA_CH0[i]
        out[:, :, ch0:ch0 + nch, :] = tab.reshape(8, nch, 128, 64).transpose(0, 2, 1, 3)
    return out


GROUPS = [[0, 1, 2, 3], [4, 5, 6, 7]]
DEPTH_TOTAL = DEPTH
PK = 2368


def emit_mods(p, banks, nlayers):
    outs = []
    p.sb_push()
    cond = p.sb("cond", [128, KC, 2], F32)
    sil = p.sb("sil", [128, KC, 2], F32)
    condT = p.dram("condT", [128, KC, 2], F32, kind="ExternalInput")
    p.dma("sp", cond[:], condT[:])
    p.act(sil[:], cond[:], AF.Silu)
    wts = [p.sb(f"wada{i}", [128, KC, 512], F32) for i in range(2)]
    ti = 0
    gathered = []
    for l in range(nlayers):
        w_q = p.dram(f"w_ada_q_l{l}", [D, 3072], F32, kind="ExternalInput")
        b_q = p.dram(f"b_adaT_q_l{l}", [128, 24], F32, kind="ExternalInput")
        bq = p.sb(f"bq{l}", [128, 24], F32)
        part = p.sb(f"part{l}", [128, 24, 2], F32)
        p.dma("sp", bq[:], b_q[:])
        w_v = w_q[:].rearrange("(k p) n -> p k n", p=128)
        psm = banks[l % 2]
        for jg in range(6):
            wt = wts[ti % 2]
            p.dma("sp" if ti % 2 == 0 else "act", wt[:], w_v[:, :, jg * 512:(jg + 1) * 512])
            ti += 1
            for jj in range(4):
                j = jg * 4 + jj
                for k in range(KC):
                    p.mm(psm[:, 2 * j:2 * j + 2], wt[:, k, jj * 128:(jj + 1) * 128], sil[:, k, :],
                         start=(k == 0), stop=(k == KC - 1))
        for c in range(2):
            p.tt(part[:, :, c], psm[:, 0:48].rearrange("p (j c) -> p j c", c=2)[:, :, c], bq[:], ALU.add)
        mpk = p.dram(f"mpk_l{l}", [128, 48], F32)
        mg = p.dram(f"mg_l{l}", [512, 48], F32)
        p.dma("sp", mpk[:], part[:].rearrange("p j c -> p (j c)"))
        gathered.append((mg, mpk))
    p.barrier()
    for (mg, mpk) in gathered:
        p.allgather(mg[:], mpk[:], GROUPS)
    p.barrier()
    gathered = [g_[0] for g_ in gathered]
    for l in range(nlayers):
        modv = p.sb(f"modv{l}", [128, 96, 2], F32)
        gts = p.sb(f"gts{l}", [128, 4, KC], F32)
        mods = p.sb(f"modsb{l}", [128, 6, KC, 2], F32)
        gT = p.dram(f"gT_l{l}", [128, 4, KC], F32, kind="ExternalInput")
        p.dma("sp", gts[:], gT[:])
        for r in range(4):
            p.dma("sp", modv[:, 24 * r:24 * r + 24, :].rearrange("p j c -> p (j c)"), gathered[l][r * 128:(r + 1) * 128, :])
        for c in range(2):
            p.stt(mods[:, 0, :, c], modv[:, 16:32, c], 1.0, gts[:, 0, :], ALU.add, ALU.mult)
            p.copy(mods[:, 1, :, c], modv[:, 0:16, c])
            p.tt(mods[:, 2, :, c], modv[:, 32:48, c], gts[:, 1, :], ALU.mult)
            p.stt(mods[:, 3, :, c], modv[:, 64:80, c], 1.0, gts[:, 2, :], ALU.add, ALU.mult)
            p.copy(mods[:, 4, :, c], modv[:, 48:64, c])
            p.tt(mods[:, 5, :, c], modv[:, 80:96, c], gts[:, 3, :], ALU.mult)
        mo = p.dram(f"mods_l{l}", [128, 6, KC, 2], F32)
        p.dma("sp", mo[:], mods[:])
        outs.append(mo)
    p.sb_pop()
    return outs


def build_fused(nlayers=DEPTH):
    p = Prog()
    banks = [p.ps(f"bank{i}", [128, 512]) for i in range(8)]
    ones = p.sb("ones", [128, 128], BF16)
    p.memset(ones[:], 1.0)
    ohs = p.sb("ohs", [128, 2, 4], F32)
    oh_in = p.dram("ohpn", [128, 2, 4], F32, kind="ExternalInput")
    p.dma("sp", ohs[:], oh_in[:])
    x_in = p.dram("xT", [D, T], F32, kind="ExternalInput")
    xo_final = p.dram("xo", [D, T], F32, kind="ExternalOutput")
    mods_l = emit_mods(p, banks, nlayers)
    for l in range(nlayers):
        sfx = f"_l{l}"
        specs = [("kv", 320), ("k0", 512), ("k1", 512), ("v0", 512), ("v1", 512)]
        pk = {n: p.dram(f"pk_{n}{sfx}", [r, TL], BF16) for n, r in specs}
        pku = [p.dram(f"pk_u{k}{sfx}", [256, TL], F32) for k in range(4)]
        nak_h = p.dram("nak_h" + sfx, [1024, 1536], BF16)
        nav_h = p.dram("nav_h" + sfx, [1536, 1024], BF16)
        ioA = IO(p, sfx, {"xT": x_in, "mods": mods_l[l],
                          "xch": {"pk": pk, "pku": pku, "nak_h": nak_h, "nav_h": nav_h}})
        wc = l < DEPTH_TOTAL - 1 if nlayers == DEPTH else True
        emit_A(p, ioA, banks, ones, wc)
        A = ioA.made
        q2 = ("sp", "act")
        g = {n: p.dram(f"g_{n}{sfx}", [4 * r, TL], BF16) for n, r in specs}
        gu = [p.dram(f"g_u{k}{sfx}", [1024, TL], F32) for k in range(4)]
        p.barrier()
        for n, _ in specs:
            p.allgather(g[n][:], pk[n][:], GROUPS)
        for k in range(4):
            p.allgather(gu[k][:], pku[k][:], GROUPS)
        p.barrier()
        p.sb_push()
        candk = p.sb("candk", [128, 4, 8, 256], BF16)
        candv = p.sb("candv", [128, 4, 2, 1024], BF16)
        selk = p.sb("selk", [128, 8, 256], BF16)
        selv = p.sb("selv", [128, 2, 1024], BF16)
        for side in range(2):
            c0 = 768 if side == 0 else 0
            d0 = 0 if side == 0 else 1280
            gv = g["v1"] if side == 0 else g["v0"]
            v0 = 256 if side == 0 else 0
            for r in range(4):
                for hf in range(2):
                    p.dma("sp", candk[:, r, 4 * hf:4 * hf + 4, :],
                          g[f"k{hf}"][r * 512:(r + 1) * 512, c0:c0 + 256].rearrange("(h p) n -> p h n", p=128))
                p.dma("act", candv[:, r, :, :],
                      gv[r * 512 + v0:r * 512 + v0 + 256, :].rearrange("(c p) f -> p c f", p=128))
            for (cand, sel) in ((candk, selk), (candv, selv)):
                cf = lambda r_: cand[:, r_, :, :].rearrange("p a b -> p (a b)")
                sf = sel[:].rearrange("p a b -> p (a b)")
                p.ts(sf, cf(0), ohs[:, side, 0:1], None, ALU.mult)
                for r in range(1, 4):
                    p.stt(sf, cf(r), ohs[:, side, r:r + 1], sf, ALU.mult, ALU.add)
            p.dma("sp", nak_h[:, d0:d0 + 256].rearrange("(h p) n -> p h n", p=128), selk[:])
            p.dma("sp", nav_h[d0:d0 + 256, :].rearrange("(c p) f -> p c f", p=128), selv[:])
        p.sb_pop()
        bound = {"xT": x_in, "mods": mods_l[l], "u_parts": (A["uT"], gu), "rgT": A["rgT"],
                 "kv_parts": (A["kvnT"], A["krT"], g["kv"]), "QnT": A["QnT"], "QrT": A["QrT"], "naqT": A["naqT"], "nak_h": nak_h,
                 "nav_h": nav_h, "nak_c": A["nakT"][:, 0:TC], "nav_c": A["nav"][0:TC, :], "sgT": A["sgT"]}
        if l == nlayers - 1:
            bound["xo"] = xo_final
        ioB = IO(p, sfx, bound)
        emit_B(p, ioB, banks, ones, wc)
        if l < nlayers - 1:
            x_in = ioB.made["xo"]
            p.new_epoch()
    return p.build()


def layer_inputs(inp, l, core):
    b, j = core // 4, core % 4
    C2, S2, pm = rope_tables()
    w_in = inp["w_in"][l]
    wq = inp["w_q_b"][l].reshape(512, 8, 192)
    wkv = inp["w_kv_b"][l].reshape(256, 8, 256)
    oh = np.zeros((128, 4), np.float32)
    oh[:, j] = 1.0
    return dict(
        w_ada_q=np.ascontiguousarray(inp["w_ada"][l][:, j * 3072:(j + 1) * 3072]),
        b_adaT_q=np.ascontiguousarray(fm(inp["b_ada"][l])[:, j * 24:(j + 1) * 24]),
        gT=np.ascontiguousarray(np.stack([fm(inp[k][l]) for k in ("g_mix_pre", "g_mix_post", "g_ffn_pre", "g_ffn_post")], 1)),
        w_fm=np.ascontiguousarray(np.concatenate([w_in[:, a:b_] for a, b_ in SEC], axis=1)),
        w_nav=np.ascontiguousarray(w_in[:, 4928:5952]),
        gqT=fm(inp["g_q_a"][l]), gkvT=fm(inp["g_kv_a"][l]),
        w_qn=np.ascontiguousarray(wq[:, :, :128].reshape(512, 1024)),
        w_qr=np.ascontiguousarray(wq[:, :, 128:].reshape(512, 512)),
        ropeC=np.ascontiguousarray(C2[:, j * TL:(j + 1) * TL]), ropeS=np.ascontiguousarray(S2[:, j * TL:(j + 1) * TL]),
        perm=pm,
        convT=np.ascontiguousarray(np.concatenate(
            [np.stack([fm(inp["conv_w"][l][t]) for t in range(4)], -1), fm(inp["conv_b"][l])[:, :, None]], -1)),
        rg_w=np.ascontiguousarray(np.concatenate([inp["rg_wa"][l], inp["rg_wx"][l]], 0)),
        rg_b=np.ascontiguousarray(np.stack([fm(inp["rg_ba"][l][0]), fm(inp["rg_ba"][l][1]),
                                            fm(inp["rg_bx"][l][0]), fm(inp["rg_bx"][l][1])], 1)),
        lamT=np.ascontiguousarray(np.stack([fm(inp["rg_lambda"][l][0]), fm(inp["rg_lambda"][l][1])], 1)),
        onehot=oh,
        w_kn=np.ascontiguousarray(wkv[:, :, :128].reshape(256, 1024)),
        w_v=np.ascontiguousarray(wkv[:, :, 128:].reshape(256, 1024)),
        BT=na_bias_table(inp["na_rpb"][l], j),
        w_mla_o=inp["w_mla_o"][l], w_rg_o=inp["w_rg_o"][l], w_na_o=inp["w_na_o"][l], w_out=inp["w_out"][l],
        w_g=inp["w_ffn_gate"][l], w_u=inp["w_ffn_up"][l], w_d=inp["w_ffn_down"][l])


_NC = {}


LAYERS_PER_LAUNCH = 2


def kernel(**inputs):
    inp = {k: np.asarray(v) for k, v in inputs.items()}
    if "F" not in _NC:
        _NC["F"] = build_fused(LAYERS_PER_LAUNCH)
    xTs = initial_xT(inp)
    for l0 in range(0, DEPTH, LAYERS_PER_LAUNCH):
        cache = {}
        maps = []
        for core in range(8):
            b, j = core // 4, core % 4
            ohpn = np.zeros((128, 2, 4), np.float32)
            ohpn[:, 0, (j - 1) % 4] = 1.0
            ohpn[:, 1, (j + 1) % 4] = 1.0
            cond = np.stack([inp["c"][b], inp["c_ctx"]], 0)
            m = {"xT": xTs[core], "ohpn": ohpn,
                 "condT": np.ascontiguousarray(cond.reshape(2, KC, 128).transpose(2, 1, 0))}
            for li_ in range(LAYERS_PER_LAUNCH):
                li = layer_inputs(inp, l0 + li_, core)
                for k, v in li.items():
                    if k not in ("w_ada_q", "b_adaT_q", "ropeC", "ropeS", "onehot", "BT"):
                        v = cache.setdefault((k, li_), v)
                    m[f"{k}_l{li_}"] = np.ascontiguousarray(v)
            maps.append(m)
        res = run_bass_kernel_spmd(_NC["F"], maps, core_ids=list(range(8)))
        xTs = [np.ascontiguousarray(np.asarray(r["xo"])) for r in res.results]
    out = np.zeros((2, SEQ, D), np.float32)
    for core in range(8):
        b, j = core // 4, core % 4
        out[b, j * TL:(j + 1) * TL, :] = xTs[core][:, TC:].T
    return out
```
